# Optimizing a Trainium2 kernel written in Bass

```python
import math
import jax, jax.numpy as jnp
from jax import lax
import numpy as np

D_MODEL = 1024
BATCH = 16
SEQ = 2048
DEPTH = 1

N_SB_HEADS = 8
SB_HEAD_DIM = 64
SB_WIDTH = N_SB_HEADS * SB_HEAD_DIM
SB_BLOCK = 128
SSM_WIDTH = D_MODEL // 2
SSM_GROUP = 16
SSM_GROUPS = SSM_WIDTH // SSM_GROUP
SSM_STATE = 64
D_FF = ((8 * D_MODEL // 3 + 127) // 128) * 128
CONV_WIDTH = 3
PLE_DIM = 256
DN_ALPHA = (2.0 * DEPTH) ** 0.25
DN_BETA = (8.0 * DEPTH) ** -0.25
LN_EPS = 1e-5
IN_WIDTH = 3 * SB_WIDTH + SSM_WIDTH + 2 * D_MODEL
_SPLITS = [SB_WIDTH, 2 * SB_WIDTH, 3 * SB_WIDTH, 3 * SB_WIDTH + SSM_WIDTH, 3 * SB_WIDTH + SSM_WIDTH + D_MODEL]

kernel_name = "hybrid_stickbreak_s5_convffn_deepnorm"


def _layer_norm(x, g, b):
    xf = x.astype(jnp.float32)
    mu = jnp.mean(xf, axis=-1, keepdims=True)
    var = jnp.mean(jnp.square(xf - mu), axis=-1, keepdims=True)
    return (xf - mu) * lax.rsqrt(var + LN_EPS) * g.astype(jnp.float32) + b.astype(jnp.float32)


def _stick_breaking_attention(q, k, v):
    b, t, _ = q.shape
    nb = t // SB_BLOCK
    def heads(a):
        return a.astype(jnp.float32).reshape(b, t, N_SB_HEADS, SB_HEAD_DIM).transpose(0, 2, 1, 3)
    qh, kh, vh = heads(q), heads(k), heads(v)
    scale = 1.0 / math.sqrt(SB_HEAD_DIM)
    q_blocks = qh.reshape(b, N_SB_HEADS, nb, SB_BLOCK, SB_HEAD_DIM).transpose(2, 0, 1, 3, 4)
    starts = jnp.arange(nb, dtype=jnp.int32) * SB_BLOCK
    kpos = jnp.arange(t, dtype=jnp.int32)

    def block(args):
        qb, start = args
        z = jnp.einsum('bhqd,bhkd->bhqk', qb, kh) * scale
        qpos = start + jnp.arange(SB_BLOCK, dtype=jnp.int32)
        mask = kpos[None, :] < qpos[:, None]
        log_om = jnp.where(mask, jax.nn.log_sigmoid(-z), 0.0)
        later = lax.cumsum(log_om, axis=3, reverse=True) - log_om
        w = jnp.where(mask, jnp.exp(jax.nn.log_sigmoid(z) + later), 0.0)
        return jnp.einsum('bhqk,bhkd->bhqd', w, vh)

    out = lax.map(block, (q_blocks, starts))
    return out.transpose(1, 0, 3, 2, 4).reshape(b, t, SB_WIDTH)


def _s5_scan(u, a_re, a_im, log_step, b_re, b_im, c_re, c_im, d):
    t = u.shape[1]
    f32 = jnp.float32
    lam = lax.complex(a_re.astype(f32), a_im.astype(f32))
    step = jnp.exp(log_step.astype(f32))[:, None]
    lam_bar = jnp.exp(lam * step)
    coef = (lam_bar - 1.0) / lam
    b_bar = coef[:, :, None] * lax.complex(b_re.astype(f32), b_im.astype(f32))
    uf = u.astype(f32)
    bu = lax.complex(jnp.einsum('btgh,gph->tbgp', uf, jnp.real(b_bar)),
                     jnp.einsum('btgh,gph->tbgp', uf, jnp.imag(b_bar)))
    a = jnp.broadcast_to(lam_bar[None, None], (t, 1) + lam_bar.shape)

    def combine(e1, e2):
        a1, b1 = e1
        a2, b2 = e2
        return a1 * a2, a2 * b1 + b2

    _, xs = lax.associative_scan(combine, (a, bu), axis=0)
    y = (jnp.einsum('tbgp,ghp->btgh', jnp.real(xs), c_re.astype(f32))
         - jnp.einsum('tbgp,ghp->btgh', jnp.imag(xs), c_im.astype(f32))
         + d.astype(f32) * uf)
    return y


def _causal_dwconv(x, w, bias):
    c = x.shape[-1]
    y = lax.conv_general_dilated(x, w[:, None, :].astype(x.dtype), window_strides=(1,),
                                 padding=((CONV_WIDTH - 1, 0),),
                                 dimension_numbers=('NWC', 'WIO', 'NWC'),
                                 feature_group_count=c)
    return y + bias.astype(x.dtype)


def setup_inputs(seed: int = 0) -> dict:
    key = jax.random.key(seed)
    ks = jax.random.split(key, 32)
    f32 = jnp.float32
    def nrm(k, shape, scale):
        return jax.random.normal(k, shape, f32) * scale
    L, G, P, H = DEPTH, SSM_GROUPS, SSM_STATE, SSM_GROUP
    a_im_base = jnp.pi * jnp.arange(P, dtype=f32)
    return {
        "x": nrm(ks[0], (BATCH, SEQ, D_MODEL), 1.0),
        "p": nrm(ks[1], (DEPTH, BATCH, SEQ, PLE_DIM), 1.0),
        "w_in": nrm(ks[2], (L, D_MODEL, IN_WIDTH), D_MODEL ** -0.5),
        "w_sb_out": nrm(ks[3], (L, SB_WIDTH, D_MODEL), SB_WIDTH ** -0.5),
        "ssm_a_re": -0.5 + nrm(ks[4], (L, G, P), 0.01),
        "ssm_a_im": a_im_base[None, None, :] + nrm(ks[5], (L, G, P), 0.01),
        "ssm_log_step": jax.random.uniform(ks[6], (L, G), f32, math.log(1e-3), math.log(1e-1)),
        "ssm_b_re": nrm(ks[7], (L, G, P, H), (2 * H) ** -0.5),
        "ssm_b_im": nrm(ks[8], (L, G, P, H), (2 * H) ** -0.5),
        "ssm_c_re": nrm(ks[9], (L, G, H, P), P ** -0.5),
        "ssm_c_im": nrm(ks[10], (L, G, H, P), P ** -0.5),
        "ssm_d": nrm(ks[11], (L, G, H), 1.0),
        "w_glu": nrm(ks[12], (L, SSM_WIDTH, 2 * D_MODEL), SSM_WIDTH ** -0.5),
        "w_o": nrm(ks[13], (L, D_MODEL, D_MODEL), DN_BETA * D_MODEL ** -0.5),
        "ln1_g": 1.0 + nrm(ks[14], (L, D_MODEL), 0.02),
        "ln1_b": nrm(ks[15], (L, D_MODEL), 0.02),
        "w_up": nrm(ks[16], (L, D_MODEL, 2 * D_FF), D_MODEL ** -0.5),
        "conv_w": nrm(ks[17], (L, CONV_WIDTH, 2 * D_FF), CONV_WIDTH ** -0.5),
        "conv_b": nrm(ks[18], (L, 2 * D_FF), 0.02),
        "w_down": nrm(ks[19], (L, D_FF, D_MODEL), DN_BETA * D_FF ** -0.5),
        "w_pe": nrm(ks[20], (L, PLE_DIM, D_MODEL), DN_BETA * PLE_DIM ** -0.5),
        "w_pe_gate": nrm(ks[21], (L, D_MODEL, D_MODEL), D_MODEL ** -0.5),
        "ln2_g": 1.0 + nrm(ks[22], (L, D_MODEL), 0.02),
        "ln2_b": nrm(ks[23], (L, D_MODEL), 0.02),
    }


def reference(x, p, w_in, w_sb_out, ssm_a_re, ssm_a_im, ssm_log_step, ssm_b_re, ssm_b_im,
              ssm_c_re, ssm_c_im, ssm_d, w_glu, w_o, ln1_g, ln1_b, w_up, conv_w, conv_b,
              w_down, w_pe, w_pe_gate, ln2_g, ln2_b):
    dt = x.dtype
    b, t, _ = x.shape
    h = x
    for i in range(DEPTH):
        proj = h @ w_in[i]
        q, k, v, u, g_attn, g_ssm = jnp.split(proj, _SPLITS, axis=-1)
        attn = _stick_breaking_attention(q, k, v).astype(dt)
        attn_br = attn @ w_sb_out[i]
        y = _s5_scan(u.reshape(b, t, SSM_GROUPS, SSM_GROUP), ssm_a_re[i], ssm_a_im[i],
                     ssm_log_step[i], ssm_b_re[i], ssm_b_im[i], ssm_c_re[i], ssm_c_im[i], ssm_d[i])
        y = jax.nn.gelu(y).reshape(b, t, SSM_WIDTH).astype(dt)
        z = y @ w_glu[i]
        ssm_br = z[..., :D_MODEL] * jax.nn.sigmoid(z[..., D_MODEL:])
        mixed = jax.nn.sigmoid(g_attn) * attn_br + jax.nn.sigmoid(g_ssm) * ssm_br
        h = _layer_norm(DN_ALPHA * h + mixed @ w_o[i], ln1_g[i], ln1_b[i]).astype(dt)
        up = _causal_dwconv(h @ w_up[i], conv_w[i], conv_b[i])
        val, gate = up[..., :D_FF], up[..., D_FF:]
        ffn = (jax.nn.silu(gate) * val) @ w_down[i]
        ple = (p[i].astype(dt) @ w_pe[i]) * jax.nn.sigmoid(h @ w_pe_gate[i])
        h = _layer_norm(DN_ALPHA * h + ffn + ple, ln2_g[i], ln2_b[i]).astype(dt)
    return h
```

```python
import numpy as np
from contextlib import ExitStack
import concourse.bass as bass
import concourse.mybir as mybir
from concourse.bass_utils import run_bass_kernel_spmd

F32 = mybir.dt.float32
BF16 = mybir.dt.bfloat16
I32 = mybir.dt.int32
AF = mybir.ActivationFunctionType
ALU = mybir.AluOpType


class Reg:
    __slots__ = ("w", "rs", "excl")

    def __init__(self):
        self.w = None
        self.rs = {}
        self.excl = False


class V:
    __slots__ = ("ap", "reg")

    def __init__(self, ap, reg):
        self.ap = ap
        self.reg = reg


class T:
    def __init__(self, t, nreg=1):
        self.t = t
        self.regs = {}
        self.default = Reg()

    def reg(self, key=None):
        if key is None:
            return self.default
        r = self.regs.get(key)
        if r is None:
            r = self.regs[key] = Reg()
        return r

    def v(self, idx=None, key=None):
        ap = self.t[idx] if idx is not None else self.t[:]
        return V(ap, self.reg(key))

    def __getitem__(self, idx):
        return V(self.t[idx], self.default)


class Eng:
    def __init__(self, kb, name, h):
        self.kb = kb
        self.name = name
        self.h = h
        self.sem = kb.newsem(name)
        self.cnt = 0
        self.seen = {}
        self.pend_r = []
        self.pend_w = []


class DSem:
    def __init__(self, kb, name):
        self.sem = kb.newsem(name)
        self.cnt = 0


class KB:
    SEM_LIMIT = 30000

    def __init__(self):
        self.nc = bass.Bass("TRN2", target_bir_lowering=False)
        self.es = ExitStack()
        self.nsem = 0
        nc = self.nc
        self.pe = Eng(self, "pe", nc.tensor)
        self.act = Eng(self, "act", nc.scalar)
        self.dve = Eng(self, "dve", nc.vector)
        self.pool = Eng(self, "pool", nc.gpsimd)
        self.sp = Eng(self, "sp", nc.sync)
        self.nwait = 0
        self.nops = 0

    def newsem(self, name):
        self.nsem += 1
        return self.es.enter_context(self.nc.semaphore(f"s{self.nsem}_{name}"))

    def sb(self, name, shape, dt):
        return T(self.es.enter_context(self.nc.sbuf_tensor(name, list(shape), dt)))

    def ps(self, name, shape, dt=F32):
        t = T(self.es.enter_context(self.nc.psum_tensor(name, list(shape), dt)))
        t.default.excl = True
        return t

    def dram(self, name, shape, dt, kind="Internal"):
        return T(self.nc.dram_tensor(name, list(shape), dt, kind=kind).ap())

    def _deps(self, e, rviews, wviews, raw_same_only=True):
        need = {}
        for v in rviews:
            w = v.reg.w
            if w is not None:
                if need.get(w[0], 0) < w[1]:
                    need[w[0]] = w[1]
            if v.reg.excl:
                for (s, val) in v.reg.rs.items():
                    if s is not e.sem and need.get(s, 0) < val:
                        need[s] = val
        skip_own = (e.name == "pe")
        for v in wviews:
            w = v.reg.w
            if w is not None and not (skip_own and w[0] is e.sem):
                if need.get(w[0], 0) < w[1]:
                    need[w[0]] = w[1]
            for (s, val) in v.reg.rs.items():
                if not (skip_own and s is e.sem) and need.get(s, 0) < val:
                    need[s] = val
        for s, val in need.items():
            if e.seen.get(s, 0) < val:
                e.h.wait_ge(s, val)
                e.seen[s] = val
                self.nwait += 1

    def _record(self, tok, rviews, wviews):
        s, val = tok
        for v in rviews:
            if v.reg.rs.get(s, 0) < val:
                v.reg.rs[s] = val
        for v in wviews:
            v.reg.w = tok
            v.reg.rs = {}

    def op(self, e, fn, r=(), w=(), inc=True):
        self._deps(e, r, w)
        ins = fn(e.h)
        self.nops += 1
        if inc:
            if e.cnt >= self.SEM_LIMIT:
                e.sem = self.newsem(e.name)
                e.cnt = 0
            e.cnt += 1
            ins.then_inc(e.sem, 1)
            if e.pend_r or e.pend_w:
                r = list(r) + e.pend_r
                w = list(w) + e.pend_w
                e.pend_r = []
                e.pend_w = []
            self._record((e.sem, e.cnt), r, w)
        else:
            e.pend_r.extend(r)
            e.pend_w.extend(w)
        return ins

    def dma(self, e, out, in_, ds, **kw):
        self._deps(e, [in_], [out])
        ins = e.h.dma_start(out=out.ap, in_=in_.ap, **kw)
        ds.cnt += 16
        ins.then_inc(ds.sem, 16)
        self._record((ds.sem, ds.cnt), [in_], [out])
        return ins

    def wait_all(self, e, views):
        self._deps(e, views, [])

    def mm(self, out, lhsT, rhs, start=True, stop=True, inc=None, extra_r=(), **kw):
        if inc is None:
            inc = stop
        return self.op(self.pe, lambda h: h.matmul(out.ap, lhsT.ap, rhs.ap, start=start, stop=stop, **kw),
                       r=[lhsT, rhs, *extra_r], w=[out], inc=inc)

    def transpose(self, out, in_, ident, inc=True):
        return self.op(self.pe, lambda h: h.transpose(out.ap, in_.ap, ident.ap), r=[in_, ident], w=[out], inc=inc)

    def actf(self, out, in_, func, scale=1.0, bias=0.0, e=None, extra_r=()):
        e = e or self.act
        r = [in_, *extra_r]
        sc = scale.ap if isinstance(scale, V) else scale
        bi = bias.ap if isinstance(bias, V) else bias
        if isinstance(scale, V):
            r.append(scale)
        if isinstance(bias, V):
            r.append(bias)
        return self.op(e, lambda h: h.activation(out=out.ap, in_=in_.ap, func=func, scale=sc, bias=bi), r=r, w=[out])

    def tt(self, out, a, b, op, e=None):
        e = e or self.dve
        return self.op(e, lambda h: h.tensor_tensor(out=out.ap, in0=a.ap, in1=b.ap, op=op), r=[a, b], w=[out])

    def ts(self, out, a, s1, s2=None, op0=ALU.mult, op1=None, e=None):
        e = e or self.dve
        r = [a]
        v1 = s1.ap if isinstance(s1, V) else s1
        v2 = s2.ap if isinstance(s2, V) else s2
        if isinstance(s1, V):
            r.append(s1)
        if isinstance(s2, V):
            r.append(s2)
        if op1 is None:
            return self.op(e, lambda h: h.tensor_scalar(out=out.ap, in0=a.ap, scalar1=v1, scalar2=None, op0=op0), r=r, w=[out])
        return self.op(e, lambda h: h.tensor_scalar(out=out.ap, in0=a.ap, scalar1=v1, scalar2=v2, op0=op0, op1=op1), r=r, w=[out])

    def stt(self, out, a, s, b, op0, op1, e=None):
        e = e or self.dve
        r = [a, b]
        sv = s.ap if isinstance(s, V) else s
        if isinstance(s, V):
            r.append(s)
        return self.op(e, lambda h: h.scalar_tensor_tensor(out=out.ap, in0=a.ap, scalar=sv, in1=b.ap, op0=op0, op1=op1), r=r, w=[out])

    def copy(self, out, a, e=None):
        e = e or self.dve
        return self.op(e, lambda h: h.tensor_copy(out=out.ap, in_=a.ap), r=[a], w=[out])

    def memset(self, out, val, e=None):
        e = e or self.dve
        return self.op(e, lambda h: h.memset(out.ap, val), r=[], w=[out])

    def scan(self, out, d0, d1, init, op0=ALU.mult, op1=ALU.add):
        r = [d0, d1]
        iv = init.ap if isinstance(init, V) else init
        if isinstance(init, V):
            r.append(init)
        return self.op(self.dve, lambda h: h.tensor_tensor_scan(out=out.ap, data0=d0.ap, data1=d1.ap, initial=iv, op0=op0, op1=op1), r=r, w=[out])


import math

NSEQ = 2
TT = 2048
DM = 1024
DFF = 2816
ALPHA = 2.0 ** 0.25
EPS = 1e-5
TWO_PI = 6.283185
C_ID, C_TRI, C_ONEG, C_CM, C_OND, C_BM, C_RM, C_K9, C_CI, C_END = 0, 128, 256, 384, 512, 640, 768, 770, 779, 1035


def host_consts():
    c = np.zeros((128, C_END), np.float32)
    i = np.arange(128)
    c[:, C_ID:C_ID + 128] = np.eye(128)
    c[:, C_TRI:C_TRI + 128] = -1.0 * (i[:, None] >= i[None, :])
    c[:, C_ONEG:C_ONEG + 128] = -1.0
    c[:, C_CM:C_CM + 128] = (i[None, :] > i[:, None])
    c[:, C_OND:C_OND + 128] = 1.0 / DM
    c[:, C_BM:C_BM + 128] = (i[:, None] // 16 == i[None, :] // 16)
    c[:, C_RM] = ((i // 16) % 2 == 0)
    c[:, C_RM + 1] = ((i // 16) % 2 == 1)
    c[:, C_K9:C_K9 + 9] = np.arange(9)[None, :]
    c[:, C_CI:C_CI + 256] = np.arange(256)[None, :]
    return c


class Arena:
    def __init__(self, kb, name, nf32):
        self.kb = kb
        self.t = kb.es.enter_context(kb.nc.sbuf_tensor(name, [128, nf32], F32))
        self.n = nf32
        self.off = 0

    def reset(self, off=0):
        self.off = off

    def alloc(self, shape, dt):
        n = 1
        for s in shape:
            n *= s
        nb = n * (4 if dt in (F32, I32) else 2)
        nf = (nb + 3) // 4
        nf = (nf + 7) // 8 * 8
        assert self.off + nf <= self.n, ("arena overflow", self.off, nf, self.n)
        ap = self.t[:, self.off:self.off + nf]
        self.off += nf
        if dt != F32:
            ap = ap.bitcast(dt)
        ap = ap[:, 0:n]
        if len(shape) == 2:
            ap = ap.rearrange("p (a b) -> p a b", a=shape[0])
        elif len(shape) == 3:
            ap = ap.rearrange("p (a b c) -> p a b c", a=shape[0], b=shape[1])
        elif len(shape) == 4:
            ap = ap.rearrange("p (a b c d) -> p a b c d", a=shape[0], b=shape[1], c=shape[2])
        return T(ap)


def barrier(kb, dsems):
    engs = [kb.pe, kb.act, kb.dve, kb.pool, kb.sp]
    for e in engs:
        for o in engs:
            if o.cnt == 0 or o is kb.sp:
                continue
            if e.seen.get(o.sem, 0) < o.cnt:
                e.h.wait_ge(o.sem, o.cnt)
                e.seen[o.sem] = o.cnt
        for ds in dsems:
            if ds.cnt and e.seen.get(ds.sem, 0) < ds.cnt:
                e.h.wait_ge(ds.sem, ds.cnt)
                e.seen[ds.sem] = ds.cnt


def build(stop=None):
    kb = KB()
    nc = kb.nc
    DS = []

    PIDX = [0]

    def dsem(name):
        i = PIDX[0]
        PIDX[0] += 1
        if i < len(DS):
            return DS[i]
        d = DSem(kb, f"g{i}")
        DS.append(d)
        return d

    def ein(name, shape):
        return kb.dram(name, shape, F32, kind="ExternalInput")

    x = ein("x", [NSEQ, TT, DM])
    pin = ein("p", [NSEQ, TT, 256])
    w_in = ein("w_in", [DM, 4096])
    w_sbo = ein("w_sb_out", [512, DM])
    a_re = ein("ssm_a_re", [32, 64])
    a_im = ein("ssm_a_im", [32, 64])
    lstep = ein("ssm_log_step", [32])
    b_re = ein("ssm_b_re", [32, 64, 16])
    b_im = ein("ssm_b_im", [32, 64, 16])
    c_re = ein("ssm_c_re", [32, 16, 64])
    c_im = ein("ssm_c_im", [32, 16, 64])
    d_in = ein("ssm_d", [32, 16])
    w_glu = ein("w_glu", [512, 2048])
    w_o = ein("w_o", [DM, DM])
    ln1_g = ein("ln1_g", [DM])
    ln1_b = ein("ln1_b", [DM])
    w_up = ein("w_up", [DM, 2 * DFF])
    conv_w = ein("conv_w", [3, 2 * DFF])
    conv_b = ein("conv_b", [2 * DFF])
    w_dn = ein("w_down", [DFF, DM])
    w_pe = ein("w_pe", [256, DM])
    w_peg = ein("w_pe_gate", [DM, DM])
    ln2_g = ein("ln2_g", [DM])
    ln2_b = ein("ln2_b", [DM])
    cst_d = ein("cst", [128, C_END])
    out = kb.dram("out", [NSEQ, TT, DM], F32, kind="ExternalOutput")
    ds_dbg = DSem(kb, "dbg")
    DSX = [ds_dbg]

    def dump(name, view, shape, dt):
        d = kb.dram("dbg_" + name, shape, dt, kind="ExternalOutput")
        kb.dma(kb.act, d[:], view, ds_dbg)

    def finish():
        barrier(kb, DS + DSX)
        return kb

    def wscratch(name, src, K, N):
        wb = kb.dram("wb_" + name, [N // 128, 128, K // 128, 128], BF16)
        ds = DSem(kb, "c_" + name)
        sv = src.t.rearrange("(kt p) (nt c) -> nt p kt c", p=128, c=128)
        import os as _os2
        for nt in range(N // 128):
            if _os2.environ.get('KNOCAST'):
                break
            kb.dma(kb.pool, V(wb.t[nt], wb.reg()), V(sv[nt], src.reg()), ds)
        return wb

    ds_c = dsem("cst")
    cst = kb.sb("cst_sb", [128, C_END], F32)
    kb.dma(kb.sp, cst[:], cst_d[:], ds_c)
    Wb_in = wscratch("in", w_in, DM, 4096)

    def cs(a, n=128, rows=slice(None)):
        return V(cst.t[rows, a:a + n], cst.reg())

    ident = cs(C_ID)
    trineg = kb.sb("trineg", [128, 128], BF16)
    onesneg = kb.sb("onesneg", [128, 128], BF16)
    onesD = kb.sb("onesD", [128, 128], BF16)
    kb.copy(trineg[:], cs(C_TRI))
    kb.copy(onesneg[:], cs(C_ONEG))
    kb.copy(onesD[:], cs(C_OND))
    cmask = cs(C_CM)

    if stop == "0":
        dump("tri", trineg[:], [128, 128], BF16)
        return finish()
    PS = [kb.ps(f"bank{i}", [128, 512], F32) for i in range(8)]
    attnT = kb.sb("attnT", [128, 4, TT], BF16)
    CPA = kb.sb("cpa", [128, 88], F32)
    CPB = kb.sb("cpb", [128, 120], F32)
    halo = kb.sb("halo", [128, 44, 2, 2], F32)
    hcT = kb.sb("hcT", [128, 44, 2], F32)
    AR_N = 44160
    arena = Arena(kb, "arena", AR_N)

    XTB = kb.dram("xtb", [NSEQ, 4, 128, 8, 512], BF16)
    XTF = kb.dram("xtf", [NSEQ, 4, 128, 8, 512], F32)
    PTB = kb.dram("ptb", [NSEQ, 4, 128, 2, 512], BF16)
    YTB = kb.dram("ytb", [NSEQ, 4, 128, 4, 512], BF16)

    arena.reset()
    xs = [arena.alloc([DM], F32) for _ in range(8)]
    pst = [arena.alloc([256], F32) for _ in range(8)]
    xtb_st = [arena.alloc([8, 512], BF16) for _ in range(2)]
    xtf_st = [arena.alloc([8, 512], F32) for _ in range(2)]
    ptb_st = [arena.alloc([2, 512], BF16) for _ in range(2)]
    ds_xs = [dsem(f"xs{i}") for i in range(8)]
    ds_ps = [dsem(f"ps{i}") for i in range(8)]
    ds_sx = [dsem("spx0"), dsem("spx1")]
    ds_sf = [dsem("spf0"), dsem("spf1")]
    ds_spp = [dsem("spp0"), dsem("spp1")]
    import os as _os
    cnt = 0
    for seq in range(NSEQ):
        for tc in range(4):
            par = cnt % 2
            cnt += 1
            for t4 in range(4):
                sl = par * 4 + t4
                tok0 = tc * 512 + t4 * 128
                kb.dma(kb.sp, xs[sl][:], V(x.t[seq, tok0:tok0 + 128, :], x.reg()), ds_xs[sl])
                if not _os.environ.get("KNOP"):
                    kb.dma(kb.sp, pst[sl][:], V(pin.t[seq, tok0:tok0 + 128, :], pin.reg()), ds_ps[sl])
            import os as _os
            _kd = int(_os.environ.get("KDBG", "0"))
            if _kd == 1:
                return finish()
            for dt_ in range(8):
                bank = PS[dt_ % 4]
                for t4 in range(4):
                    sl = par * 4 + t4
                    kb.transpose(V(bank.t[:, t4 * 128:(t4 + 1) * 128], bank.reg()),
                                 V(xs[sl].t[:, dt_ * 128:(dt_ + 1) * 128], xs[sl].reg()), ident, inc=(t4 == 3))
                if _os.environ.get("KNOEV") not in ("1", "3"):
                    kb.actf(V(xtb_st[par].t[:, dt_, :], xtb_st[par].reg()), bank[:], AF.Copy)
                if _os.environ.get("KNOEV") not in ("1", "2"):
                    kb.copy(V(xtf_st[par].t[:, dt_, :], xtf_st[par].reg()), bank[:])
            for d2 in range(0 if _os.environ.get("KNOP") else 2):
                bank = PS[4 + d2]
                for t4 in range(4):
                    sl = par * 4 + t4
                    kb.transpose(V(bank.t[:, t4 * 128:(t4 + 1) * 128], bank.reg()),
                                 V(pst[sl].t[:, d2 * 128:(d2 + 1) * 128], pst[sl].reg()), ident, inc=(t4 == 3))
                kb.actf(V(ptb_st[par].t[:, d2, :], ptb_st[par].reg()), bank[:], AF.Copy)
            if _kd == 2:
                return finish()
            _kv = int(_os.environ.get("KVAR", "0"))
            _tcw = 0 if _kv == 4 else tc
            if _kv == 6 and cnt == 1:
                _kv = 1
            _qe = kb.act
            if _kv in (0, 2, 4, 6):
                kb.dma(_qe, V(XTB.t[seq, _tcw], XTB.reg((seq, tc))), xtb_st[par][:], ds_sx[par])
            if _kv in (0, 3, 4, 6):
                kb.dma(_qe, V(XTF.t[seq, _tcw], XTF.reg((seq, tc))), xtf_st[par][:], ds_sf[par])
            if _kv in (0, 5, 4, 6):
                kb.dma(_qe, V(PTB.t[seq, _tcw], PTB.reg((seq, tc))), ptb_st[par][:], ds_spp[par])
            if _kd == 3 or (_kd >= 10 and cnt == _kd - 10):
                return finish()

    if stop == "A0":
        return finish()
    if stop == "A":
        barrier(kb, DS)
        dump("xtb", V(XTB.t[:], XTB.reg()), [NSEQ, 4, 128, 8, 512], BF16)
        dump("xtf", V(XTF.t[:], XTF.reg()), [NSEQ, 4, 128, 8, 512], F32)
        dump("ptb", V(PTB.t[:], PTB.reg()), [NSEQ, 4, 128, 2, 512], BF16)
        return finish()
    barrier(kb, DS)
    PIDX[0] = 0
    arena.reset()
    W_end = arena.alloc([4, 8, 2, 128], BF16)
    W_car = arena.alloc([16, 8, 2, 32], BF16)
    Kblk = arena.alloc([4, 8, 128], BF16)
    f8s = arena.alloc([16], F32)
    r8s = arena.alloc([16], F32)
    p_off = arena.off
    ds_na = dsem("natA"); ds_nb = dsem("natB"); ds_nl = dsem("natL"); ds_ls = dsem("LS")
    ds_br = dsem("Bre"); ds_bi = dsem("Bim"); ds_ncc = dsem("natC"); ds_dc = dsem("Dcol")
    natA = arena.alloc([128], F32)
    natB = arena.alloc([128], F32)
    kb.memset(natA[:], 0.0)
    kb.memset(natB[:], 0.0)
    cw_v = conv_w.t.rearrange("k (n p) -> k n p", p=128)
    kb.dma(kb.sp, V(natA.t[0:44, :], natA.reg()), V(cw_v[0], conv_w.reg()), ds_na)
    kb.dma(kb.sp, V(natA.t[44:88, :], natA.reg()), V(cw_v[1], conv_w.reg()), ds_na)
    kb.dma(kb.sp, V(natB.t[0:44, :], natB.reg()), V(cw_v[2], conv_w.reg()), ds_nb)
    kb.dma(kb.sp, V(natB.t[44:88, :], natB.reg()), V(conv_b.t.rearrange("(n p) -> n p", p=128), conv_b.reg()), ds_nb)
    for k, prm in enumerate([ln1_g, ln1_b, ln2_g, ln2_b]):
        kb.dma(kb.sp, V(natB.t[88 + 8 * k:96 + 8 * k, :], natB.reg()), V(prm.t.rearrange("(n p) -> n p", p=128), prm.reg()), ds_nb)
    bk = PS[0]
    kb.transpose(V(bk.t[:, 0:128], bk.reg()), natA[:], ident)
    kb.copy(CPA[:], V(bk.t[:, 0:88], bk.reg()))
    bk = PS[1]
    kb.transpose(V(bk.t[:, 0:128], bk.reg()), natB[:], ident)
    kb.copy(CPB[:], V(bk.t[:, 0:120], bk.reg()))

    def cwcol(k, c44):
        if k < 2:
            return V(CPA.t[:, k * 44 + c44:k * 44 + c44 + 1], CPA.reg())
        return V(CPB.t[:, c44:c44 + 1], CPB.reg())

    def cbcol(c44):
        return V(CPB.t[:, 44 + c44:45 + c44], CPB.reg())

    def lncol(k, n):
        return V(CPB.t[:, 88 + 8 * k + n:89 + 8 * k + n], CPB.reg())

    natL = arena.alloc([2, 128], F32)
    kb.memset(natL[:], 0.0)
    for ri, prm in enumerate([a_re, a_im]):
        for dup in range(2):
            kb.dma(kb.sp, V(natL.t[0:32, ri, dup * 64:(dup + 1) * 64], natL.reg()), prm[:], ds_nl)
    ARt = arena.alloc([32], F32)
    AIt = arena.alloc([32], F32)
    for ri, dst in enumerate([ARt, AIt]):
        bk = PS[2 + ri]
        kb.transpose(V(bk.t[:, 0:32], bk.reg()), V(natL.t[0:32, ri, :], natL.reg()), V(cst.t[0:32, C_ID:C_ID + 32], cst.reg()))
        kb.copy(dst[:], V(bk.t[:, 0:32], bk.reg()))
    LS = arena.alloc([32], F32)
    kb.dma(kb.sp, LS[:], V(lstep.t.partition_broadcast(128), lstep.reg()), ds_ls)
    Bre = arena.alloc([32, 16], F32)
    Bim = arena.alloc([32, 16], F32)
    for dst, prm, dsb in [(Bre, b_re, ds_br), (Bim, b_im, ds_bi)]:
        for dup in range(2):
            kb.dma(kb.sp, V(dst.t[dup * 64:(dup + 1) * 64], dst.reg()), V(prm.t.rearrange("g p h -> p g h"), prm.reg()), dsb)
    natC = arena.alloc([4, 2, 128], F32)
    for ri, prm in enumerate([c_re, c_im]):
        for dup in range(2):
            kb.dma(kb.sp, V(natC.t[:, :, ri, dup * 64:(dup + 1) * 64], natC.reg()),
                   V(prm.t.rearrange("g h p -> (g h) p").rearrange("(r q) p -> q r p", q=128), prm.reg()), ds_ncc)
    CTre = arena.alloc([32, 16], F32)
    CTim = arena.alloc([32, 16], F32)
    for ri, dst in enumerate([CTre, CTim]):
        for r in range(4):
            bk = PS[4 + (r % 2) + 2 * ri]
            kb.transpose(V(bk.t[:, 0:128], bk.reg()), V(natC.t[:, r, ri, :], natC.reg()), ident)
            kb.copy(V(dst.t[:, r * 8:(r + 1) * 8, :].rearrange("p g h -> p (g h)"), dst.reg()), V(bk.t[:, 0:128], bk.reg()))
    Dcol = arena.alloc([4], F32)
    kb.dma(kb.sp, Dcol[:], V(d_in.t.rearrange("(s gl) h -> (gl h) s", gl=8), d_in.reg()), ds_dc, allow_slow_non_contiguous=True)

    def frac(dst, src, shape):
        ti = arena.alloc(shape, I32)
        tg = arena.alloc(shape, F32)
        kb.copy(ti[:], src)
        kb.tt(tg[:], src, ti[:], ALU.subtract)
        kb.stt(tg[:], tg[:], 0.5, tg[:], ALU.is_gt, ALU.subtract)
        kb.stt(dst, tg[:], 0.5, tg[:], ALU.is_gt, ALU.subtract)

    def al(shape):
        return arena.alloc(shape, F32)

    step = al([32]); lr = al([32]); thn = al([32]); f1 = al([32])
    kb.actf(step[:], LS[:], AF.Exp)
    kb.tt(lr[:], ARt[:], step[:], ALU.mult)
    kb.tt(thn[:], AIt[:], step[:], ALU.mult)
    kb.ts(thn[:], thn[:], 1.0 / (2 * math.pi), None, op0=ALU.mult)
    frac(f1[:], thn[:], [32])
    K9 = V(cst.t[:, C_K9:C_K9 + 9].unsqueeze(2).to_broadcast([128, 9, 32]), cst.reg())

    def b9(t):
        return V(t.t[:, :].unsqueeze(1).to_broadcast([128, 9, 32]), t.reg())
    klr = al([9, 32]); mag = al([9, 32]); kf = al([9, 32]); ang = al([9, 32]); kf2 = al([9, 32]); angc = al([9, 32])
    Sn = al([9, 32]); Cs = al([9, 32]); PR = al([9, 32]); PI_ = al([9, 32])
    kb.tt(klr[:], K9, b9(lr), ALU.mult)
    kb.actf(mag[:], klr[:], AF.Exp)
    kb.tt(kf[:], K9, b9(f1), ALU.mult)
    frac(ang[:], kf[:], [9, 32])
    kb.ts(kf2[:], kf[:], 0.25, None, op0=ALU.add)
    frac(angc[:], kf2[:], [9, 32])
    kb.actf(Sn[:], ang[:], AF.Sin, scale=TWO_PI)
    kb.actf(Cs[:], angc[:], AF.Sin, scale=TWO_PI)
    kb.tt(PR[:], mag[:], Cs[:], ALU.mult)
    kb.tt(PI_[:], mag[:], Sn[:], ALU.mult)
    den = al([32]); t1 = al([32]); t2 = al([32]); nr = al([32]); cre = al([32]); cim = al([32])
    PR1 = V(PR.t[:, 1, :], PR.reg()); PI1 = V(PI_.t[:, 1, :], PI_.reg())
    kb.tt(den[:], ARt[:], ARt[:], ALU.mult)
    kb.tt(t1[:], AIt[:], AIt[:], ALU.mult)
    kb.tt(den[:], den[:], t1[:], ALU.add)
    kb.op(kb.dve, lambda h: h.reciprocal(out=den.t[:], in_=den.t[:]), r=[den[:]], w=[den[:]])
    kb.ts(nr[:], PR1, -1.0, None, op0=ALU.add)
    kb.tt(t1[:], nr[:], ARt[:], ALU.mult)
    kb.tt(t2[:], PI1, AIt[:], ALU.mult)
    kb.tt(t1[:], t1[:], t2[:], ALU.add)
    kb.tt(cre[:], t1[:], den[:], ALU.mult)
    kb.tt(t1[:], PI1, ARt[:], ALU.mult)
    kb.tt(t2[:], nr[:], AIt[:], ALU.mult)
    kb.tt(t1[:], t1[:], t2[:], ALU.subtract)
    kb.tt(cim[:], t1[:], den[:], ALU.mult)
    bbr = al([32, 16]); bbi = al([32, 16]); u1 = al([32, 16]); u2 = al([32, 16])

    def bh(t):
        return V(t.t[:, :].unsqueeze(2).to_broadcast([128, 32, 16]), t.reg())
    kb.tt(u1[:], bh(cre), Bre[:], ALU.mult)
    kb.tt(u2[:], bh(cim), Bim[:], ALU.mult)
    kb.tt(bbr[:], u1[:], u2[:], ALU.subtract)
    kb.tt(u1[:], bh(cre), Bim[:], ALU.mult)
    kb.tt(u2[:], bh(cim), Bre[:], ALU.mult)
    kb.tt(bbi[:], u1[:], u2[:], ALU.add)
    WEr = al([8, 32, 16]); WEi = al([8, 32, 16]); X1 = al([8, 32, 16]); X2 = al([8, 32, 16])

    def pk(t, k0):
        return V(t.t[:, k0:k0 + 8, :].unsqueeze(3).to_broadcast([128, 8, 32, 16]), t.reg())

    def bk8(t):
        return V(t.t[:, :, :].unsqueeze(1).to_broadcast([128, 8, 32, 16]), t.reg())

    def cprod(outr, outi, k0, vr, vi, neg_i=False):
        kb.tt(X1[:], pk(PR, k0), vr, ALU.mult)
        kb.tt(X2[:], pk(PI_, k0), vi, ALU.mult)
        kb.tt(outr[:], X1[:], X2[:], ALU.subtract)
        kb.tt(X1[:], pk(PR, k0), vi, ALU.mult)
        kb.tt(X2[:], pk(PI_, k0), vr, ALU.mult)
        kb.tt(outi[:], X1[:], X2[:], ALU.add)
    cprod(WEr, WEi, 0, bk8(bbr), bk8(bbi))
    MK = al([8, 512]); CK = al([512])
    kb.copy(V(MK.t[0:64], MK.reg()), V(WEr.t[0:64].rearrange("p k g h -> p k (g h)"), WEr.reg()))
    kb.copy(V(MK.t[64:128], MK.reg()), V(WEi.t[64:128].rearrange("p k g h -> p k (g h)"), WEi.reg()))
    kb.copy(V(CK.t[0:64], CK.reg()), V(CTre.t[0:64].rearrange("p g h -> p (g h)"), CTre.reg()))
    kb.ts(V(CK.t[64:128], CK.reg()), V(CTim.t[64:128].rearrange("p g h -> p (g h)"), CTim.reg()), -1.0, None, op0=ALU.mult)
    ktmp = al([128])
    for s in range(4):
        for tau in range(8):
            bk = PS[(s * 8 + tau) % 4]
            kb.mm(V(bk.t[:, 0:128], bk.reg()), V(MK.t[:, tau, s * 128:(s + 1) * 128], MK.reg()),
                  V(CK.t[:, s * 128:(s + 1) * 128], CK.reg()))
            if tau == 0:
                kb.tt(ktmp[:], V(bk.t[:, 0:128], bk.reg()), cs(C_BM), ALU.mult)
                kb.stt(V(Kblk.t[:, s, tau, :], Kblk.reg()), ident, V(Dcol.t[:, s:s + 1], Dcol.reg()), ktmp[:], ALU.mult, ALU.add)
            else:
                kb.tt(V(Kblk.t[:, s, tau, :], Kblk.reg()), V(bk.t[:, 0:128], bk.reg()), cs(C_BM), ALU.mult)
    id64 = V(cst.t[0:64, C_ID:C_ID + 64], cst.reg())
    nb = 0
    for s in range(4):
        for i in range(8):
            k = 7 - i
            for ri, WE in enumerate([WEr, WEi]):
                bk = PS[4 + nb % 4]
                nb += 1
                kb.transpose(V(bk.t[:, 0:64], bk.reg()),
                             V(WE.t[0:64, k, s * 8:(s + 1) * 8, :].rearrange("p g h -> p (g h)"), WE.reg()), id64)
                for gp in range(2):
                    wv_ = V(W_end.t[:, s, i, ri, gp * 64:(gp + 1) * 64], W_end.reg((s, i, ri, gp)))
                    rmc = V(cst.t[:, C_RM + gp:C_RM + gp + 1], cst.reg())
                    if gp == 0:
                        kb.ts(wv_, V(bk.t[:, 0:64], bk.reg()), rmc, None, op0=ALU.mult)
                    else:
                        kb.actf(wv_, V(bk.t[:, 0:64], bk.reg()), AF.Identity, scale=rmc)
    cprod(WEr, WEi, 1, bk8(CTre), bk8(CTim))
    kb.memset(W_car[:], 0.0, e=kb.pool)
    for gp in range(2):
        hs = slice(gp * 64, (gp + 1) * 64)
        for ri, WE in enumerate([WEr, WEi]):
            src = V(WE.t[hs, :, gp::2, :].rearrange("p j q h -> p q j h"), WE.reg())
            dst = V(W_car.t[hs, :, :, ri, gp * 16:(gp + 1) * 16], W_car.reg())
            if ri == 0:
                kb.copy(dst, src)
            else:
                kb.ts(dst, src, -1.0, None, op0=ALU.mult)
        kb.copy(V(f8s.t[hs, :], f8s.reg()), V(ang.t[hs, 8, gp::2], ang.reg()))
        kb.copy(V(r8s.t[hs, :], r8s.reg()), V(mag.t[hs, 8, gp::2], mag.reg()))

    barrier(kb, DS)
    PIDX[0] = 0
    if stop == "P":
        dump("wend", W_end[:], [128, 4, 8, 2, 128], BF16)
        dump("wcar", W_car[:], [128, 16, 8, 2, 32], BF16)
        dump("kblk", Kblk[:], [128, 4, 8, 128], BF16)
        dump("f8s", f8s[:], [128, 16], F32)
        dump("r8s", r8s[:], [128, 16], F32)
        dump("cpa", CPA[:], [128, 88], F32)
        dump("cpb", CPB[:], [128, 120], F32)
        dump("pr", PR[:], [128, 9, 32], F32)
        dump("pi", PI_[:], [128, 9, 32], F32)
        return finish()
    arena.reset(p_off)
    UD = arena.alloc([4, 8, 256], BF16)
    XH = arena.alloc([16, 2, 257], BF16)
    yT = arena.alloc([4, TT], BF16)
    wu = arena.alloc([4, 8, 128], BF16)
    xc = [arena.alloc([8, 512], BF16) for _ in range(2)]
    RCall = arena.alloc([16, 256], F32)
    RSall = arena.alloc([16, 256], F32)
    SA2 = [arena.alloc([256], F32) for _ in range(2)]; SB2 = [arena.alloc([256], F32) for _ in range(2)]
    SA3 = [arena.alloc([256], F32) for _ in range(2)]
    RT = [dict(phi=arena.alloc([256], F32), phf=arena.alloc([256], F32), ti=arena.alloc([256], I32), tg=arena.alloc([256], F32))
          for _ in range(2)]
    Ep = [arena.alloc([2, 256], F32) for _ in range(2)]
    Wp = [arena.alloc([2, 256], F32) for _ in range(2)]
    SA = [arena.alloc([256], F32) for _ in range(2)]; SB = [arena.alloc([256], F32) for _ in range(2)]
    s_frac_off = arena.off

    def interleave(*chains):
        n = max(len(c) for c in chains)
        for k in range(n):
            for c in chains:
                if k < len(c):
                    c[k]()
    ds_wu = dsem("wu"); ds_xc = [dsem("xc0"), dsem("xc1")]; ds_y = [dsem(f"ysp{i}") for i in range(4)]
    kb.memset(XH[:], 0.0)
    CI = cs(C_CI, 256)
    def rot_chain(pair, t):
        phi, phf, ti, tg = t["phi"], t["phf"], t["ti"], t["tg"]

        def fr(dst):
            return [lambda: kb.copy(ti[:], phi[:]),
                    lambda: kb.tt(tg[:], phi[:], ti[:], ALU.subtract),
                    lambda: kb.stt(tg[:], tg[:], 0.5, tg[:], ALU.is_gt, ALU.subtract),
                    lambda: kb.stt(dst[:], tg[:], 0.5, tg[:], ALU.is_gt, ALU.subtract)]
        ch = [lambda: kb.ts(phi[:], CI, V(f8s.t[:, pair:pair + 1], f8s.reg()), None, op0=ALU.mult)]
        ch += fr(phf)
        ch += [lambda: kb.actf(V(RSall.t[:, pair, :], RSall.reg(pair)), phf[:], AF.Sin, scale=TWO_PI),
               lambda: kb.ts(tg[:], phf[:], 0.25, None, op0=ALU.add),
               lambda: kb.stt(tg[:], tg[:], 0.5, tg[:], ALU.is_gt, ALU.subtract),
               lambda: kb.stt(phi[:], tg[:], 0.5, tg[:], ALU.is_gt, ALU.subtract),
               lambda: kb.actf(V(RCall.t[:, pair, :], RCall.reg(pair)), phi[:], AF.Sin, scale=TWO_PI)]
        return ch

    for pair in range(0, 16, 2):
        interleave(rot_chain(pair, RT[0]), rot_chain(pair + 1, RT[1]))
    xcn = 0
    for seq in range(NSEQ):
        kb.dma(kb.sp, wu[:], V(Wb_in.t[12:16].rearrange("n p k c -> p n k c"), Wb_in.reg()), ds_wu)
        for tc in range(4):
            xb = xc[xcn % 2]
            kb.dma(kb.sp, xb[:], V(XTB.t[seq, tc], XTB.reg((seq, tc))), ds_xc[xcn % 2])
            xcn += 1
            for s in range(4):
                bk = PS[s]
                for kt in range(8):
                    kb.mm(bk[:], V(wu.t[:, s, kt, :], wu.reg()), V(xb.t[:, kt, :], xb.reg()), start=(kt == 0), stop=(kt == 7))
                dstv = V(UD.t[:, s, :, tc * 64:(tc + 1) * 64], UD.reg())
                srcv = V(bk.t[:, :].rearrange("p (c i) -> p i c", i=8), bk.reg())
                if s % 2 == 0:
                    kb.actf(dstv, srcv, AF.Copy)
                else:
                    kb.copy(dstv, srcv)
        def pair_chains(pair):
            s, q = pair // 4, pair % 4
            bk = PS[4 + pair % 4]
            for ri in range(2):
                for i in range(8):
                    kb.mm(V(bk.t[:, ri * 256:(ri + 1) * 256], bk.reg()),
                          V(W_end.t[32 * q:32 * q + 32, s, i, ri, :], W_end.reg()),
                          V(UD.t[32 * q:32 * q + 32, s, i, :], UD.reg()),
                          start=(i == 0), stop=(i == 7), tile_position=(32 * q, 0))
            k2 = pair % 2
            rc = V(RCall.t[:, pair, :], RCall.reg(pair)); rs = V(RSall.t[:, pair, :], RSall.reg(pair))
            ep, wp = Ep[k2], Wp[k2]
            sa, sb_, sa2, sb2, sa3 = SA[k2], SB[k2], SA2[k2], SB2[k2], SA3[k2]
            ere = V(bk.t[:, 0:256], bk.reg()); eim = V(bk.t[:, 256:512], bk.reg())
            epr = V(ep.t[:, 0, :], ep.reg()); epi = V(ep.t[:, 1, :], ep.reg())
            dec = V(r8s.t[:, pair:pair + 1].to_broadcast([128, 256]), r8s.reg())
            wr = V(wp.t[:, 0, :], wp.reg()); wi = V(wp.t[:, 1, :], wp.reg())
            PL = kb.pool
            xre = V(XH.t[:, pair, 0, 1:257], XH.reg(pair)); xim = V(XH.t[:, pair, 1, 1:257], XH.reg(pair))
            dchain = [lambda: kb.tt(sa[:], rc, ere, ALU.mult),
                      lambda: kb.tt(sb_[:], rs, eim, ALU.mult),
                      lambda: kb.tt(epr, sa[:], sb_[:], ALU.add),
                      lambda: kb.tt(sa[:], rc, eim, ALU.mult),
                      lambda: kb.tt(sb_[:], rs, ere, ALU.mult),
                      lambda: kb.tt(epi, sa[:], sb_[:], ALU.subtract),
                      lambda: kb.scan(wr, dec, epr, 0.0),
                      lambda: kb.scan(wi, dec, epi, 0.0)]
            pchain = [lambda: kb.tt(sa2[:], wr, rc, ALU.mult, e=PL),
                      lambda: kb.tt(sb2[:], wi, rs, ALU.mult, e=PL),
                      lambda: kb.tt(xre, sa2[:], sb2[:], ALU.subtract, e=PL),
                      lambda: kb.tt(sa3[:], wi, rc, ALU.mult, e=PL)]
            dtail = [lambda: kb.tt(sb_[:], wr, rs, ALU.mult),
                     lambda: kb.tt(xim, sa3[:], sb_[:], ALU.add)]
            return dchain, pchain, dtail

        for pair in range(0, 16, 2):
            da, pa, ta = pair_chains(pair)
            db, pb, tb = pair_chains(pair + 1)
            interleave(da, db)
            interleave(pa, pb)
            interleave(ta, tb)
        nb = 0
        for s in range(4):
            for j in range(8):
                bk = PS[nb % 4]
                half = (nb // 4) % 2
                nb += 1
                yv = V(bk.t[:, half * 256:(half + 1) * 256], bk.reg())
                for i in range(j + 1):
                    kb.mm(yv, V(Kblk.t[:, s, j - i, :], Kblk.reg()), V(UD.t[:, s, i, :], UD.reg()), start=(i == 0), stop=False, inc=False)
                for q in range(4):
                    pair = s * 4 + q
                    for ri in range(2):
                        last = (q == 3 and ri == 1)
                        kb.mm(V(bk.t[32 * q:32 * q + 32, half * 256:(half + 1) * 256], bk.reg()),
                              V(W_car.t[:, pair, j, ri, :], W_car.reg()),
                              V(XH.t[:, pair, ri, 0:256], XH.reg(pair)),
                              start=False, stop=(ri == 1), inc=last, tile_position=(0, 32 * q))
                kb.actf(V(yT.t[:, s, j::8], yT.reg()), yv, AF.Gelu_apprx_tanh)
        for tc in range(4):
            kb.dma(kb.act, V(YTB.t[seq, tc], YTB.reg((seq, tc))), V(yT.t[:, :, tc * 512:(tc + 1) * 512], yT.reg()), ds_y[tc])

    if stop == "S":
        barrier(kb, DS)
        dump("ytb", V(YTB.t[:], YTB.reg()), [NSEQ, 4, 128, 4, 512], BF16)
        dump("ud", UD[:], [128, 4, 8, 256], BF16)
        dump("xh", XH[:], [128, 16, 2, 257], BF16)
        return finish()
    Wb_sbo = wscratch("sbo", w_sbo, 512, DM)
    Wb_glu = wscratch("glu", w_glu, 512, 2048)
    Wb_o = wscratch("o", w_o, DM, DM)
    Wb_up = wscratch("up", w_up, DM, 2 * DFF)
    Wb_dn = wscratch("dn", w_dn, DFF, DM)
    Wb_pe = wscratch("pe", w_pe, 256, DM)
    Wb_peg = wscratch("peg", w_peg, DM, DM)

    for seq in range(NSEQ):
        barrier(kb, DS)
        PIDX[0] = 0
        ds_o = [dsem("o0"), dsem("o1")]
        arena.reset()
        qT = arena.alloc([4, TT], BF16)
        kz = arena.alloc([8, TT], BF16)
        vv = arena.alloc([16, 512], BF16)
        wqk = arena.alloc([8, 8, 128], BF16)
        wv = arena.alloc([4, 8, 128], BF16)
        xc = [arena.alloc([8, 512], BF16) for _ in range(2)]
        NBUF = 3
        e_t = [arena.alloc([512], F32) for _ in range(NBUF)]
        sp_t = [arena.alloc([512], BF16) for _ in range(NBUF)]
        e3_t = [arena.alloc([512], F32) for _ in range(NBUF)]
        w_t = [arena.alloc([512], BF16) for _ in range(NBUF)]
        A_t = arena.alloc([512], BF16)
        ds_w1 = dsem("wqk"); ds_w2 = dsem("wv"); ds_x = [dsem("qx0"), dsem("qx1")]
        kb.dma(kb.sp, wqk[:], V(Wb_in.t[0:8].rearrange("n p k c -> p n k c"), Wb_in.reg()), ds_w1)
        kb.dma(kb.sp, wv[:], V(Wb_in.t[8:12].rearrange("n p k c -> p n k c"), Wb_in.reg()), ds_w2)
        for tc_ in range(4):
            kb.memset(V(kz.t[:, :, tc_ * 512:(tc_ + 1) * 512], kz.reg(tc_)), 0.0)
        nb = 0
        for tc in range(4):
            xb = xc[tc % 2]
            kb.dma(kb.sp, xb[:], V(XTB.t[seq, tc], XTB.reg((seq, tc))), ds_x[tc % 2])
            for nt in range(8):
                bk = PS[nb % 4]
                nb += 1
                for kt in range(8):
                    kb.mm(bk[:], V(wqk.t[:, nt, kt, :], wqk.reg()), V(xb.t[:, kt, :], xb.reg()), start=(kt == 0), stop=(kt == 7))
                if nt < 4:
                    dv = V(qT.t[:, nt, tc * 512:(tc + 1) * 512], qT.reg(tc))
                    if nt % 2 == 0:
                        kb.actf(dv, bk[:], AF.Copy, scale=0.125)
                    else:
                        kb.ts(dv, bk[:], 0.125, None, op0=ALU.mult)
                else:
                    for hh in range(2):
                        hd = 2 * (nt - 4) + hh
                        hs_ = slice(hh * 64, hh * 64 + 64)
                        dv = V(kz.t[hs_, hd, tc * 512:(tc + 1) * 512], kz.reg(tc))
                        sv_ = V(bk.t[hs_, :], bk.reg())
                        if nt % 2 == 0:
                            kb.actf(dv, sv_, AF.Copy)
                        else:
                            kb.copy(dv, sv_)
            for t4 in range(4):
                bk = PS[4 + t4 % 4]
                for j in range(4):
                    for kt in range(8):
                        kb.mm(V(bk.t[:, j * 128:(j + 1) * 128], bk.reg()), V(xb.t[:, kt, t4 * 128:(t4 + 1) * 128], xb.reg()),
                              V(wv.t[:, j, kt, :], wv.reg()), start=(kt == 0), stop=(kt == 7), inc=(kt == 7 and j == 3))
                dv = V(vv.t[:, tc * 4 + t4, :], vv.reg(tc))
                if t4 % 2 == 0:
                    kb.actf(dv, bk[:], AF.Copy)
                else:
                    kb.copy(dv, bk[:])
        tiles = []
        for h in range(8):
            for qc in range(4):
                for kbk in range(4 * qc + 3, -1, -1):
                    m = kbk - 4 * qc
                    c0 = 128 * m if m > 0 else 0
                    tiles.append((h, qc, kbk, c0, m >= 0, kbk == 4 * qc + 3, kbk == 0))
        ZB = [PS[0], PS[1]]
        RB = [PS[2], PS[3]]
        OB = [PS[4], PS[5]]
        nt_ = len(tiles)
        ostate = {}

        def hp(h):
            return slice((h % 2) * 64, (h % 2) * 64 + 64)

        def stage1(i):
            h, qc, kbk, c0, diag, first, lastk = tiles[i]
            zb = ZB[i % 2]
            n = 512 - c0
            qv = V(qT.t[:, h // 2, qc * 512 + c0:(qc + 1) * 512], qT.reg(qc))
            kv = V(kz.t[:, h, kbk * 128:(kbk + 1) * 128], kz.reg(kbk // 4))
            kb.mm(V(zb.t[:, c0:512], zb.reg()), kv, qv)
            ev = V(e_t[i % NBUF].t[:, c0:512], e_t[i % NBUF].reg())
            kb.actf(ev, V(zb.t[:, c0:512], zb.reg()), AF.Exp)
            if diag:
                e1 = V(e_t[i % NBUF].t[:, c0:c0 + 128], e_t[i % NBUF].reg())
                kb.tt(e1, e1, cmask, ALU.mult)
            spv = V(sp_t[i % NBUF].t[:, c0:512], sp_t[i % NBUF].reg())
            kb.actf(spv, ev, AF.Ln, bias=1.0)

        def stage2(i):
            h, qc, kbk, c0, diag, first, lastk = tiles[i]
            rb = RB[i % 2]
            spv = V(sp_t[i % NBUF].t[:, c0:512], sp_t[i % NBUF].reg())
            rv = V(rb.t[:, c0:512], rb.reg())
            kb.mm(rv, trineg[:], spv, start=True, stop=first)
            if not first:
                kb.mm(rv, onesneg[:], V(A_t.t[:, c0:512], A_t.reg()), start=False, stop=True)
            if not lastk:
                if first:
                    kb.memset(A_t[:], 0.0)
                kb.tt(V(A_t.t[:, c0:512], A_t.reg()), V(A_t.t[:, c0:512], A_t.reg()), spv, ALU.add)
            e3v = V(e3_t[i % NBUF].t[:, c0:512], e3_t[i % NBUF].reg())
            kb.actf(e3v, rv, AF.Exp)
            ev = V(e_t[i % NBUF].t[:, c0:512], e_t[i % NBUF].reg())
            kb.tt(V(w_t[i % NBUF].t[:, c0:512], w_t[i % NBUF].reg()), ev, e3v, ALU.mult)
            if first and c0 > 0:
                kb.memset(V(w_t[i % NBUF].t[:, 0:c0], w_t[i % NBUF].reg()), 0.0)

        def stage3(i):
            h, qc, kbk, c0, diag, first, lastk = tiles[i]
            ob = OB[(h * 4 + qc) % 2]
            ov = V(ob.t[:, c0:512], ob.reg())
            vt = V(vv.t[:, kbk, (h // 2) * 128:(h // 2 + 1) * 128], vv.reg(kbk // 4))
            wt = w_t[i % NBUF]
            if first:
                kb.mm(ob[:], vt, wt[:], start=True, stop=lastk, inc=True)
            else:
                kb.mm(ov, vt, V(wt.t[:, c0:512], wt.reg()), start=False, stop=lastk, inc=True)
            if lastk:
                dv = V(attnT.t[hp(h), h // 2, qc * 512:(qc + 1) * 512], attnT.reg(qc))
                kb.copy(dv, V(ob.t[hp(h), :], ob.reg()))

        for st in range(nt_ + 2):
            if st < nt_:
                stage1(st)
            if 0 <= st - 1 < nt_:
                stage2(st - 1)
            if 0 <= st - 2 < nt_:
                stage3(st - 2)

        barrier(kb, DS)
        PIDX[0] = 2
        if stop == "Q":
            dump("attn", attnT[:], [128, 4, TT], BF16)
            dump("qT", qT[:], [128, 4, TT], BF16)
            dump("vv", vv[:], [128, 16, 512], BF16)
            return finish()
        arena.reset()
        xcb = arena.alloc([8, 512], BF16)
        ycb = arena.alloc([4, 512], BF16)
        pcb = arena.alloc([2, 512], BF16)
        xr = [arena.alloc([512], F32) for _ in range(2)]
        wga = [arena.alloc([8, 128], BF16) for _ in range(2)]
        wgs = [arena.alloc([8, 128], BF16) for _ in range(2)]
        wsb = [arena.alloc([4, 128], BF16) for _ in range(2)]
        wz1 = [arena.alloc([4, 128], BF16) for _ in range(2)]
        wz2 = [arena.alloc([4, 128], BF16) for _ in range(2)]
        wo_ = [arena.alloc([8, 128], BF16) for _ in range(4)]
        wuv = [arena.alloc([8, 128], BF16) for _ in range(3)]
        wug = [arena.alloc([8, 128], BF16) for _ in range(3)]
        wd_ = [arena.alloc([22, 128], BF16) for _ in range(2)]
        wpe_ = [arena.alloc([2, 128], BF16) for _ in range(2)]
        wpg_ = [arena.alloc([8, 128], BF16) for _ in range(2)]
        tmp = [arena.alloc([512], F32) for _ in range(6)]
        mixT = arena.alloc([8, 512], BF16)
        r1 = arena.alloc([8, 512], F32)
        h1 = arena.alloc([8, 512], F32)
        h1b = arena.alloc([8, 512], BF16)
        aT = arena.alloc([22, 512], BF16)
        mean_sb = arena.alloc([512], F32); m2 = arena.alloc([512], F32); rstd = arena.alloc([512], F32); lt = arena.alloc([512], F32)
        ot = [arena.alloc([DM], F32) for _ in range(2)]
        dsn = lambda nm, k=2: [dsem(nm + str(i_)) for i_ in range(k)]
        d_x = dsem("ex"); d_y = dsem("ey"); d_p = dsem("ep"); d_xr = dsn("xr")
        d_ga = dsn("ga"); d_gs = dsn("gs"); d_sb = dsn("sb"); d_z1 = dsn("z1"); d_z2 = dsn("z2"); d_wo = dsn("wo", 4)
        d_uv = dsn("uv", 3); d_ug = dsn("ug", 3); d_wd = dsn("wd"); d_pe = dsn("pe"); d_pg = dsn("pg")
        kb.memset(halo[:], 0.0)

        wlc = {}

        def wl(dst, dsl, wb, nt, i):
            k = wlc.get(id(dst), 0)
            wlc[id(dst)] = k + 1
            sl = k % len(dst)
            kb.dma(kb.sp, dst[sl][:], V(wb.t[nt], wb.reg()), dsl[sl])
            return dst[sl]

        bkc = [0]

        def nbk():
            b = PS[bkc[0] % 8]
            bkc[0] += 1
            return b

        def ln_pre(src, n):
            kb.actf(V(aT.t[:, n, :], aT.reg(n)), V(src.t[:, n, :], src.reg(n)), AF.Copy)
            kb.actf(V(aT.t[:, 8 + n, :], aT.reg(8 + n)), V(src.t[:, n, :], src.reg(n)), AF.Square)

        ltb = [arena.alloc([512], F32) for _ in range(2)]
        cvx = arena.alloc([512], F32)

        def layer_norm(src, gk, bk_, dst, dstb, pre_done, hook=None, mid_hook=None):
            if not pre_done:
                for n in range(8):
                    ln_pre(src, n)
            if mid_hook is not None:
                mid_hook()
            bm, bq = nbk(), nbk()
            for kt in range(8):
                kb.mm(bm[:], onesD[:], V(aT.t[:, kt, :], aT.reg(kt)), start=(kt == 0), stop=(kt == 7))
            for kt in range(8):
                kb.mm(bq[:], onesD[:], V(aT.t[:, 8 + kt, :], aT.reg(8 + kt)), start=(kt == 0), stop=(kt == 7))
            kb.actf(mean_sb[:], bm[:], AF.Copy)
            kb.tt(m2[:], mean_sb[:], mean_sb[:], ALU.mult)
            kb.tt(m2[:], bq[:], m2[:], ALU.subtract)
            kb.actf(rstd[:], m2[:], AF.Sqrt, bias=EPS)
            kb.op(kb.dve, lambda h: h.reciprocal(out=rstd.t[:], in_=rstd.t[:]), r=[rstd[:]], w=[rstd[:]])
            for n in range(8):
                e = kb.pool if n in (1, 4, 6) else kb.dve
                lt_ = ltb[n % 2]
                kb.tt(lt_[:], V(src.t[:, n, :], src.reg(n)), mean_sb[:], ALU.subtract, e=e)
                kb.tt(lt_[:], lt_[:], rstd[:], ALU.mult, e=e)
                kb.actf(V(dst.t[:, n, :], dst.reg(n)), lt_[:], AF.Identity, scale=lncol(gk, n), bias=lncol(bk_, n))
                if dstb is not None:
                    kb.actf(V(dstb.t[:, n, :], dstb.reg(n)), lt_[:], AF.Identity, scale=lncol(gk, n), bias=lncol(bk_, n))
                if hook is not None:
                    hook(n)

        def e1_loads(tc):
            kb.dma(kb.sp, xcb[:], V(XTB.t[seq, tc], XTB.reg((seq, tc))), d_x)
            kb.dma(kb.sp, ycb[:], V(YTB.t[seq, tc], YTB.reg((seq, tc))), d_y)

        def e1_step(tc, n):
            tsl = slice(tc * 512, (tc + 1) * 512)
            a1 = wl(wga, d_ga, Wb_in, 16 + n, n); a2 = wl(wgs, d_gs, Wb_in, 24 + n, n)
            a3 = wl(wsb, d_sb, Wb_sbo, n, n); a4 = wl(wz1, d_z1, Wb_glu, n, n); a5 = wl(wz2, d_z2, Wb_glu, 8 + n, n)
            bga, bgs, bab, bz1, bz2 = nbk(), nbk(), nbk(), nbk(), nbk()
            for kt in range(8):
                kb.mm(bga[:], V(a1.t[:, kt, :], a1.reg()), V(xcb.t[:, kt, :], xcb.reg()), start=(kt == 0), stop=(kt == 7))
            for kt in range(8):
                kb.mm(bgs[:], V(a2.t[:, kt, :], a2.reg()), V(xcb.t[:, kt, :], xcb.reg()), start=(kt == 0), stop=(kt == 7))
            for kt in range(4):
                kb.mm(bab[:], V(a3.t[:, kt, :], a3.reg()), V(attnT.t[:, kt, tsl], attnT.reg(tc)), start=(kt == 0), stop=(kt == 3))
            for kt in range(4):
                kb.mm(bz1[:], V(a4.t[:, kt, :], a4.reg()), V(ycb.t[:, kt, :], ycb.reg()), start=(kt == 0), stop=(kt == 3))
            for kt in range(4):
                kb.mm(bz2[:], V(a5.t[:, kt, :], a5.reg()), V(ycb.t[:, kt, :], ycb.reg()), start=(kt == 0), stop=(kt == 3))
            kb.actf(tmp[0][:], bga[:], AF.Sigmoid)
            kb.actf(tmp[1][:], bgs[:], AF.Sigmoid)
            kb.actf(tmp[2][:], bz2[:], AF.Sigmoid)
            kb.tt(tmp[3][:], tmp[0][:], bab[:], ALU.mult)
            kb.tt(tmp[4][:], tmp[2][:], bz1[:], ALU.mult)
            kb.tt(tmp[4][:], tmp[4][:], tmp[1][:], ALU.mult)
            kb.tt(V(mixT.t[:, n, :], mixT.reg()), tmp[3][:], tmp[4][:], ALU.add)

        for tc in range(4):
            nxt = tc + 1 if tc < 3 else None
            if tc == 0:
                e1_loads(0)
                for n in range(8):
                    e1_step(0, n)
            kb.dma(kb.sp, pcb[:], V(PTB.t[seq, tc], PTB.reg((seq, tc))), d_p)

            def hook1(n, nxt=nxt):
                if nxt is not None and n in (1, 3, 5):
                    e1_step(nxt, 1 + n // 2)

            def mid1(nxt=nxt):
                if nxt is not None:
                    e1_step(nxt, 0)

            def hook2(n, nxt=nxt):
                if nxt is not None and n in (1, 3, 5):
                    e1_step(nxt, 5 + n // 2)

            def mid2(nxt=nxt):
                if nxt is not None:
                    e1_step(nxt, 4)
            for n in range(8):
                a1 = wl(wo_, d_wo, Wb_o, n, n)
                xrn = xr[n % 2]
                kb.dma(kb.sp, xrn[:], V(XTF.t[seq, tc, :, n, :], XTF.reg((seq, tc))), d_xr[n % 2])
                bk = nbk()
                for kt in range(8):
                    kb.mm(bk[:], V(a1.t[:, kt, :], a1.reg()), V(mixT.t[:, kt, :], mixT.reg()), start=(kt == 0), stop=(kt == 7))
                kb.stt(V(r1.t[:, n, :], r1.reg(n)), xrn[:], ALPHA, bk[:], ALU.mult, ALU.add)
                ln_pre(r1, n)
            if nxt is not None:
                e1_loads(nxt)
            layer_norm(r1, 0, 1, h1, h1b, True, hook=hook1, mid_hook=mid1)
            f1st = {}

            def f1_A(j):
                a1 = wl(wuv, d_uv, Wb_up, j, j); a2 = wl(wug, d_ug, Wb_up, 22 + j, j)
                bv, bg = nbk(), nbk()
                for kt in range(8):
                    kb.mm(bv[:], V(a1.t[:, kt, :], a1.reg()), V(h1b.t[:, kt, :], h1b.reg(kt)), start=(kt == 0), stop=(kt == 7))
                for kt in range(8):
                    kb.mm(bg[:], V(a2.t[:, kt, :], a2.reg()), V(h1b.t[:, kt, :], h1b.reg(kt)), start=(kt == 0), stop=(kt == 7))
                for which, (bk, c44) in enumerate([(bv, j), (bg, 22 + j)]):
                    cv = ([tmp[0], tmp[1], cvx][j % 3]) if which == 0 else tmp[2 + j % 2]
                    hnew = V(halo.t[:, c44, tc % 2, :], halo.reg((c44, tc % 2)))
                    kb.actf(cv[:], bk[:], AF.Identity, scale=cwcol(2, c44), bias=cbcol(c44))
                    kb.actf(hnew, V(bk.t[:, 510:512], bk.reg()), AF.Copy)
                    hold = lambda a, b: V(halo.t[:, c44, (tc + 1) % 2, a:b], halo.reg((c44, (tc + 1) % 2)))
                    hc2 = V(hcT.t[:, c44, 0:2], hcT.reg(c44)); hc1 = V(hcT.t[:, c44, 0:1], hcT.reg(c44))
                    kb.actf(hc2, hold(0, 2), AF.Identity, scale=cwcol(0, c44))
                    kb.actf(hc1, hold(1, 2), AF.Identity, scale=cwcol(1, c44), bias=hc1)
                f1st[j] = (bv, bg)

            def f1_B(j):
                bv, bg = f1st.pop(j)
                cvs = []
                for which, (bk, c44) in enumerate([(bv, j), (bg, 22 + j)]):
                    cv = ([tmp[0], tmp[1], cvx][j % 3]) if which == 0 else tmp[2 + j % 2]
                    hl = lambda a, b: V(halo.t[:, c44, (tc + 1) % 2, a:b], halo.reg((c44, (tc + 1) % 2)))
                    kb.stt(V(cv.t[:, 1:512], cv.reg()), V(bk.t[:, 0:511], bk.reg()), cwcol(1, c44), V(cv.t[:, 1:512], cv.reg()), ALU.mult, ALU.add)
                    kb.stt(V(cv.t[:, 2:512], cv.reg()), V(bk.t[:, 0:510], bk.reg()), cwcol(0, c44), V(cv.t[:, 2:512], cv.reg()), ALU.mult, ALU.add)
                    kb.tt(V(cv.t[:, 0:2], cv.reg()), V(cv.t[:, 0:2], cv.reg()), V(hcT.t[:, c44, 0:2], hcT.reg(c44)), ALU.add)
                    cvs.append(cv)
                sg = tmp[4 + j % 2]
                kb.actf(sg[:], cvs[1][:], AF.Silu)
                kb.tt(V(aT.t[:, j, :], aT.reg(j)), sg[:], cvs[0][:], ALU.mult, e=kb.pool)

            for j in range(23):
                if j < 22:
                    f1_A(j)
                if j >= 1:
                    f1_B(j - 1)
            for n in range(8):
                a1 = wl(wd_, d_wd, Wb_dn, n, n); a2 = wl(wpe_, d_pe, Wb_pe, n, n); a3 = wl(wpg_, d_pg, Wb_peg, n, n)
                bf_, bpe, bpg = nbk(), nbk(), nbk()
                for kt in range(22):
                    kb.mm(bf_[:], V(a1.t[:, kt, :], a1.reg()), V(aT.t[:, kt, :], aT.reg(kt)), start=(kt == 0), stop=(kt == 21))
                for kt in range(2):
                    kb.mm(bpe[:], V(a2.t[:, kt, :], a2.reg()), V(pcb.t[:, kt, :], pcb.reg()), start=(kt == 0), stop=(kt == 1))
                for kt in range(8):
                    kb.mm(bpg[:], V(a3.t[:, kt, :], a3.reg()), V(h1b.t[:, kt, :], h1b.reg(kt)), start=(kt == 0), stop=(kt == 7))
                sgt = tmp[n % 2]; tt_ = tmp[2 + n % 2]
                kb.actf(sgt[:], bpg[:], AF.Sigmoid)
                kb.tt(tt_[:], sgt[:], bpe[:], ALU.mult)
                kb.tt(tt_[:], tt_[:], bf_[:], ALU.add)
                kb.stt(V(r1.t[:, n, :], r1.reg(n)), V(h1.t[:, n, :], h1.reg(n)), ALPHA, tt_[:], ALU.mult, ALU.add)
            layer_norm(r1, 2, 3, r1, None, False, hook=hook2, mid_hook=mid2)
            for t4 in range(4):
                o_ = ot[t4 % 2]
                for hh in range(2):
                    bk = nbk()
                    for d4 in range(4):
                        n = hh * 4 + d4
                        kb.transpose(V(bk.t[:, d4 * 128:(d4 + 1) * 128], bk.reg()),
                                     V(r1.t[:, n, t4 * 128:(t4 + 1) * 128], r1.reg(n)), ident, inc=(d4 == 3))
                    if hh == 0:
                        kb.actf(V(o_.t[:, 0:512], o_.reg()), bk[:], AF.Copy)
                    else:
                        kb.copy(V(o_.t[:, 512:1024], o_.reg()), bk[:])
                tok0 = tc * 512 + t4 * 128
                kb.dma(kb.act, V(out.t[seq, tok0:tok0 + 128, :], out.reg((seq, tc, t4))), o_[:], ds_o[t4 % 2])
    barrier(kb, DS)
    return kb


_CACHE = {}


def kernel(**inputs):
    n = 8
    if "kb" not in _CACHE:
        _CACHE["kb"] = build()
    kb = _CACHE["kb"]
    cst = host_consts()
    x = np.ascontiguousarray(np.asarray(inputs["x"], np.float32))
    p = np.ascontiguousarray(np.asarray(inputs["p"], np.float32))[0]
    shared = {"cst": cst}
    for k in ["w_in", "w_sb_out", "ssm_a_re", "ssm_a_im", "ssm_log_step", "ssm_b_re", "ssm_b_im", "ssm_c_re",
              "ssm_c_im", "ssm_d", "w_glu", "w_o", "ln1_g", "ln1_b", "w_up", "conv_w", "conv_b", "w_down",
              "w_pe", "w_pe_gate", "ln2_g", "ln2_b"]:
        shared[k] = np.ascontiguousarray(np.asarray(inputs[k], np.float32)[0])
    in_maps = []
    for c in range(n):
        m = dict(shared)
        m["x"] = x[2 * c:2 * c + 2]
        m["p"] = p[2 * c:2 * c + 2]
        in_maps.append(m)
    res = run_bass_kernel_spmd(kb.nc, in_maps, core_ids=list(range(n)))
    return np.concatenate([r["out"] for r in res.results], axis=0)
```

```python
import numpy as np
from contextlib import ExitStack
import concourse.bass as bass
import concourse.mybir as mybir
from concourse.bass_utils import run_bass_kernel_spmd

F32 = mybir.dt.float32
BF16 = mybir.dt.bfloat16
I32 = mybir.dt.int32
AF = mybir.ActivationFunctionType
ALU = mybir.AluOpType


class Reg:
    __slots__ = ("w", "rs", "excl")

    def __init__(self):
        self.w = None
        self.rs = {}
        self.excl = False


class V:
    __slots__ = ("ap", "reg")

    def __init__(self, ap, reg):
        self.ap = ap
        self.reg = reg


class T:
    def __init__(self, t, nreg=1):
        self.t = t
        self.regs = {}
        self.default = Reg()

    def reg(self, key=None):
        if key is None:
            return self.default
        r = self.regs.get(key)
        if r is None:
            r = self.regs[key] = Reg()
        return r

    def v(self, idx=None, key=None):
        ap = self.t[idx] if idx is not None else self.t[:]
        return V(ap, self.reg(key))

    def __getitem__(self, idx):
        return V(self.t[idx], self.default)


class Eng:
    def __init__(self, kb, name, h):
        self.kb = kb
        self.name = name
        self.h = h
        self.sem = kb.newsem(name)
        self.cnt = 0
        self.seen = {}
        self.pend_r = []
        self.pend_w = []


class DSem:
    def __init__(self, kb, name):
        self.sem = kb.newsem(name)
        self.cnt = 0


class KB:
    SEM_LIMIT = 30000

    def __init__(self):
        self.nc = bass.Bass("TRN2", target_bir_lowering=False)
        self.es = ExitStack()
        self.nsem = 0
        nc = self.nc
        self.pe = Eng(self, "pe", nc.tensor)
        self.act = Eng(self, "act", nc.scalar)
        self.dve = Eng(self, "dve", nc.vector)
        self.pool = Eng(self, "pool", nc.gpsimd)
        self.sp = Eng(self, "sp", nc.sync)
        self.nwait = 0
        self.nops = 0

    def newsem(self, name):
        self.nsem += 1
        return self.es.enter_context(self.nc.semaphore(f"s{self.nsem}_{name}"))

    def sb(self, name, shape, dt):
        return T(self.es.enter_context(self.nc.sbuf_tensor(name, list(shape), dt)))

    def ps(self, name, shape, dt=F32):
        t = T(self.es.enter_context(self.nc.psum_tensor(name, list(shape), dt)))
        t.default.excl = True
        return t

    def dram(self, name, shape, dt, kind="Internal"):
        return T(self.nc.dram_tensor(name, list(shape), dt, kind=kind).ap())

    def _deps(self, e, rviews, wviews, raw_same_only=True):
        need = {}
        for v in rviews:
            w = v.reg.w
            if w is not None:
                if need.get(w[0], 0) < w[1]:
                    need[w[0]] = w[1]
            if v.reg.excl:
                for (s, val) in v.reg.rs.items():
                    if s is not e.sem and need.get(s, 0) < val:
                        need[s] = val
        skip_own = (e.name == "pe")
        for v in wviews:
            w = v.reg.w
            if w is not None and not (skip_own and w[0] is e.sem):
                if need.get(w[0], 0) < w[1]:
                    need[w[0]] = w[1]
            for (s, val) in v.reg.rs.items():
                if not (skip_own and s is e.sem) and need.get(s, 0) < val:
                    need[s] = val
        for s, val in need.items():
            if e.seen.get(s, 0) < val:
                e.h.wait_ge(s, val)
                e.seen[s] = val
                self.nwait += 1

    def _record(self, tok, rviews, wviews):
        s, val = tok
        for v in rviews:
            if v.reg.rs.get(s, 0) < val:
                v.reg.rs[s] = val
        for v in wviews:
            v.reg.w = tok
            v.reg.rs = {}

    def op(self, e, fn, r=(), w=(), inc=True):
        self._deps(e, r, w)
        ins = fn(e.h)
        self.nops += 1
        if inc:
            if e.cnt >= self.SEM_LIMIT:
                e.sem = self.newsem(e.name)
                e.cnt = 0
            e.cnt += 1
            ins.then_inc(e.sem, 1)
            if e.pend_r or e.pend_w:
                r = list(r) + e.pend_r
                w = list(w) + e.pend_w
                e.pend_r = []
                e.pend_w = []
            self._record((e.sem, e.cnt), r, w)
        else:
            e.pend_r.extend(r)
            e.pend_w.extend(w)
        return ins

    def dma(self, e, out, in_, ds, **kw):
        self._deps(e, [in_], [out])
        ins = e.h.dma_start(out=out.ap, in_=in_.ap, **kw)
        ds.cnt += 16
        ins.then_inc(ds.sem, 16)
        self._record((ds.sem, ds.cnt), [in_], [out])
        return ins

    def wait_all(self, e, views):
        self._deps(e, views, [])

    def mm(self, out, lhsT, rhs, start=True, stop=True, inc=None, extra_r=(), **kw):
        if inc is None:
            inc = stop
        return self.op(self.pe, lambda h: h.matmul(out.ap, lhsT.ap, rhs.ap, start=start, stop=stop, **kw),
                       r=[lhsT, rhs, *extra_r], w=[out], inc=inc)

    def transpose(self, out, in_, ident, inc=True):
        return self.op(self.pe, lambda h: h.transpose(out.ap, in_.ap, ident.ap), r=[in_, ident], w=[out], inc=inc)

    def actf(self, out, in_, func, scale=1.0, bias=0.0, e=None, extra_r=()):
        e = e or self.act
        r = [in_, *extra_r]
        sc = scale.ap if isinstance(scale, V) else scale
        bi = bias.ap if isinstance(bias, V) else bias
        if isinstance(scale, V):
            r.append(scale)
        if isinstance(bias, V):
            r.append(bias)
        return self.op(e, lambda h: h.activation(out=out.ap, in_=in_.ap, func=func, scale=sc, bias=bi), r=r, w=[out])

    def tt(self, out, a, b, op, e=None):
        e = e or self.dve
        return self.op(e, lambda h: h.tensor_tensor(out=out.ap, in0=a.ap, in1=b.ap, op=op), r=[a, b], w=[out])

    def ts(self, out, a, s1, s2=None, op0=ALU.mult, op1=None, e=None):
        e = e or self.dve
        r = [a]
        v1 = s1.ap if isinstance(s1, V) else s1
        v2 = s2.ap if isinstance(s2, V) else s2
        if isinstance(s1, V):
            r.append(s1)
        if isinstance(s2, V):
            r.append(s2)
        if op1 is None:
            return self.op(e, lambda h: h.tensor_scalar(out=out.ap, in0=a.ap, scalar1=v1, scalar2=None, op0=op0), r=r, w=[out])
        return self.op(e, lambda h: h.tensor_scalar(out=out.ap, in0=a.ap, scalar1=v1, scalar2=v2, op0=op0, op1=op1), r=r, w=[out])

    def stt(self, out, a, s, b, op0, op1, e=None):
        e = e or self.dve
        r = [a, b]
        sv = s.ap if isinstance(s, V) else s
        if isinstance(s, V):
            r.append(s)
        return self.op(e, lambda h: h.scalar_tensor_tensor(out=out.ap, in0=a.ap, scalar=sv, in1=b.ap, op0=op0, op1=op1), r=r, w=[out])

    def copy(self, out, a, e=None):
        e = e or self.dve
        return self.op(e, lambda h: h.tensor_copy(out=out.ap, in_=a.ap), r=[a], w=[out])

    def memset(self, out, val, e=None):
        e = e or self.dve
        return self.op(e, lambda h: h.memset(out.ap, val), r=[], w=[out])

    def scan(self, out, d0, d1, init, op0=ALU.mult, op1=ALU.add):
        r = [d0, d1]
        iv = init.ap if isinstance(init, V) else init
        if isinstance(init, V):
            r.append(init)
        return self.op(self.dve, lambda h: h.tensor_tensor_scan(out=out.ap, data0=d0.ap, data1=d1.ap, initial=iv, op0=op0, op1=op1), r=r, w=[out])


import math

NSEQ = 2
TT = 2048
DM = 1024
DFF = 2816
ALPHA = 2.0 ** 0.25
EPS = 1e-5
TWO_PI = 6.283185
C_ID, C_TRI, C_ONEG, C_CM, C_OND, C_BM, C_RM, C_K9, C_CI, C_END = 0, 128, 256, 384, 512, 640, 768, 770, 779, 1035


def host_consts():
    c = np.zeros((128, C_END), np.float32)
    i = np.arange(128)
    c[:, C_ID:C_ID + 128] = np.eye(128)
    c[:, C_TRI:C_TRI + 128] = -1.0 * (i[:, None] >= i[None, :])
    c[:, C_ONEG:C_ONEG + 128] = -1.0
    c[:, C_CM:C_CM + 128] = (i[None, :] > i[:, None])
    c[:, C_OND:C_OND + 128] = 1.0 / DM
    c[:, C_BM:C_BM + 128] = (i[:, None] // 16 == i[None, :] // 16)
    c[:, C_RM] = ((i // 16) % 2 == 0)
    c[:, C_RM + 1] = ((i // 16) % 2 == 1)
    c[:, C_K9:C_K9 + 9] = np.arange(9)[None, :]
    c[:, C_CI:C_CI + 256] = np.arange(256)[None, :]
    return c


class Arena:
    def __init__(self, kb, name, nf32):
        self.kb = kb
        self.t = kb.es.enter_context(kb.nc.sbuf_tensor(name, [128, nf32], F32))
        self.n = nf32
        self.off = 0

    def reset(self, off=0):
        self.off = off

    def alloc(self, shape, dt):
        n = 1
        for s in shape:
            n *= s
        nb = n * (4 if dt in (F32, I32) else 2)
        nf = (nb + 3) // 4
        nf = (nf + 7) // 8 * 8
        assert self.off + nf <= self.n, ("arena overflow", self.off, nf, self.n)
        ap = self.t[:, self.off:self.off + nf]
        self.off += nf
        if dt != F32:
            ap = ap.bitcast(dt)
        ap = ap[:, 0:n]
        if len(shape) == 2:
            ap = ap.rearrange("p (a b) -> p a b", a=shape[0])
        elif len(shape) == 3:
            ap = ap.rearrange("p (a b c) -> p a b c", a=shape[0], b=shape[1])
        elif len(shape) == 4:
            ap = ap.rearrange("p (a b c d) -> p a b c d", a=shape[0], b=shape[1], c=shape[2])
        return T(ap)


def barrier(kb, dsems):
    engs = [kb.pe, kb.act, kb.dve, kb.pool, kb.sp]
    for e in engs:
        for o in engs:
            if o.cnt == 0 or o is kb.sp:
                continue
            if e.seen.get(o.sem, 0) < o.cnt:
                e.h.wait_ge(o.sem, o.cnt)
                e.seen[o.sem] = o.cnt
        for ds in dsems:
            if ds.cnt and e.seen.get(ds.sem, 0) < ds.cnt:
                e.h.wait_ge(ds.sem, ds.cnt)
                e.seen[ds.sem] = ds.cnt


def build(stop=None):
    kb = KB()
    nc = kb.nc
    DS = []

    PIDX = [0]

    def dsem(name):
        i = PIDX[0]
        PIDX[0] += 1
        if i < len(DS):
            return DS[i]
        d = DSem(kb, f"g{i}")
        DS.append(d)
        return d

    def ein(name, shape):
        return kb.dram(name, shape, F32, kind="ExternalInput")

    x = ein("x", [NSEQ, TT, DM])
    pin = ein("p", [NSEQ, TT, 256])
    w_in = ein("w_in", [DM, 4096])
    w_sbo = ein("w_sb_out", [512, DM])
    a_re = ein("ssm_a_re", [32, 64])
    a_im = ein("ssm_a_im", [32, 64])
    lstep = ein("ssm_log_step", [32])
    b_re = ein("ssm_b_re", [32, 64, 16])
    b_im = ein("ssm_b_im", [32, 64, 16])
    c_re = ein("ssm_c_re", [32, 16, 64])
    c_im = ein("ssm_c_im", [32, 16, 64])
    d_in = ein("ssm_d", [32, 16])
    w_glu = ein("w_glu", [512, 2048])
    w_o = ein("w_o", [DM, DM])
    ln1_g = ein("ln1_g", [DM])
    ln1_b = ein("ln1_b", [DM])
    w_up = ein("w_up", [DM, 2 * DFF])
    conv_w = ein("conv_w", [3, 2 * DFF])
    conv_b = ein("conv_b", [2 * DFF])
    w_dn = ein("w_down", [DFF, DM])
    w_pe = ein("w_pe", [256, DM])
    w_peg = ein("w_pe_gate", [DM, DM])
    ln2_g = ein("ln2_g", [DM])
    ln2_b = ein("ln2_b", [DM])
    cst_d = ein("cst", [128, C_END])
    out = kb.dram("out", [NSEQ, TT, DM], F32, kind="ExternalOutput")
    ds_dbg = DSem(kb, "dbg")
    DSX = [ds_dbg]

    def dump(name, view, shape, dt):
        d = kb.dram("dbg_" + name, shape, dt, kind="ExternalOutput")
        kb.dma(kb.act, d[:], view, ds_dbg)

    def finish():
        barrier(kb, DS + DSX)
        return kb

    def wscratch(name, src, K, N):
        wb = kb.dram("wb_" + name, [N // 128, 128, K // 128, 128], BF16)
        ds = DSem(kb, "c_" + name)
        sv = src.t.rearrange("(kt p) (nt c) -> nt p kt c", p=128, c=128)
        import os as _os2
        for nt in range(N // 128):
            if _os2.environ.get('KNOCAST'):
                break
            kb.dma(kb.pool, V(wb.t[nt], wb.reg()), V(sv[nt], src.reg()), ds)
        return wb

    ds_c = dsem("cst")
    cst = kb.sb("cst_sb", [128, C_END], F32)
    kb.dma(kb.sp, cst[:], cst_d[:], ds_c)
    Wb_in = wscratch("in", w_in, DM, 4096)

    def cs(a, n=128, rows=slice(None)):
        return V(cst.t[rows, a:a + n], cst.reg())

    ident = cs(C_ID)
    trineg = kb.sb("trineg", [128, 128], BF16)
    onesneg = kb.sb("onesneg", [128, 128], BF16)
    onesD = kb.sb("onesD", [128, 128], BF16)
    kb.copy(trineg[:], cs(C_TRI))
    kb.copy(onesneg[:], cs(C_ONEG))
    kb.copy(onesD[:], cs(C_OND))
    cmask = cs(C_CM)

    if stop == "0":
        dump("tri", trineg[:], [128, 128], BF16)
        return finish()
    PS = [kb.ps(f"bank{i}", [128, 512], F32) for i in range(8)]
    attnT = kb.sb("attnT", [128, 4, TT], BF16)
    CPA = kb.sb("cpa", [128, 88], F32)
    CPB = kb.sb("cpb", [128, 120], F32)
    halo = kb.sb("halo", [128, 44, 2, 2], F32)
    hcT = kb.sb("hcT", [128, 44, 2], F32)
    AR_N = 44160
    arena = Arena(kb, "arena", AR_N)

    XTB = kb.dram("xtb", [NSEQ, 4, 128, 8, 512], BF16)
    XTF = kb.dram("xtf", [NSEQ, 4, 128, 8, 512], F32)
    PTB = kb.dram("ptb", [NSEQ, 4, 128, 2, 512], BF16)
    YTB = kb.dram("ytb", [NSEQ, 4, 128, 4, 512], BF16)

    arena.reset()
    xs = [arena.alloc([DM], F32) for _ in range(8)]
    pst = [arena.alloc([256], F32) for _ in range(8)]
    xtb_st = [arena.alloc([8, 512], BF16) for _ in range(2)]
    xtf_st = [arena.alloc([8, 512], F32) for _ in range(2)]
    ptb_st = [arena.alloc([2, 512], BF16) for _ in range(2)]
    ds_xs = [dsem(f"xs{i}") for i in range(8)]
    ds_ps = [dsem(f"ps{i}") for i in range(8)]
    ds_sx = [dsem("spx0"), dsem("spx1")]
    ds_sf = [dsem("spf0"), dsem("spf1")]
    ds_spp = [dsem("spp0"), dsem("spp1")]
    import os as _os
    cnt = 0
    for seq in range(NSEQ):
        for tc in range(4):
            par = cnt % 2
            cnt += 1
            for t4 in range(4):
                sl = par * 4 + t4
                tok0 = tc * 512 + t4 * 128
                kb.dma(kb.sp, xs[sl][:], V(x.t[seq, tok0:tok0 + 128, :], x.reg()), ds_xs[sl])
                if not _os.environ.get("KNOP"):
                    kb.dma(kb.sp, pst[sl][:], V(pin.t[seq, tok0:tok0 + 128, :], pin.reg()), ds_ps[sl])
            import os as _os
            _kd = int(_os.environ.get("KDBG", "0"))
            if _kd == 1:
                return finish()
            for dt_ in range(8):
                bank = PS[dt_ % 4]
                for t4 in range(4):
                    sl = par * 4 + t4
                    kb.transpose(V(bank.t[:, t4 * 128:(t4 + 1) * 128], bank.reg()),
                                 V(xs[sl].t[:, dt_ * 128:(dt_ + 1) * 128], xs[sl].reg()), ident, inc=(t4 == 3))
                if _os.environ.get("KNOEV") not in ("1", "3"):
                    kb.actf(V(xtb_st[par].t[:, dt_, :], xtb_st[par].reg()), bank[:], AF.Copy)
                if _os.environ.get("KNOEV") not in ("1", "2"):
                    kb.copy(V(xtf_st[par].t[:, dt_, :], xtf_st[par].reg()), bank[:])
            for d2 in range(0 if _os.environ.get("KNOP") else 2):
                bank = PS[4 + d2]
                for t4 in range(4):
                    sl = par * 4 + t4
                    kb.transpose(V(bank.t[:, t4 * 128:(t4 + 1) * 128], bank.reg()),
                                 V(pst[sl].t[:, d2 * 128:(d2 + 1) * 128], pst[sl].reg()), ident, inc=(t4 == 3))
                kb.actf(V(ptb_st[par].t[:, d2, :], ptb_st[par].reg()), bank[:], AF.Copy)
            if _kd == 2:
                return finish()
            _kv = int(_os.environ.get("KVAR", "0"))
            _tcw = 0 if _kv == 4 else tc
            if _kv == 6 and cnt == 1:
                _kv = 1
            _qe = kb.act
            if _kv in (0, 2, 4, 6):
                kb.dma(_qe, V(XTB.t[seq, _tcw], XTB.reg((seq, tc))), xtb_st[par][:], ds_sx[par])
            if _kv in (0, 3, 4, 6):
                kb.dma(_qe, V(XTF.t[seq, _tcw], XTF.reg((seq, tc))), xtf_st[par][:], ds_sf[par])
            if _kv in (0, 5, 4, 6):
                kb.dma(_qe, V(PTB.t[seq, _tcw], PTB.reg((seq, tc))), ptb_st[par][:], ds_spp[par])
            if _kd == 3 or (_kd >= 10 and cnt == _kd - 10):
                return finish()

    if stop == "A0":
        return finish()
    if stop == "A":
        barrier(kb, DS)
        dump("xtb", V(XTB.t[:], XTB.reg()), [NSEQ, 4, 128, 8, 512], BF16)
        dump("xtf", V(XTF.t[:], XTF.reg()), [NSEQ, 4, 128, 8, 512], F32)
        dump("ptb", V(PTB.t[:], PTB.reg()), [NSEQ, 4, 128, 2, 512], BF16)
        return finish()
    barrier(kb, DS)
    PIDX[0] = 0
    arena.reset()
    W_end = arena.alloc([4, 8, 2, 128], BF16)
    W_car = arena.alloc([16, 8, 2, 32], BF16)
    Kblk = arena.alloc([4, 8, 128], BF16)
    f8s = arena.alloc([16], F32)
    r8s = arena.alloc([16], F32)
    p_off = arena.off
    ds_na = dsem("natA"); ds_nb = dsem("natB"); ds_nl = dsem("natL"); ds_ls = dsem("LS")
    ds_br = dsem("Bre"); ds_bi = dsem("Bim"); ds_ncc = dsem("natC"); ds_dc = dsem("Dcol")
    natA = arena.alloc([128], F32)
    natB = arena.alloc([128], F32)
    kb.memset(natA[:], 0.0)
    kb.memset(natB[:], 0.0)
    cw_v = conv_w.t.rearrange("k (n p) -> k n p", p=128)
    kb.dma(kb.sp, V(natA.t[0:44, :], natA.reg()), V(cw_v[0], conv_w.reg()), ds_na)
    kb.dma(kb.sp, V(natA.t[44:88, :], natA.reg()), V(cw_v[1], conv_w.reg()), ds_na)
    kb.dma(kb.sp, V(natB.t[0:44, :], natB.reg()), V(cw_v[2], conv_w.reg()), ds_nb)
    kb.dma(kb.sp, V(natB.t[44:88, :], natB.reg()), V(conv_b.t.rearrange("(n p) -> n p", p=128), conv_b.reg()), ds_nb)
    for k, prm in enumerate([ln1_g, ln1_b, ln2_g, ln2_b]):
        kb.dma(kb.sp, V(natB.t[88 + 8 * k:96 + 8 * k, :], natB.reg()), V(prm.t.rearrange("(n p) -> n p", p=128), prm.reg()), ds_nb)
    bk = PS[0]
    kb.transpose(V(bk.t[:, 0:128], bk.reg()), natA[:], ident)
    kb.copy(CPA[:], V(bk.t[:, 0:88], bk.reg()))
    bk = PS[1]
    kb.transpose(V(bk.t[:, 0:128], bk.reg()), natB[:], ident)
    kb.copy(CPB[:], V(bk.t[:, 0:120], bk.reg()))

    def cwcol(k, c44):
        if k < 2:
            return V(CPA.t[:, k * 44 + c44:k * 44 + c44 + 1], CPA.reg())
        return V(CPB.t[:, c44:c44 + 1], CPB.reg())

    def cbcol(c44):
        return V(CPB.t[:, 44 + c44:45 + c44], CPB.reg())

    def lncol(k, n):
        return V(CPB.t[:, 88 + 8 * k + n:89 + 8 * k + n], CPB.reg())

    natL = arena.alloc([2, 128], F32)
    kb.memset(natL[:], 0.0)
    for ri, prm in enumerate([a_re, a_im]):
        for dup in range(2):
            kb.dma(kb.sp, V(natL.t[0:32, ri, dup * 64:(dup + 1) * 64], natL.reg()), prm[:], ds_nl)
    ARt = arena.alloc([32], F32)
    AIt = arena.alloc([32], F32)
    for ri, dst in enumerate([ARt, AIt]):
        bk = PS[2 + ri]
        kb.transpose(V(bk.t[:, 0:32], bk.reg()), V(natL.t[0:32, ri, :], natL.reg()), V(cst.t[0:32, C_ID:C_ID + 32], cst.reg()))
        kb.copy(dst[:], V(bk.t[:, 0:32], bk.reg()))
    LS = arena.alloc([32], F32)
    kb.dma(kb.sp, LS[:], V(lstep.t.partition_broadcast(128), lstep.reg()), ds_ls)
    Bre = arena.alloc([32, 16], F32)
    Bim = arena.alloc([32, 16], F32)
    for dst, prm, dsb in [(Bre, b_re, ds_br), (Bim, b_im, ds_bi)]:
        for dup in range(2):
            kb.dma(kb.sp, V(dst.t[dup * 64:(dup + 1) * 64], dst.reg()), V(prm.t.rearrange("g p h -> p g h"), prm.reg()), dsb)
    natC = arena.alloc([4, 2, 128], F32)
    for ri, prm in enumerate([c_re, c_im]):
        for dup in range(2):
            kb.dma(kb.sp, V(natC.t[:, :, ri, dup * 64:(dup + 1) * 64], natC.reg()),
                   V(prm.t.rearrange("g h p -> (g h) p").rearrange("(r q) p -> q r p", q=128), prm.reg()), ds_ncc)
    CTre = arena.alloc([32, 16], F32)
    CTim = arena.alloc([32, 16], F32)
    for ri, dst in enumerate([CTre, CTim]):
        for r in range(4):
            bk = PS[4 + (r % 2) + 2 * ri]
            kb.transpose(V(bk.t[:, 0:128], bk.reg()), V(natC.t[:, r, ri, :], natC.reg()), ident)
            kb.copy(V(dst.t[:, r * 8:(r + 1) * 8, :].rearrange("p g h -> p (g h)"), dst.reg()), V(bk.t[:, 0:128], bk.reg()))
    Dcol = arena.alloc([4], F32)
    kb.dma(kb.sp, Dcol[:], V(d_in.t.rearrange("(s gl) h -> (gl h) s", gl=8), d_in.reg()), ds_dc, allow_slow_non_contiguous=True)

    def frac(dst, src, shape):
        ti = arena.alloc(shape, I32)
        tg = arena.alloc(shape, F32)
        kb.copy(ti[:], src)
        kb.tt(tg[:], src, ti[:], ALU.subtract)
        kb.stt(tg[:], tg[:], 0.5, tg[:], ALU.is_gt, ALU.subtract)
        kb.stt(dst, tg[:], 0.5, tg[:], ALU.is_gt, ALU.subtract)

    def al(shape):
        return arena.alloc(shape, F32)

    step = al([32]); lr = al([32]); thn = al([32]); f1 = al([32])
    kb.actf(step[:], LS[:], AF.Exp)
    kb.tt(lr[:], ARt[:], step[:], ALU.mult)
    kb.tt(thn[:], AIt[:], step[:], ALU.mult)
    kb.ts(thn[:], thn[:], 1.0 / (2 * math.pi), None, op0=ALU.mult)
    frac(f1[:], thn[:], [32])
    K9 = V(cst.t[:, C_K9:C_K9 + 9].unsqueeze(2).to_broadcast([128, 9, 32]), cst.reg())

    def b9(t):
        return V(t.t[:, :].unsqueeze(1).to_broadcast([128, 9, 32]), t.reg())
    klr = al([9, 32]); mag = al([9, 32]); kf = al([9, 32]); ang = al([9, 32]); kf2 = al([9, 32]); angc = al([9, 32])
    Sn = al([9, 32]); Cs = al([9, 32]); PR = al([9, 32]); PI_ = al([9, 32])
    kb.tt(klr[:], K9, b9(lr), ALU.mult)
    kb.actf(mag[:], klr[:], AF.Exp)
    kb.tt(kf[:], K9, b9(f1), ALU.mult)
    frac(ang[:], kf[:], [9, 32])
    kb.ts(kf2[:], kf[:], 0.25, None, op0=ALU.add)
    frac(angc[:], kf2[:], [9, 32])
    kb.actf(Sn[:], ang[:], AF.Sin, scale=TWO_PI)
    kb.actf(Cs[:], angc[:], AF.Sin, scale=TWO_PI)
    kb.tt(PR[:], mag[:], Cs[:], ALU.mult)
    kb.tt(PI_[:], mag[:], Sn[:], ALU.mult)
    den = al([32]); t1 = al([32]); t2 = al([32]); nr = al([32]); cre = al([32]); cim = al([32])
    PR1 = V(PR.t[:, 1, :], PR.reg()); PI1 = V(PI_.t[:, 1, :], PI_.reg())
    kb.tt(den[:], ARt[:], ARt[:], ALU.mult)
    kb.tt(t1[:], AIt[:], AIt[:], ALU.mult)
    kb.tt(den[:], den[:], t1[:], ALU.add)
    kb.op(kb.dve, lambda h: h.reciprocal(out=den.t[:], in_=den.t[:]), r=[den[:]], w=[den[:]])
    kb.ts(nr[:], PR1, -1.0, None, op0=ALU.add)
    kb.tt(t1[:], nr[:], ARt[:], ALU.mult)
    kb.tt(t2[:], PI1, AIt[:], ALU.mult)
    kb.tt(t1[:], t1[:], t2[:], ALU.add)
    kb.tt(cre[:], t1[:], den[:], ALU.mult)
    kb.tt(t1[:], PI1, ARt[:], ALU.mult)
    kb.tt(t2[:], nr[:], AIt[:], ALU.mult)
    kb.tt(t1[:], t1[:], t2[:], ALU.subtract)
    kb.tt(cim[:], t1[:], den[:], ALU.mult)
    bbr = al([32, 16]); bbi = al([32, 16]); u1 = al([32, 16]); u2 = al([32, 16])

    def bh(t):
        return V(t.t[:, :].unsqueeze(2).to_broadcast([128, 32, 16]), t.reg())
    kb.tt(u1[:], bh(cre), Bre[:], ALU.mult)
    kb.tt(u2[:], bh(cim), Bim[:], ALU.mult)
    kb.tt(bbr[:], u1[:], u2[:], ALU.subtract)
    kb.tt(u1[:], bh(cre), Bim[:], ALU.mult)
    kb.tt(u2[:], bh(cim), Bre[:], ALU.mult)
    kb.tt(bbi[:], u1[:], u2[:], ALU.add)
    WEr = al([8, 32, 16]); WEi = al([8, 32, 16]); X1 = al([8, 32, 16]); X2 = al([8, 32, 16])

    def pk(t, k0):
        return V(t.t[:, k0:k0 + 8, :].unsqueeze(3).to_broadcast([128, 8, 32, 16]), t.reg())

    def bk8(t):
        return V(t.t[:, :, :].unsqueeze(1).to_broadcast([128, 8, 32, 16]), t.reg())

    def cprod(outr, outi, k0, vr, vi, neg_i=False):
        kb.tt(X1[:], pk(PR, k0), vr, ALU.mult)
        kb.tt(X2[:], pk(PI_, k0), vi, ALU.mult)
        kb.tt(outr[:], X1[:], X2[:], ALU.subtract)
        kb.tt(X1[:], pk(PR, k0), vi, ALU.mult)
        kb.tt(X2[:], pk(PI_, k0), vr, ALU.mult)
        kb.tt(outi[:], X1[:], X2[:], ALU.add)
    cprod(WEr, WEi, 0, bk8(bbr), bk8(bbi))
    MK = al([8, 512]); CK = al([512])
    kb.copy(V(MK.t[0:64], MK.reg()), V(WEr.t[0:64].rearrange("p k g h -> p k (g h)"), WEr.reg()))
    kb.copy(V(MK.t[64:128], MK.reg()), V(WEi.t[64:128].rearrange("p k g h -> p k (g h)"), WEi.reg()))
    kb.copy(V(CK.t[0:64], CK.reg()), V(CTre.t[0:64].rearrange("p g h -> p (g h)"), CTre.reg()))
    kb.ts(V(CK.t[64:128], CK.reg()), V(CTim.t[64:128].rearrange("p g h -> p (g h)"), CTim.reg()), -1.0, None, op0=ALU.mult)
    ktmp = al([128])
    for s in range(4):
        for tau in range(8):
            bk = PS[(s * 8 + tau) % 4]
            kb.mm(V(bk.t[:, 0:128], bk.reg()), V(MK.t[:, tau, s * 128:(s + 1) * 128], MK.reg()),
                  V(CK.t[:, s * 128:(s + 1) * 128], CK.reg()))
            if tau == 0:
                kb.tt(ktmp[:], V(bk.t[:, 0:128], bk.reg()), cs(C_BM), ALU.mult)
                kb.stt(V(Kblk.t[:, s, tau, :], Kblk.reg()), ident, V(Dcol.t[:, s:s + 1], Dcol.reg()), ktmp[:], ALU.mult, ALU.add)
            else:
                kb.tt(V(Kblk.t[:, s, tau, :], Kblk.reg()), V(bk.t[:, 0:128], bk.reg()), cs(C_BM), ALU.mult)
    id64 = V(cst.t[0:64, C_ID:C_ID + 64], cst.reg())
    nb = 0
    for s in range(4):
        for i in range(8):
            k = 7 - i
            for ri, WE in enumerate([WEr, WEi]):
                bk = PS[4 + nb % 4]
                nb += 1
                kb.transpose(V(bk.t[:, 0:64], bk.reg()),
                             V(WE.t[0:64, k, s * 8:(s + 1) * 8, :].rearrange("p g h -> p (g h)"), WE.reg()), id64)
                for gp in range(2):
                    wv_ = V(W_end.t[:, s, i, ri, gp * 64:(gp + 1) * 64], W_end.reg((s, i, ri, gp)))
                    rmc = V(cst.t[:, C_RM + gp:C_RM + gp + 1], cst.reg())
                    if gp == 0:
                        kb.ts(wv_, V(bk.t[:, 0:64], bk.reg()), rmc, None, op0=ALU.mult)
                    else:
                        kb.actf(wv_, V(bk.t[:, 0:64], bk.reg()), AF.Identity, scale=rmc)
    cprod(WEr, WEi, 1, bk8(CTre), bk8(CTim))
    kb.memset(W_car[:], 0.0, e=kb.pool)
    for gp in range(2):
        hs = slice(gp * 64, (gp + 1) * 64)
        for ri, WE in enumerate([WEr, WEi]):
            src = V(WE.t[hs, :, gp::2, :].rearrange("p j q h -> p q j h"), WE.reg())
            dst = V(W_car.t[hs, :, :, ri, gp * 16:(gp + 1) * 16], W_car.reg())
            if ri == 0:
                kb.copy(dst, src)
            else:
                kb.ts(dst, src, -1.0, None, op0=ALU.mult)
        kb.copy(V(f8s.t[hs, :], f8s.reg()), V(ang.t[hs, 8, gp::2], ang.reg()))
        kb.copy(V(r8s.t[hs, :], r8s.reg()), V(mag.t[hs, 8, gp::2], mag.reg()))

    barrier(kb, DS)
    PIDX[0] = 0
    if stop == "P":
        dump("wend", W_end[:], [128, 4, 8, 2, 128], BF16)
        dump("wcar", W_car[:], [128, 16, 8, 2, 32], BF16)
        dump("kblk", Kblk[:], [128, 4, 8, 128], BF16)
        dump("f8s", f8s[:], [128, 16], F32)
        dump("r8s", r8s[:], [128, 16], F32)
        dump("cpa", CPA[:], [128, 88], F32)
        dump("cpb", CPB[:], [128, 120], F32)
        dump("pr", PR[:], [128, 9, 32], F32)
        dump("pi", PI_[:], [128, 9, 32], F32)
        return finish()
    arena.reset(p_off)
    UD = arena.alloc([4, 8, 256], BF16)
    XH = arena.alloc([16, 2, 257], BF16)
    yT = arena.alloc([4, TT], BF16)
    wu = arena.alloc([4, 8, 128], BF16)
    xc = [arena.alloc([8, 512], BF16) for _ in range(2)]
    RCall = arena.alloc([16, 256], F32)
    RSall = arena.alloc([16, 256], F32)
    SA2 = [arena.alloc([256], F32) for _ in range(2)]; SB2 = [arena.alloc([256], F32) for _ in range(2)]
    SA3 = [arena.alloc([256], F32) for _ in range(2)]
    RT = [dict(phi=arena.alloc([256], F32), phf=arena.alloc([256], F32), ti=arena.alloc([256], I32), tg=arena.alloc([256], F32))
          for _ in range(2)]
    Ep = [arena.alloc([2, 256], F32) for _ in range(2)]
    Wp = [arena.alloc([2, 256], F32) for _ in range(2)]
    SA = [arena.alloc([256], F32) for _ in range(2)]; SB = [arena.alloc([256], F32) for _ in range(2)]
    s_frac_off = arena.off

    def interleave(*chains):
        n = max(len(c) for c in chains)
        for k in range(n):
            for c in chains:
                if k < len(c):
                    c[k]()
    ds_wu = dsem("wu"); ds_xc = [dsem("xc0"), dsem("xc1")]; ds_y = [dsem(f"ysp{i}") for i in range(4)]
    kb.memset(XH[:], 0.0)
    CI = cs(C_CI, 256)
    def rot_chain(pair, t):
        phi, phf, ti, tg = t["phi"], t["phf"], t["ti"], t["tg"]

        def fr(dst):
            return [lambda: kb.copy(ti[:], phi[:]),
                    lambda: kb.tt(tg[:], phi[:], ti[:], ALU.subtract),
                    lambda: kb.stt(tg[:], tg[:], 0.5, tg[:], ALU.is_gt, ALU.subtract),
                    lambda: kb.stt(dst[:], tg[:], 0.5, tg[:], ALU.is_gt, ALU.subtract)]
        ch = [lambda: kb.ts(phi[:], CI, V(f8s.t[:, pair:pair + 1], f8s.reg()), None, op0=ALU.mult)]
        ch += fr(phf)
        ch += [lambda: kb.actf(V(RSall.t[:, pair, :], RSall.reg(pair)), phf[:], AF.Sin, scale=TWO_PI),
               lambda: kb.ts(tg[:], phf[:], 0.25, None, op0=ALU.add),
               lambda: kb.stt(tg[:], tg[:], 0.5, tg[:], ALU.is_gt, ALU.subtract),
               lambda: kb.stt(phi[:], tg[:], 0.5, tg[:], ALU.is_gt, ALU.subtract),
               lambda: kb.actf(V(RCall.t[:, pair, :], RCall.reg(pair)), phi[:], AF.Sin, scale=TWO_PI)]
        return ch

    for pair in range(0, 16, 2):
        interleave(rot_chain(pair, RT[0]), rot_chain(pair + 1, RT[1]))
    xcn = 0
    for seq in range(NSEQ):
        kb.dma(kb.sp, wu[:], V(Wb_in.t[12:16].rearrange("n p k c -> p n k c"), Wb_in.reg()), ds_wu)
        for tc in range(4):
            xb = xc[xcn % 2]
            kb.dma(kb.sp, xb[:], V(XTB.t[seq, tc], XTB.reg((seq, tc))), ds_xc[xcn % 2])
            xcn += 1
            for s in range(4):
                bk = PS[s]
                for kt in range(8):
                    kb.mm(bk[:], V(wu.t[:, s, kt, :], wu.reg()), V(xb.t[:, kt, :], xb.reg()), start=(kt == 0), stop=(kt == 7))
                dstv = V(UD.t[:, s, :, tc * 64:(tc + 1) * 64], UD.reg())
                srcv = V(bk.t[:, :].rearrange("p (c i) -> p i c", i=8), bk.reg())
                if s % 2 == 0:
                    kb.actf(dstv, srcv, AF.Copy)
                else:
                    kb.copy(dstv, srcv)
        def pair_chains(pair):
            s, q = pair // 4, pair % 4
            bk = PS[4 + pair % 4]
            for ri in range(2):
                for i in range(8):
                    kb.mm(V(bk.t[:, ri * 256:(ri + 1) * 256], bk.reg()),
                          V(W_end.t[32 * q:32 * q + 32, s, i, ri, :], W_end.reg()),
                          V(UD.t[32 * q:32 * q + 32, s, i, :], UD.reg()),
                          start=(i == 0), stop=(i == 7), tile_position=(32 * q, 0))
            k2 = pair % 2
            rc = V(RCall.t[:, pair, :], RCall.reg(pair)); rs = V(RSall.t[:, pair, :], RSall.reg(pair))
            ep, wp = Ep[k2], Wp[k2]
            sa, sb_, sa2, sb2, sa3 = SA[k2], SB[k2], SA2[k2], SB2[k2], SA3[k2]
            ere = V(bk.t[:, 0:256], bk.reg()); eim = V(bk.t[:, 256:512], bk.reg())
            epr = V(ep.t[:, 0, :], ep.reg()); epi = V(ep.t[:, 1, :], ep.reg())
            dec = V(r8s.t[:, pair:pair + 1].to_broadcast([128, 256]), r8s.reg())
            wr = V(wp.t[:, 0, :], wp.reg()); wi = V(wp.t[:, 1, :], wp.reg())
            PL = kb.pool
            xre = V(XH.t[:, pair, 0, 1:257], XH.reg(pair)); xim = V(XH.t[:, pair, 1, 1:257], XH.reg(pair))
            dchain = [lambda: kb.tt(sa[:], rc, ere, ALU.mult),
                      lambda: kb.tt(sb_[:], rs, eim, ALU.mult),
                      lambda: kb.tt(epr, sa[:], sb_[:], ALU.add),
                      lambda: kb.tt(sa[:], rc, eim, ALU.mult),
                      lambda: kb.tt(sb_[:], rs, ere, ALU.mult),
                      lambda: kb.tt(epi, sa[:], sb_[:], ALU.subtract),
                      lambda: kb.scan(wr, dec, epr, 0.0),
                      lambda: kb.scan(wi, dec, epi, 0.0)]
            pchain = [lambda: kb.tt(sa2[:], wr, rc, ALU.mult, e=PL),
                      lambda: kb.tt(sb2[:], wi, rs, ALU.mult, e=PL),
                      lambda: kb.tt(xre, sa2[:], sb2[:], ALU.subtract, e=PL),
                      lambda: kb.tt(sa3[:], wi, rc, ALU.mult, e=PL)]
            dtail = [lambda: kb.tt(sb_[:], wr, rs, ALU.mult),
                     lambda: kb.tt(xim, sa3[:], sb_[:], ALU.add)]
            return dchain, pchain, dtail

        for pair in range(0, 16, 2):
            da, pa, ta = pair_chains(pair)
            db, pb, tb = pair_chains(pair + 1)
            interleave(da, db)
            interleave(pa, pb)
            interleave(ta, tb)
        nb = 0
        for s in range(4):
            for j in range(8):
                bk = PS[nb % 4]
                half = (nb // 4) % 2
                nb += 1
                yv = V(bk.t[:, half * 256:(half + 1) * 256], bk.reg())
                for i in range(j + 1):
                    kb.mm(yv, V(Kblk.t[:, s, j - i, :], Kblk.reg()), V(UD.t[:, s, i, :], UD.reg()), start=(i == 0), stop=False, inc=False)
                for q in range(4):
                    pair = s * 4 + q
                    for ri in range(2):
                        last = (q == 3 and ri == 1)
                        kb.mm(V(bk.t[32 * q:32 * q + 32, half * 256:(half + 1) * 256], bk.reg()),
                              V(W_car.t[:, pair, j, ri, :], W_car.reg()),
                              V(XH.t[:, pair, ri, 0:256], XH.reg(pair)),
                              start=False, stop=(ri == 1), inc=last, tile_position=(0, 32 * q))
                kb.actf(V(yT.t[:, s, j::8], yT.reg()), yv, AF.Gelu_apprx_tanh)
        for tc in range(4):
            kb.dma(kb.act, V(YTB.t[seq, tc], YTB.reg((seq, tc))), V(yT.t[:, :, tc * 512:(tc + 1) * 512], yT.reg()), ds_y[tc])

    if stop == "S":
        barrier(kb, DS)
        dump("ytb", V(YTB.t[:], YTB.reg()), [NSEQ, 4, 128, 4, 512], BF16)
        dump("ud", UD[:], [128, 4, 8, 256], BF16)
        dump("xh", XH[:], [128, 16, 2, 257], BF16)
        return finish()
    Wb_sbo = wscratch("sbo", w_sbo, 512, DM)
    Wb_glu = wscratch("glu", w_glu, 512, 2048)
    Wb_o = wscratch("o", w_o, DM, DM)
    Wb_up = wscratch("up", w_up, DM, 2 * DFF)
    Wb_dn = wscratch("dn", w_dn, DFF, DM)
    Wb_pe = wscratch("pe", w_pe, 256, DM)
    Wb_peg = wscratch("peg", w_peg, DM, DM)

    for seq in range(NSEQ):
        barrier(kb, DS)
        PIDX[0] = 0
        ds_o = [dsem("o0"), dsem("o1")]
        arena.reset()
        qT = arena.alloc([4, TT], BF16)
        kz = arena.alloc([8, TT], BF16)
        vv = arena.alloc([16, 512], BF16)
        wqk = arena.alloc([8, 8, 128], BF16)
        wv = arena.alloc([4, 8, 128], BF16)
        xc = [arena.alloc([8, 512], BF16) for _ in range(2)]
        NBUF = 3
        e_t = [arena.alloc([512], F32) for _ in range(NBUF)]
        sp_t = [arena.alloc([512], BF16) for _ in range(NBUF)]
        e3_t = [arena.alloc([512], F32) for _ in range(NBUF)]
        w_t = [arena.alloc([512], BF16) for _ in range(NBUF)]
        A_t = arena.alloc([512], BF16)
        ds_w1 = dsem("wqk"); ds_w2 = dsem("wv"); ds_x = [dsem("qx0"), dsem("qx1")]
        kb.dma(kb.sp, wqk[:], V(Wb_in.t[0:8].rearrange("n p k c -> p n k c"), Wb_in.reg()), ds_w1)
        kb.dma(kb.sp, wv[:], V(Wb_in.t[8:12].rearrange("n p k c -> p n k c"), Wb_in.reg()), ds_w2)
        for tc_ in range(4):
            kb.memset(V(kz.t[:, :, tc_ * 512:(tc_ + 1) * 512], kz.reg(tc_)), 0.0)
        nb = 0
        for tc in range(4):
            xb = xc[tc % 2]
            kb.dma(kb.sp, xb[:], V(XTB.t[seq, tc], XTB.reg((seq, tc))), ds_x[tc % 2])
            for nt in range(8):
                bk = PS[nb % 4]
                nb += 1
                for kt in range(8):
                    kb.mm(bk[:], V(wqk.t[:, nt, kt, :], wqk.reg()), V(xb.t[:, kt, :], xb.reg()), start=(kt == 0), stop=(kt == 7))
                if nt < 4:
                    dv = V(qT.t[:, nt, tc * 512:(tc + 1) * 512], qT.reg(tc))
                    if nt % 2 == 0:
                        kb.actf(dv, bk[:], AF.Copy, scale=0.125)
                    else:
                        kb.ts(dv, bk[:], 0.125, None, op0=ALU.mult)
                else:
                    for hh in range(2):
                        hd = 2 * (nt - 4) + hh
                        hs_ = slice(hh * 64, hh * 64 + 64)
                        dv = V(kz.t[hs_, hd, tc * 512:(tc + 1) * 512], kz.reg(tc))
                        sv_ = V(bk.t[hs_, :], bk.reg())
                        if nt % 2 == 0:
                            kb.actf(dv, sv_, AF.Copy)
                        else:
                            kb.copy(dv, sv_)
            for t4 in range(4):
                bk = PS[4 + t4 % 4]
                for j in range(4):
                    for kt in range(8):
                        kb.mm(V(bk.t[:, j * 128:(j + 1) * 128], bk.reg()), V(xb.t[:, kt, t4 * 128:(t4 + 1) * 128], xb.reg()),
                              V(wv.t[:, j, kt, :], wv.reg()), start=(kt == 0), stop=(kt == 7), inc=(kt == 7 and j == 3))
                dv = V(vv.t[:, tc * 4 + t4, :], vv.reg(tc))
                if t4 % 2 == 0:
                    kb.actf(dv, bk[:], AF.Copy)
                else:
                    kb.copy(dv, bk[:])
        tiles = []
        for h in range(8):
            for qc in range(4):
                for kbk in range(4 * qc + 3, -1, -1):
                    m = kbk - 4 * qc
                    c0 = 128 * m if m > 0 else 0
                    tiles.append((h, qc, kbk, c0, m >= 0, kbk == 4 * qc + 3, kbk == 0))
        ZB = [PS[0], PS[1]]
        RB = [PS[2], PS[3]]
        OB = [PS[4], PS[5]]
        nt_ = len(tiles)
        ostate = {}

        def hp(h):
            return slice((h % 2) * 64, (h % 2) * 64 + 64)

        def stage1(i):
            h, qc, kbk, c0, diag, first, lastk = tiles[i]
            zb = ZB[i % 2]
            n = 512 - c0
            qv = V(qT.t[:, h // 2, qc * 512 + c0:(qc + 1) * 512], qT.reg(qc))
            kv = V(kz.t[:, h, kbk * 128:(kbk + 1) * 128], kz.reg(kbk // 4))
            kb.mm(V(zb.t[:, c0:512], zb.reg()), kv, qv)
            ev = V(e_t[i % NBUF].t[:, c0:512], e_t[i % NBUF].reg())
            kb.actf(ev, V(zb.t[:, c0:512], zb.reg()), AF.Exp)
            if diag:
                e1 = V(e_t[i % NBUF].t[:, c0:c0 + 128], e_t[i % NBUF].reg())
                kb.tt(e1, e1, cmask, ALU.mult)
            spv = V(sp_t[i % NBUF].t[:, c0:512], sp_t[i % NBUF].reg())
            kb.actf(spv, ev, AF.Ln, bias=1.0)

        def stage2(i):
            h, qc, kbk, c0, diag, first, lastk = tiles[i]
            rb = RB[i % 2]
            spv = V(sp_t[i % NBUF].t[:, c0:512], sp_t[i % NBUF].reg())
            rv = V(rb.t[:, c0:512], rb.reg())
            kb.mm(rv, trineg[:], spv, start=True, stop=first)
            if not first:
                kb.mm(rv, onesneg[:], V(A_t.t[:, c0:512], A_t.reg()), start=False, stop=True)
            if not lastk:
                if first:
                    kb.memset(A_t[:], 0.0)
                kb.tt(V(A_t.t[:, c0:512], A_t.reg()), V(A_t.t[:, c0:512], A_t.reg()), spv, ALU.add)
            e3v = V(e3_t[i % NBUF].t[:, c0:512], e3_t[i % NBUF].reg())
            kb.actf(e3v, rv, AF.Exp)
            ev = V(e_t[i % NBUF].t[:, c0:512], e_t[i % NBUF].reg())
            kb.tt(V(w_t[i % NBUF].t[:, c0:512], w_t[i % NBUF].reg()), ev, e3v, ALU.mult)
            if first and c0 > 0:
                kb.memset(V(w_t[i % NBUF].t[:, 0:c0], w_t[i % NBUF].reg()), 0.0)

        def stage3(i):
            h, qc, kbk, c0, diag, first, lastk = tiles[i]
            ob = OB[(h * 4 + qc) % 2]
            ov = V(ob.t[:, c0:512], ob.reg())
            vt = V(vv.t[:, kbk, (h // 2) * 128:(h // 2 + 1) * 128], vv.reg(kbk // 4))
            wt = w_t[i % NBUF]
            if first:
                kb.mm(ob[:], vt, wt[:], start=True, stop=lastk, inc=True)
            else:
                kb.mm(ov, vt, V(wt.t[:, c0:512], wt.reg()), start=False, stop=lastk, inc=True)
            if lastk:
                dv = V(attnT.t[hp(h), h // 2, qc * 512:(qc + 1) * 512], attnT.reg(qc))
                kb.copy(dv, V(ob.t[hp(h), :], ob.reg()))

        for st in range(nt_ + 2):
            if st < nt_:
                stage1(st)
            if 0 <= st - 1 < nt_:
                stage2(st - 1)
            if 0 <= st - 2 < nt_:
                stage3(st - 2)

        barrier(kb, DS)
        PIDX[0] = 2
        if stop == "Q":
            dump("attn", attnT[:], [128, 4, TT], BF16)
            dump("qT", qT[:], [128, 4, TT], BF16)
            dump("vv", vv[:], [128, 16, 512], BF16)
            return finish()
        arena.reset()
        xcb = arena.alloc([8, 512], BF16)
        ycb = arena.alloc([4, 512], BF16)
        pcb = arena.alloc([2, 512], BF16)
        xr = [arena.alloc([512], F32) for _ in range(2)]
        wga = [arena.alloc([8, 128], BF16) for _ in range(2)]
        wgs = [arena.alloc([8, 128], BF16) for _ in range(2)]
        wsb = [arena.alloc([4, 128], BF16) for _ in range(2)]
        wz1 = [arena.alloc([4, 128], BF16) for _ in range(2)]
        wz2 = [arena.alloc([4, 128], BF16) for _ in range(2)]
        wo_ = [arena.alloc([8, 128], BF16) for _ in range(4)]
        wuv = [arena.alloc([8, 128], BF16) for _ in range(3)]
        wug = [arena.alloc([8, 128], BF16) for _ in range(3)]
        wd_ = [arena.alloc([22, 128], BF16) for _ in range(2)]
        wpe_ = [arena.alloc([2, 128], BF16) for _ in range(2)]
        wpg_ = [arena.alloc([8, 128], BF16) for _ in range(2)]
        tmp = [arena.alloc([512], F32) for _ in range(6)]
        mixT = arena.alloc([8, 512], BF16)
        r1 = arena.alloc([8, 512], F32)
        h1 = arena.alloc([8, 512], F32)
        h1b = arena.alloc([8, 512], BF16)
        aT = arena.alloc([22, 512], BF16)
        mean_sb = arena.alloc([512], F32); m2 = arena.alloc([512], F32); rstd = arena.alloc([512], F32); lt = arena.alloc([512], F32)
        ot = [arena.alloc([DM], F32) for _ in range(2)]
        dsn = lambda nm, k=2: [dsem(nm + str(i_)) for i_ in range(k)]
        d_x = dsem("ex"); d_y = dsem("ey"); d_p = dsem("ep"); d_xr = dsn("xr")
        d_ga = dsn("ga"); d_gs = dsn("gs"); d_sb = dsn("sb"); d_z1 = dsn("z1"); d_z2 = dsn("z2"); d_wo = dsn("wo", 4)
        d_uv = dsn("uv", 3); d_ug = dsn("ug", 3); d_wd = dsn("wd"); d_pe = dsn("pe"); d_pg = dsn("pg")
        kb.memset(halo[:], 0.0)

        wlc = {}

        def wl(dst, dsl, wb, nt, i):
            k = wlc.get(id(dst), 0)
            wlc[id(dst)] = k + 1
            sl = k % len(dst)
            kb.dma(kb.sp, dst[sl][:], V(wb.t[nt], wb.reg()), dsl[sl])
            return dst[sl]

        bkc = [0]

        def nbk():
            b = PS[bkc[0] % 8]
            bkc[0] += 1
            return b

        def ln_pre(src, n):
            kb.actf(V(aT.t[:, n, :], aT.reg(n)), V(src.t[:, n, :], src.reg(n)), AF.Copy)
            kb.actf(V(aT.t[:, 8 + n, :], aT.reg(8 + n)), V(src.t[:, n, :], src.reg(n)), AF.Square)

        ltb = [arena.alloc([512], F32) for _ in range(2)]
        cvx = arena.alloc([512], F32)

        ln_deferred = []

        def ln_flush(k=1):
            for _ in range(k):
                if ln_deferred:
                    dst_, n_, gk_, bk2_ = ln_deferred.pop(0)
                    dv_ = V(dst_.t[:, n_, :], dst_.reg(n_))
                    kb.ts(dv_, dv_, lncol(gk_, n_), lncol(bk2_, n_), op0=ALU.mult, op1=ALU.add)

        def layer_norm(src, gk, bk_, dst, dstb, pre_done, hook=None, mid_hook=None):
            if not pre_done:
                for n in range(8):
                    ln_pre(src, n)
            if mid_hook is not None:
                mid_hook()
            bm, bq = nbk(), nbk()
            for kt in range(8):
                kb.mm(bm[:], onesD[:], V(aT.t[:, kt, :], aT.reg(kt)), start=(kt == 0), stop=(kt == 7))
            for kt in range(8):
                kb.mm(bq[:], onesD[:], V(aT.t[:, 8 + kt, :], aT.reg(8 + kt)), start=(kt == 0), stop=(kt == 7))
            kb.actf(mean_sb[:], bm[:], AF.Copy)
            kb.tt(m2[:], mean_sb[:], mean_sb[:], ALU.mult)
            kb.tt(m2[:], bq[:], m2[:], ALU.subtract)
            kb.actf(rstd[:], m2[:], AF.Sqrt, bias=EPS)
            kb.op(kb.dve, lambda h: h.reciprocal(out=rstd.t[:], in_=rstd.t[:]), r=[rstd[:]], w=[rstd[:]])
            for n in range(8):
                e = kb.pool if n in (1, 4, 6) else kb.dve
                if dstb is not None:
                    dn = V(dst.t[:, n, :], dst.reg(n))
                    kb.tt(dn, V(src.t[:, n, :], src.reg(n)), mean_sb[:], ALU.subtract, e=e)
                    kb.tt(dn, dn, rstd[:], ALU.mult, e=e)
                    kb.actf(V(dstb.t[:, n, :], dstb.reg(n)), dn, AF.Identity, scale=lncol(gk, n), bias=lncol(bk_, n))
                    ln_deferred.append((dst, n, gk, bk_))
                else:
                    lt_ = ltb[n % 2]
                    kb.tt(lt_[:], V(src.t[:, n, :], src.reg(n)), mean_sb[:], ALU.subtract, e=e)
                    kb.tt(lt_[:], lt_[:], rstd[:], ALU.mult, e=e)
                    kb.actf(V(dst.t[:, n, :], dst.reg(n)), lt_[:], AF.Identity, scale=lncol(gk, n), bias=lncol(bk_, n))
                if hook is not None:
                    hook(n)

        def e1_loads(tc):
            kb.dma(kb.sp, xcb[:], V(XTB.t[seq, tc], XTB.reg((seq, tc))), d_x)
            kb.dma(kb.sp, ycb[:], V(YTB.t[seq, tc], YTB.reg((seq, tc))), d_y)

        def e1_step(tc, n):
            tsl = slice(tc * 512, (tc + 1) * 512)
            a1 = wl(wga, d_ga, Wb_in, 16 + n, n); a2 = wl(wgs, d_gs, Wb_in, 24 + n, n)
            a3 = wl(wsb, d_sb, Wb_sbo, n, n); a4 = wl(wz1, d_z1, Wb_glu, n, n); a5 = wl(wz2, d_z2, Wb_glu, 8 + n, n)
            bga, bgs, bab, bz1, bz2 = nbk(), nbk(), nbk(), nbk(), nbk()
            for kt in range(8):
                kb.mm(bga[:], V(a1.t[:, kt, :], a1.reg()), V(xcb.t[:, kt, :], xcb.reg()), start=(kt == 0), stop=(kt == 7))
            for kt in range(8):
                kb.mm(bgs[:], V(a2.t[:, kt, :], a2.reg()), V(xcb.t[:, kt, :], xcb.reg()), start=(kt == 0), stop=(kt == 7))
            for kt in range(4):
                kb.mm(bab[:], V(a3.t[:, kt, :], a3.reg()), V(attnT.t[:, kt, tsl], attnT.reg(tc)), start=(kt == 0), stop=(kt == 3))
            for kt in range(4):
                kb.mm(bz1[:], V(a4.t[:, kt, :], a4.reg()), V(ycb.t[:, kt, :], ycb.reg()), start=(kt == 0), stop=(kt == 3))
            for kt in range(4):
                kb.mm(bz2[:], V(a5.t[:, kt, :], a5.reg()), V(ycb.t[:, kt, :], ycb.reg()), start=(kt == 0), stop=(kt == 3))
            kb.actf(tmp[0][:], bga[:], AF.Sigmoid)
            kb.actf(tmp[1][:], bgs[:], AF.Sigmoid)
            kb.actf(tmp[2][:], bz2[:], AF.Sigmoid)
            kb.tt(tmp[3][:], tmp[0][:], bab[:], ALU.mult)
            kb.tt(tmp[4][:], tmp[2][:], bz1[:], ALU.mult)
            kb.tt(tmp[4][:], tmp[4][:], tmp[1][:], ALU.mult)
            kb.tt(V(mixT.t[:, n, :], mixT.reg()), tmp[3][:], tmp[4][:], ALU.add)

        for tc in range(4):
            nxt = tc + 1 if tc < 3 else None
            if tc == 0:
                e1_loads(0)
                for n in range(8):
                    e1_step(0, n)
            kb.dma(kb.sp, pcb[:], V(PTB.t[seq, tc], PTB.reg((seq, tc))), d_p)

            def hook1(n, nxt=nxt):
                if nxt is not None and n % 2 == 1:
                    e1_step(nxt, n // 2)

            def hook2(n, nxt=nxt):
                if nxt is not None and n in (1, 3, 5):
                    e1_step(nxt, 5 + n // 2)

            def mid2(nxt=nxt):
                if nxt is not None:
                    e1_step(nxt, 4)
            for n in range(8):
                a1 = wl(wo_, d_wo, Wb_o, n, n)
                xrn = xr[n % 2]
                kb.dma(kb.sp, xrn[:], V(XTF.t[seq, tc, :, n, :], XTF.reg((seq, tc))), d_xr[n % 2])
                bk = nbk()
                for kt in range(8):
                    kb.mm(bk[:], V(a1.t[:, kt, :], a1.reg()), V(mixT.t[:, kt, :], mixT.reg()), start=(kt == 0), stop=(kt == 7))
                kb.stt(V(r1.t[:, n, :], r1.reg(n)), xrn[:], ALPHA, bk[:], ALU.mult, ALU.add)
                ln_pre(r1, n)
            if nxt is not None:
                e1_loads(nxt)
            layer_norm(r1, 0, 1, h1, h1b, True, hook=hook1)
            f1st = {}

            def f1_A(j):
                a1 = wl(wuv, d_uv, Wb_up, j, j); a2 = wl(wug, d_ug, Wb_up, 22 + j, j)
                bv, bg = nbk(), nbk()
                for kt in range(8):
                    kb.mm(bv[:], V(a1.t[:, kt, :], a1.reg()), V(h1b.t[:, kt, :], h1b.reg(kt)), start=(kt == 0), stop=(kt == 7))
                for kt in range(8):
                    kb.mm(bg[:], V(a2.t[:, kt, :], a2.reg()), V(h1b.t[:, kt, :], h1b.reg(kt)), start=(kt == 0), stop=(kt == 7))
                for which, (bk, c44) in enumerate([(bv, j), (bg, 22 + j)]):
                    cv = ([tmp[0], tmp[1], cvx][j % 3]) if which == 0 else tmp[2 + j % 2]
                    hnew = V(halo.t[:, c44, tc % 2, :], halo.reg((c44, tc % 2)))
                    kb.actf(cv[:], bk[:], AF.Identity, scale=cwcol(2, c44), bias=cbcol(c44))
                    kb.actf(hnew, V(bk.t[:, 510:512], bk.reg()), AF.Copy)
                    hold = lambda a, b: V(halo.t[:, c44, (tc + 1) % 2, a:b], halo.reg((c44, (tc + 1) % 2)))
                    hc2 = V(hcT.t[:, c44, 0:2], hcT.reg(c44)); hc1 = V(hcT.t[:, c44, 0:1], hcT.reg(c44))
                    kb.actf(hc2, hold(0, 2), AF.Identity, scale=cwcol(0, c44))
                    kb.actf(hc1, hold(1, 2), AF.Identity, scale=cwcol(1, c44), bias=hc1)
                f1st[j] = (bv, bg)

            def f1_B(j):
                bv, bg = f1st.pop(j)
                ln_flush(1)
                cvs = []
                for which, (bk, c44) in enumerate([(bv, j), (bg, 22 + j)]):
                    cv = ([tmp[0], tmp[1], cvx][j % 3]) if which == 0 else tmp[2 + j % 2]
                    hl = lambda a, b: V(halo.t[:, c44, (tc + 1) % 2, a:b], halo.reg((c44, (tc + 1) % 2)))
                    kb.stt(V(cv.t[:, 1:512], cv.reg()), V(bk.t[:, 0:511], bk.reg()), cwcol(1, c44), V(cv.t[:, 1:512], cv.reg()), ALU.mult, ALU.add)
                    kb.stt(V(cv.t[:, 2:512], cv.reg()), V(bk.t[:, 0:510], bk.reg()), cwcol(0, c44), V(cv.t[:, 2:512], cv.reg()), ALU.mult, ALU.add)
                    kb.tt(V(cv.t[:, 0:2], cv.reg()), V(cv.t[:, 0:2], cv.reg()), V(hcT.t[:, c44, 0:2], hcT.reg(c44)), ALU.add)
                    cvs.append(cv)
                sg = tmp[4 + j % 2]
                kb.actf(sg[:], cvs[1][:], AF.Silu)
                kb.tt(V(aT.t[:, j, :], aT.reg(j)), sg[:], cvs[0][:], ALU.mult, e=kb.pool)

            for j in range(23):
                if j < 22:
                    f1_A(j)
                if j >= 1:
                    f1_B(j - 1)
            for n in range(8):
                a1 = wl(wd_, d_wd, Wb_dn, n, n); a2 = wl(wpe_, d_pe, Wb_pe, n, n); a3 = wl(wpg_, d_pg, Wb_peg, n, n)
                bf_, bpe, bpg = nbk(), nbk(), nbk()
                for kt in range(22):
                    kb.mm(bf_[:], V(a1.t[:, kt, :], a1.reg()), V(aT.t[:, kt, :], aT.reg(kt)), start=(kt == 0), stop=(kt == 21))
                for kt in range(2):
                    kb.mm(bpe[:], V(a2.t[:, kt, :], a2.reg()), V(pcb.t[:, kt, :], pcb.reg()), start=(kt == 0), stop=(kt == 1))
                for kt in range(8):
                    kb.mm(bpg[:], V(a3.t[:, kt, :], a3.reg()), V(h1b.t[:, kt, :], h1b.reg(kt)), start=(kt == 0), stop=(kt == 7))
                sgt = tmp[n % 2]; tt_ = tmp[2 + n % 2]
                kb.actf(sgt[:], bpg[:], AF.Sigmoid)
                kb.tt(tt_[:], sgt[:], bpe[:], ALU.mult)
                kb.tt(tt_[:], tt_[:], bf_[:], ALU.add)
                kb.stt(V(r1.t[:, n, :], r1.reg(n)), V(h1.t[:, n, :], h1.reg(n)), ALPHA, tt_[:], ALU.mult, ALU.add)
            layer_norm(r1, 2, 3, r1, None, False, hook=hook2, mid_hook=mid2)
            for t4 in range(4):
                o_ = ot[t4 % 2]
                for hh in range(2):
                    bk = nbk()
                    for d4 in range(4):
                        n = hh * 4 + d4
                        kb.transpose(V(bk.t[:, d4 * 128:(d4 + 1) * 128], bk.reg()),
                                     V(r1.t[:, n, t4 * 128:(t4 + 1) * 128], r1.reg(n)), ident, inc=(d4 == 3))
                    if hh == 0:
                        kb.actf(V(o_.t[:, 0:512], o_.reg()), bk[:], AF.Copy)
                    else:
                        kb.copy(V(o_.t[:, 512:1024], o_.reg()), bk[:])
                tok0 = tc * 512 + t4 * 128
                kb.dma(kb.act, V(out.t[seq, tok0:tok0 + 128, :], out.reg((seq, tc, t4))), o_[:], ds_o[t4 % 2])
    barrier(kb, DS)
    return kb


_CACHE = {}


def kernel(**inputs):
    n = 8
    if "kb" not in _CACHE:
        _CACHE["kb"] = build()
    kb = _CACHE["kb"]
    cst = host_consts()
    x = np.ascontiguousarray(np.asarray(inputs["x"], np.float32))
    p = np.ascontiguousarray(np.asarray(inputs["p"], np.float32))[0]
    shared = {"cst": cst}
    for k in ["w_in", "w_sb_out", "ssm_a_re", "ssm_a_im", "ssm_log_step", "ssm_b_re", "ssm_b_im", "ssm_c_re",
              "ssm_c_im", "ssm_d", "w_glu", "w_o", "ln1_g", "ln1_b", "w_up", "conv_w", "conv_b", "w_down",
              "w_pe", "w_pe_gate", "ln2_g", "ln2_b"]:
        shared[k] = np.ascontiguousarray(np.asarray(inputs[k], np.float32)[0])
    in_maps = []
    for c in range(n):
        m = dict(shared)
        m["x"] = x[2 * c:2 * c + 2]
        m["p"] = p[2 * c:2 * c + 2]
        in_maps.append(m)
    res = run_bass_kernel_spmd(kb.nc, in_maps, core_ids=list(range(n)))
    return np.concatenate([r["out"] for r in res.results], axis=0)
```

```python
import numpy as np
from contextlib import ExitStack
import concourse.bass as bass
import concourse.mybir as mybir
from concourse.bass_utils import run_bass_kernel_spmd

F32 = mybir.dt.float32
BF16 = mybir.dt.bfloat16
I32 = mybir.dt.int32
AF = mybir.ActivationFunctionType
ALU = mybir.AluOpType


class Reg:
    __slots__ = ("w", "rs", "excl")

    def __init__(self):
        self.w = None
        self.rs = {}
        self.excl = False


class V:
    __slots__ = ("ap", "reg")

    def __init__(self, ap, reg):
        self.ap = ap
        self.reg = reg


class T:
    def __init__(self, t, nreg=1):
        self.t = t
        self.regs = {}
        self.default = Reg()

    def reg(self, key=None):
        if key is None:
            return self.default
        r = self.regs.get(key)
        if r is None:
            r = self.regs[key] = Reg()
        return r

    def v(self, idx=None, key=None):
        ap = self.t[idx] if idx is not None else self.t[:]
        return V(ap, self.reg(key))

    def __getitem__(self, idx):
        return V(self.t[idx], self.default)


class Eng:
    def __init__(self, kb, name, h):
        self.kb = kb
        self.name = name
        self.h = h
        self.sem = kb.newsem(name)
        self.cnt = 0
        self.seen = {}
        self.pend_r = []
        self.pend_w = []


class DSem:
    def __init__(self, kb, name):
        self.sem = kb.newsem(name)
        self.cnt = 0


class KB:
    SEM_LIMIT = 30000

    def __init__(self):
        self.nc = bass.Bass("TRN2", target_bir_lowering=False)
        self.es = ExitStack()
        self.nsem = 0
        nc = self.nc
        self.pe = Eng(self, "pe", nc.tensor)
        self.act = Eng(self, "act", nc.scalar)
        self.dve = Eng(self, "dve", nc.vector)
        self.pool = Eng(self, "pool", nc.gpsimd)
        self.sp = Eng(self, "sp", nc.sync)
        self.nwait = 0
        self.nops = 0

    def newsem(self, name):
        self.nsem += 1
        return self.es.enter_context(self.nc.semaphore(f"s{self.nsem}_{name}"))

    def sb(self, name, shape, dt):
        return T(self.es.enter_context(self.nc.sbuf_tensor(name, list(shape), dt)))

    def ps(self, name, shape, dt=F32):
        t = T(self.es.enter_context(self.nc.psum_tensor(name, list(shape), dt)))
        t.default.excl = True
        return t

    def dram(self, name, shape, dt, kind="Internal"):
        return T(self.nc.dram_tensor(name, list(shape), dt, kind=kind).ap())

    def _deps(self, e, rviews, wviews, raw_same_only=True):
        need = {}
        for v in rviews:
            w = v.reg.w
            if w is not None:
                if need.get(w[0], 0) < w[1]:
                    need[w[0]] = w[1]
            if v.reg.excl:
                for (s, val) in v.reg.rs.items():
                    if s is not e.sem and need.get(s, 0) < val:
                        need[s] = val
        skip_own = (e.name == "pe")
        for v in wviews:
            w = v.reg.w
            if w is not None and not (skip_own and w[0] is e.sem):
                if need.get(w[0], 0) < w[1]:
                    need[w[0]] = w[1]
            for (s, val) in v.reg.rs.items():
                if not (skip_own and s is e.sem) and need.get(s, 0) < val:
                    need[s] = val
        for s, val in need.items():
            if e.seen.get(s, 0) < val:
                e.h.wait_ge(s, val)
                e.seen[s] = val
                self.nwait += 1

    def _record(self, tok, rviews, wviews):
        s, val = tok
        for v in rviews:
            if v.reg.rs.get(s, 0) < val:
                v.reg.rs[s] = val
        for v in wviews:
            v.reg.w = tok
            v.reg.rs = {}

    def op(self, e, fn, r=(), w=(), inc=True):
        self._deps(e, r, w)
        ins = fn(e.h)
        self.nops += 1
        if inc:
            if e.cnt >= self.SEM_LIMIT:
                e.sem = self.newsem(e.name)
                e.cnt = 0
            e.cnt += 1
            ins.then_inc(e.sem, 1)
            if e.pend_r or e.pend_w:
                r = list(r) + e.pend_r
                w = list(w) + e.pend_w
                e.pend_r = []
                e.pend_w = []
            self._record((e.sem, e.cnt), r, w)
        else:
            e.pend_r.extend(r)
            e.pend_w.extend(w)
        return ins

    def dma(self, e, out, in_, ds, **kw):
        self._deps(e, [in_], [out])
        ins = e.h.dma_start(out=out.ap, in_=in_.ap, **kw)
        ds.cnt += 16
        ins.then_inc(ds.sem, 16)
        self._record((ds.sem, ds.cnt), [in_], [out])
        return ins

    def wait_all(self, e, views):
        self._deps(e, views, [])

    def mm(self, out, lhsT, rhs, start=True, stop=True, inc=None, extra_r=(), **kw):
        if inc is None:
            inc = stop
        return self.op(self.pe, lambda h: h.matmul(out.ap, lhsT.ap, rhs.ap, start=start, stop=stop, **kw),
                       r=[lhsT, rhs, *extra_r], w=[out], inc=inc)

    def transpose(self, out, in_, ident, inc=True):
        return self.op(self.pe, lambda h: h.transpose(out.ap, in_.ap, ident.ap), r=[in_, ident], w=[out], inc=inc)

    def actf(self, out, in_, func, scale=1.0, bias=0.0, e=None, extra_r=()):
        e = e or self.act
        r = [in_, *extra_r]
        sc = scale.ap if isinstance(scale, V) else scale
        bi = bias.ap if isinstance(bias, V) else bias
        if isinstance(scale, V):
            r.append(scale)
        if isinstance(bias, V):
            r.append(bias)
        return self.op(e, lambda h: h.activation(out=out.ap, in_=in_.ap, func=func, scale=sc, bias=bi), r=r, w=[out])

    def tt(self, out, a, b, op, e=None):
        e = e or self.dve
        return self.op(e, lambda h: h.tensor_tensor(out=out.ap, in0=a.ap, in1=b.ap, op=op), r=[a, b], w=[out])

    def ts(self, out, a, s1, s2=None, op0=ALU.mult, op1=None, e=None):
        e = e or self.dve
        r = [a]
        v1 = s1.ap if isinstance(s1, V) else s1
        v2 = s2.ap if isinstance(s2, V) else s2
        if isinstance(s1, V):
            r.append(s1)
        if isinstance(s2, V):
            r.append(s2)
        if op1 is None:
            return self.op(e, lambda h: h.tensor_scalar(out=out.ap, in0=a.ap, scalar1=v1, scalar2=None, op0=op0), r=r, w=[out])
        return self.op(e, lambda h: h.tensor_scalar(out=out.ap, in0=a.ap, scalar1=v1, scalar2=v2, op0=op0, op1=op1), r=r, w=[out])

    def stt(self, out, a, s, b, op0, op1, e=None):
        e = e or self.dve
        r = [a, b]
        sv = s.ap if isinstance(s, V) else s
        if isinstance(s, V):
            r.append(s)
        return self.op(e, lambda h: h.scalar_tensor_tensor(out=out.ap, in0=a.ap, scalar=sv, in1=b.ap, op0=op0, op1=op1), r=r, w=[out])

    def copy(self, out, a, e=None):
        e = e or self.dve
        return self.op(e, lambda h: h.tensor_copy(out=out.ap, in_=a.ap), r=[a], w=[out])

    def memset(self, out, val, e=None):
        e = e or self.dve
        return self.op(e, lambda h: h.memset(out.ap, val), r=[], w=[out])

    def scan(self, out, d0, d1, init, op0=ALU.mult, op1=ALU.add):
        r = [d0, d1]
        iv = init.ap if isinstance(init, V) else init
        if isinstance(init, V):
            r.append(init)
        return self.op(self.dve, lambda h: h.tensor_tensor_scan(out=out.ap, data0=d0.ap, data1=d1.ap, initial=iv, op0=op0, op1=op1), r=r, w=[out])


import math

NSEQ = 2
TT = 2048
DM = 1024
DFF = 2816
ALPHA = 2.0 ** 0.25
EPS = 1e-5
TWO_PI = 6.283185
C_ID, C_TRI, C_ONEG, C_CM, C_OND, C_BM, C_RM, C_K9, C_CI, C_END = 0, 128, 256, 384, 512, 640, 768, 770, 779, 1035


def host_consts():
    c = np.zeros((128, C_END), np.float32)
    i = np.arange(128)
    c[:, C_ID:C_ID + 128] = np.eye(128)
    c[:, C_TRI:C_TRI + 128] = -1.0 * (i[:, None] >= i[None, :])
    c[:, C_ONEG:C_ONEG + 128] = -1.0
    c[:, C_CM:C_CM + 128] = (i[None, :] > i[:, None])
    c[:, C_OND:C_OND + 128] = 1.0 / DM
    c[:, C_BM:C_BM + 128] = (i[:, None] // 16 == i[None, :] // 16)
    c[:, C_RM] = ((i // 16) % 2 == 0)
    c[:, C_RM + 1] = ((i // 16) % 2 == 1)
    c[:, C_K9:C_K9 + 9] = np.arange(9)[None, :]
    c[:, C_CI:C_CI + 256] = np.arange(256)[None, :]
    return c


class Arena:
    def __init__(self, kb, name, nf32):
        self.kb = kb
        self.t = kb.es.enter_context(kb.nc.sbuf_tensor(name, [128, nf32], F32))
        self.n = nf32
        self.off = 0

    def reset(self, off=0):
        self.off = off

    def alloc(self, shape, dt):
        n = 1
        for s in shape:
            n *= s
        nb = n * (4 if dt in (F32, I32) else 2)
        nf = (nb + 3) // 4
        nf = (nf + 7) // 8 * 8
        assert self.off + nf <= self.n, ("arena overflow", self.off, nf, self.n)
        ap = self.t[:, self.off:self.off + nf]
        self.off += nf
        if dt != F32:
            ap = ap.bitcast(dt)
        ap = ap[:, 0:n]
        if len(shape) == 2:
            ap = ap.rearrange("p (a b) -> p a b", a=shape[0])
        elif len(shape) == 3:
            ap = ap.rearrange("p (a b c) -> p a b c", a=shape[0], b=shape[1])
        elif len(shape) == 4:
            ap = ap.rearrange("p (a b c d) -> p a b c d", a=shape[0], b=shape[1], c=shape[2])
        return T(ap)


def barrier(kb, dsems):
    engs = [kb.pe, kb.act, kb.dve, kb.pool, kb.sp]
    for e in engs:
        for o in engs:
            if o.cnt == 0 or o is kb.sp:
                continue
            if e.seen.get(o.sem, 0) < o.cnt:
                e.h.wait_ge(o.sem, o.cnt)
                e.seen[o.sem] = o.cnt
        for ds in dsems:
            if ds.cnt and e.seen.get(ds.sem, 0) < ds.cnt:
                e.h.wait_ge(ds.sem, ds.cnt)
                e.seen[ds.sem] = ds.cnt


def build(stop=None):
    kb = KB()
    nc = kb.nc
    DS = []

    PIDX = [0]

    def dsem(name):
        i = PIDX[0]
        PIDX[0] += 1
        if i < len(DS):
            return DS[i]
        d = DSem(kb, f"g{i}")
        DS.append(d)
        return d

    def ein(name, shape):
        return kb.dram(name, shape, F32, kind="ExternalInput")

    x = ein("x", [NSEQ, TT, DM])
    pin = ein("p", [NSEQ, TT, 256])
    w_in = ein("w_in", [DM, 4096])
    w_sbo = ein("w_sb_out", [512, DM])
    a_re = ein("ssm_a_re", [32, 64])
    a_im = ein("ssm_a_im", [32, 64])
    lstep = ein("ssm_log_step", [32])
    b_re = ein("ssm_b_re", [32, 64, 16])
    b_im = ein("ssm_b_im", [32, 64, 16])
    c_re = ein("ssm_c_re", [32, 16, 64])
    c_im = ein("ssm_c_im", [32, 16, 64])
    d_in = ein("ssm_d", [32, 16])
    w_glu = ein("w_glu", [512, 2048])
    w_o = ein("w_o", [DM, DM])
    ln1_g = ein("ln1_g", [DM])
    ln1_b = ein("ln1_b", [DM])
    w_up = ein("w_up", [DM, 2 * DFF])
    conv_w = ein("conv_w", [3, 2 * DFF])
    conv_b = ein("conv_b", [2 * DFF])
    w_dn = ein("w_down", [DFF, DM])
    w_pe = ein("w_pe", [256, DM])
    w_peg = ein("w_pe_gate", [DM, DM])
    ln2_g = ein("ln2_g", [DM])
    ln2_b = ein("ln2_b", [DM])
    cst_d = ein("cst", [128, C_END])
    out = kb.dram("out", [NSEQ, TT, DM], F32, kind="ExternalOutput")
    ds_dbg = DSem(kb, "dbg")
    DSX = [ds_dbg]

    def dump(name, view, shape, dt):
        d = kb.dram("dbg_" + name, shape, dt, kind="ExternalOutput")
        kb.dma(kb.act, d[:], view, ds_dbg)

    def finish():
        barrier(kb, DS + DSX)
        return kb

    def wscratch(name, src, K, N):
        wb = kb.dram("wb_" + name, [N // 128, 128, K // 128, 128], BF16)
        ds = DSem(kb, "c_" + name)
        sv = src.t.rearrange("(kt p) (nt c) -> nt p kt c", p=128, c=128)
        import os as _os2
        for nt in range(N // 128):
            if _os2.environ.get('KNOCAST'):
                break
            kb.dma(kb.pool, V(wb.t[nt], wb.reg()), V(sv[nt], src.reg()), ds)
        return wb

    ds_c = dsem("cst")
    cst = kb.sb("cst_sb", [128, C_END], F32)
    kb.dma(kb.sp, cst[:], cst_d[:], ds_c)
    Wb_in = wscratch("in", w_in, DM, 4096)

    def cs(a, n=128, rows=slice(None)):
        return V(cst.t[rows, a:a + n], cst.reg())

    ident = cs(C_ID)
    trineg = kb.sb("trineg", [128, 128], BF16)
    onesneg = kb.sb("onesneg", [128, 128], BF16)
    onesD = kb.sb("onesD", [128, 128], BF16)
    kb.copy(trineg[:], cs(C_TRI))
    kb.copy(onesneg[:], cs(C_ONEG))
    kb.copy(onesD[:], cs(C_OND))
    cmask = cs(C_CM)

    if stop == "0":
        dump("tri", trineg[:], [128, 128], BF16)
        return finish()
    PS = [kb.ps(f"bank{i}", [128, 512], F32) for i in range(8)]
    attnT = kb.sb("attnT", [128, 4, TT], BF16)
    CPA = kb.sb("cpa", [128, 88], F32)
    CPB = kb.sb("cpb", [128, 120], F32)
    halo = kb.sb("halo", [128, 44, 2, 2], F32)
    hcT = kb.sb("hcT", [128, 44, 2], F32)
    AR_N = 44160
    arena = Arena(kb, "arena", AR_N)

    XTB = kb.dram("xtb", [NSEQ, 4, 128, 8, 512], BF16)
    XTF = kb.dram("xtf", [NSEQ, 4, 128, 8, 512], F32)
    PTB = kb.dram("ptb", [NSEQ, 4, 128, 2, 512], BF16)
    YTB = kb.dram("ytb", [NSEQ, 4, 128, 4, 512], BF16)

    arena.reset()
    xs = [arena.alloc([DM], F32) for _ in range(8)]
    pst = [arena.alloc([256], F32) for _ in range(8)]
    xtb_st = [arena.alloc([8, 512], BF16) for _ in range(2)]
    xtf_st = [arena.alloc([8, 512], F32) for _ in range(2)]
    ptb_st = [arena.alloc([2, 512], BF16) for _ in range(2)]
    ds_xs = [dsem(f"xs{i}") for i in range(8)]
    ds_ps = [dsem(f"ps{i}") for i in range(8)]
    ds_sx = [dsem("spx0"), dsem("spx1")]
    ds_sf = [dsem("spf0"), dsem("spf1")]
    ds_spp = [dsem("spp0"), dsem("spp1")]
    import os as _os
    cnt = 0
    for seq in range(NSEQ):
        for tc in range(4):
            par = cnt % 2
            cnt += 1
            for t4 in range(4):
                sl = par * 4 + t4
                tok0 = tc * 512 + t4 * 128
                kb.dma(kb.sp, xs[sl][:], V(x.t[seq, tok0:tok0 + 128, :], x.reg()), ds_xs[sl])
                if not _os.environ.get("KNOP"):
                    kb.dma(kb.sp, pst[sl][:], V(pin.t[seq, tok0:tok0 + 128, :], pin.reg()), ds_ps[sl])
            import os as _os
            _kd = int(_os.environ.get("KDBG", "0"))
            if _kd == 1:
                return finish()
            for dt_ in range(8):
                bank = PS[dt_ % 4]
                for t4 in range(4):
                    sl = par * 4 + t4
                    kb.transpose(V(bank.t[:, t4 * 128:(t4 + 1) * 128], bank.reg()),
                                 V(xs[sl].t[:, dt_ * 128:(dt_ + 1) * 128], xs[sl].reg()), ident, inc=(t4 == 3))
                if _os.environ.get("KNOEV") not in ("1", "3"):
                    kb.actf(V(xtb_st[par].t[:, dt_, :], xtb_st[par].reg()), bank[:], AF.Copy)
                if _os.environ.get("KNOEV") not in ("1", "2"):
                    kb.copy(V(xtf_st[par].t[:, dt_, :], xtf_st[par].reg()), bank[:])
            for d2 in range(0 if _os.environ.get("KNOP") else 2):
                bank = PS[4 + d2]
                for t4 in range(4):
                    sl = par * 4 + t4
                    kb.transpose(V(bank.t[:, t4 * 128:(t4 + 1) * 128], bank.reg()),
                                 V(pst[sl].t[:, d2 * 128:(d2 + 1) * 128], pst[sl].reg()), ident, inc=(t4 == 3))
                kb.actf(V(ptb_st[par].t[:, d2, :], ptb_st[par].reg()), bank[:], AF.Copy)
            if _kd == 2:
                return finish()
            _kv = int(_os.environ.get("KVAR", "0"))
            _tcw = 0 if _kv == 4 else tc
            if _kv == 6 and cnt == 1:
                _kv = 1
            _qe = kb.act
            if _kv in (0, 2, 4, 6):
                kb.dma(_qe, V(XTB.t[seq, _tcw], XTB.reg((seq, tc))), xtb_st[par][:], ds_sx[par])
            if _kv in (0, 3, 4, 6):
                kb.dma(_qe, V(XTF.t[seq, _tcw], XTF.reg((seq, tc))), xtf_st[par][:], ds_sf[par])
            if _kv in (0, 5, 4, 6):
                kb.dma(_qe, V(PTB.t[seq, _tcw], PTB.reg((seq, tc))), ptb_st[par][:], ds_spp[par])
            if _kd == 3 or (_kd >= 10 and cnt == _kd - 10):
                return finish()

    if stop == "A0":
        return finish()
    if stop == "A":
        barrier(kb, DS)
        dump("xtb", V(XTB.t[:], XTB.reg()), [NSEQ, 4, 128, 8, 512], BF16)
        dump("xtf", V(XTF.t[:], XTF.reg()), [NSEQ, 4, 128, 8, 512], F32)
        dump("ptb", V(PTB.t[:], PTB.reg()), [NSEQ, 4, 128, 2, 512], BF16)
        return finish()
    barrier(kb, DS)
    PIDX[0] = 0
    arena.reset()
    W_end = arena.alloc([4, 8, 2, 128], BF16)
    W_car = arena.alloc([16, 8, 2, 32], BF16)
    Kblk = arena.alloc([4, 8, 128], BF16)
    f8s = arena.alloc([16], F32)
    r8s = arena.alloc([16], F32)
    p_off = arena.off
    ds_na = dsem("natA"); ds_nb = dsem("natB"); ds_nl = dsem("natL"); ds_ls = dsem("LS")
    ds_br = dsem("Bre"); ds_bi = dsem("Bim"); ds_ncc = dsem("natC"); ds_dc = dsem("Dcol")
    natA = arena.alloc([128], F32)
    natB = arena.alloc([128], F32)
    kb.memset(natA[:], 0.0)
    kb.memset(natB[:], 0.0)
    cw_v = conv_w.t.rearrange("k (n p) -> k n p", p=128)
    kb.dma(kb.sp, V(natA.t[0:44, :], natA.reg()), V(cw_v[0], conv_w.reg()), ds_na)
    kb.dma(kb.sp, V(natA.t[44:88, :], natA.reg()), V(cw_v[1], conv_w.reg()), ds_na)
    kb.dma(kb.sp, V(natB.t[0:44, :], natB.reg()), V(cw_v[2], conv_w.reg()), ds_nb)
    kb.dma(kb.sp, V(natB.t[44:88, :], natB.reg()), V(conv_b.t.rearrange("(n p) -> n p", p=128), conv_b.reg()), ds_nb)
    for k, prm in enumerate([ln1_g, ln1_b, ln2_g, ln2_b]):
        kb.dma(kb.sp, V(natB.t[88 + 8 * k:96 + 8 * k, :], natB.reg()), V(prm.t.rearrange("(n p) -> n p", p=128), prm.reg()), ds_nb)
    bk = PS[0]
    kb.transpose(V(bk.t[:, 0:128], bk.reg()), natA[:], ident)
    kb.copy(CPA[:], V(bk.t[:, 0:88], bk.reg()))
    bk = PS[1]
    kb.transpose(V(bk.t[:, 0:128], bk.reg()), natB[:], ident)
    kb.copy(CPB[:], V(bk.t[:, 0:120], bk.reg()))

    def cwcol(k, c44):
        if k < 2:
            return V(CPA.t[:, k * 44 + c44:k * 44 + c44 + 1], CPA.reg())
        return V(CPB.t[:, c44:c44 + 1], CPB.reg())

    def cbcol(c44):
        return V(CPB.t[:, 44 + c44:45 + c44], CPB.reg())

    def lncol(k, n):
        return V(CPB.t[:, 88 + 8 * k + n:89 + 8 * k + n], CPB.reg())

    natL = arena.alloc([2, 128], F32)
    kb.memset(natL[:], 0.0)
    for ri, prm in enumerate([a_re, a_im]):
        for dup in range(2):
            kb.dma(kb.sp, V(natL.t[0:32, ri, dup * 64:(dup + 1) * 64], natL.reg()), prm[:], ds_nl)
    ARt = arena.alloc([32], F32)
    AIt = arena.alloc([32], F32)
    for ri, dst in enumerate([ARt, AIt]):
        bk = PS[2 + ri]
        kb.transpose(V(bk.t[:, 0:32], bk.reg()), V(natL.t[0:32, ri, :], natL.reg()), V(cst.t[0:32, C_ID:C_ID + 32], cst.reg()))
        kb.copy(dst[:], V(bk.t[:, 0:32], bk.reg()))
    LS = arena.alloc([32], F32)
    kb.dma(kb.sp, LS[:], V(lstep.t.partition_broadcast(128), lstep.reg()), ds_ls)
    Bre = arena.alloc([32, 16], F32)
    Bim = arena.alloc([32, 16], F32)
    for dst, prm, dsb in [(Bre, b_re, ds_br), (Bim, b_im, ds_bi)]:
        for dup in range(2):
            kb.dma(kb.sp, V(dst.t[dup * 64:(dup + 1) * 64], dst.reg()), V(prm.t.rearrange("g p h -> p g h"), prm.reg()), dsb)
    natC = arena.alloc([4, 2, 128], F32)
    for ri, prm in enumerate([c_re, c_im]):
        for dup in range(2):
            kb.dma(kb.sp, V(natC.t[:, :, ri, dup * 64:(dup + 1) * 64], natC.reg()),
                   V(prm.t.rearrange("g h p -> (g h) p").rearrange("(r q) p -> q r p", q=128), prm.reg()), ds_ncc)
    CTre = arena.alloc([32, 16], F32)
    CTim = arena.alloc([32, 16], F32)
    for ri, dst in enumerate([CTre, CTim]):
        for r in range(4):
            bk = PS[4 + (r % 2) + 2 * ri]
            kb.transpose(V(bk.t[:, 0:128], bk.reg()), V(natC.t[:, r, ri, :], natC.reg()), ident)
            kb.copy(V(dst.t[:, r * 8:(r + 1) * 8, :].rearrange("p g h -> p (g h)"), dst.reg()), V(bk.t[:, 0:128], bk.reg()))
    Dcol = arena.alloc([4], F32)
    kb.dma(kb.sp, Dcol[:], V(d_in.t.rearrange("(s gl) h -> (gl h) s", gl=8), d_in.reg()), ds_dc, allow_slow_non_contiguous=True)

    def frac(dst, src, shape):
        ti = arena.alloc(shape, I32)
        tg = arena.alloc(shape, F32)
        kb.copy(ti[:], src)
        kb.tt(tg[:], src, ti[:], ALU.subtract)
        kb.stt(tg[:], tg[:], 0.5, tg[:], ALU.is_gt, ALU.subtract)
        kb.stt(dst, tg[:], 0.5, tg[:], ALU.is_gt, ALU.subtract)

    def al(shape):
        return arena.alloc(shape, F32)

    step = al([32]); lr = al([32]); thn = al([32]); f1 = al([32])
    kb.actf(step[:], LS[:], AF.Exp)
    kb.tt(lr[:], ARt[:], step[:], ALU.mult)
    kb.tt(thn[:], AIt[:], step[:], ALU.mult)
    kb.ts(thn[:], thn[:], 1.0 / (2 * math.pi), None, op0=ALU.mult)
    frac(f1[:], thn[:], [32])
    K9 = V(cst.t[:, C_K9:C_K9 + 9].unsqueeze(2).to_broadcast([128, 9, 32]), cst.reg())

    def b9(t):
        return V(t.t[:, :].unsqueeze(1).to_broadcast([128, 9, 32]), t.reg())
    klr = al([9, 32]); mag = al([9, 32]); kf = al([9, 32]); ang = al([9, 32]); kf2 = al([9, 32]); angc = al([9, 32])
    Sn = al([9, 32]); Cs = al([9, 32]); PR = al([9, 32]); PI_ = al([9, 32])
    kb.tt(klr[:], K9, b9(lr), ALU.mult)
    kb.actf(mag[:], klr[:], AF.Exp)
    kb.tt(kf[:], K9, b9(f1), ALU.mult)
    frac(ang[:], kf[:], [9, 32])
    kb.ts(kf2[:], kf[:], 0.25, None, op0=ALU.add)
    frac(angc[:], kf2[:], [9, 32])
    kb.actf(Sn[:], ang[:], AF.Sin, scale=TWO_PI)
    kb.actf(Cs[:], angc[:], AF.Sin, scale=TWO_PI)
    kb.tt(PR[:], mag[:], Cs[:], ALU.mult)
    kb.tt(PI_[:], mag[:], Sn[:], ALU.mult)
    den = al([32]); t1 = al([32]); t2 = al([32]); nr = al([32]); cre = al([32]); cim = al([32])
    PR1 = V(PR.t[:, 1, :], PR.reg()); PI1 = V(PI_.t[:, 1, :], PI_.reg())
    kb.tt(den[:], ARt[:], ARt[:], ALU.mult)
    kb.tt(t1[:], AIt[:], AIt[:], ALU.mult)
    kb.tt(den[:], den[:], t1[:], ALU.add)
    kb.op(kb.dve, lambda h: h.reciprocal(out=den.t[:], in_=den.t[:]), r=[den[:]], w=[den[:]])
    kb.ts(nr[:], PR1, -1.0, None, op0=ALU.add)
    kb.tt(t1[:], nr[:], ARt[:], ALU.mult)
    kb.tt(t2[:], PI1, AIt[:], ALU.mult)
    kb.tt(t1[:], t1[:], t2[:], ALU.add)
    kb.tt(cre[:], t1[:], den[:], ALU.mult)
    kb.tt(t1[:], PI1, ARt[:], ALU.mult)
    kb.tt(t2[:], nr[:], AIt[:], ALU.mult)
    kb.tt(t1[:], t1[:], t2[:], ALU.subtract)
    kb.tt(cim[:], t1[:], den[:], ALU.mult)
    bbr = al([32, 16]); bbi = al([32, 16]); u1 = al([32, 16]); u2 = al([32, 16])

    def bh(t):
        return V(t.t[:, :].unsqueeze(2).to_broadcast([128, 32, 16]), t.reg())
    kb.tt(u1[:], bh(cre), Bre[:], ALU.mult)
    kb.tt(u2[:], bh(cim), Bim[:], ALU.mult)
    kb.tt(bbr[:], u1[:], u2[:], ALU.subtract)
    kb.tt(u1[:], bh(cre), Bim[:], ALU.mult)
    kb.tt(u2[:], bh(cim), Bre[:], ALU.mult)
    kb.tt(bbi[:], u1[:], u2[:], ALU.add)
    WEr = al([8, 32, 16]); WEi = al([8, 32, 16]); X1 = al([8, 32, 16]); X2 = al([8, 32, 16])

    def pk(t, k0):
        return V(t.t[:, k0:k0 + 8, :].unsqueeze(3).to_broadcast([128, 8, 32, 16]), t.reg())

    def bk8(t):
        return V(t.t[:, :, :].unsqueeze(1).to_broadcast([128, 8, 32, 16]), t.reg())

    def cprod(outr, outi, k0, vr, vi, neg_i=False):
        kb.tt(X1[:], pk(PR, k0), vr, ALU.mult)
        kb.tt(X2[:], pk(PI_, k0), vi, ALU.mult)
        kb.tt(outr[:], X1[:], X2[:], ALU.subtract)
        kb.tt(X1[:], pk(PR, k0), vi, ALU.mult)
        kb.tt(X2[:], pk(PI_, k0), vr, ALU.mult)
        kb.tt(outi[:], X1[:], X2[:], ALU.add)
    cprod(WEr, WEi, 0, bk8(bbr), bk8(bbi))
    MK = al([8, 512]); CK = al([512])
    kb.copy(V(MK.t[0:64], MK.reg()), V(WEr.t[0:64].rearrange("p k g h -> p k (g h)"), WEr.reg()))
    kb.copy(V(MK.t[64:128], MK.reg()), V(WEi.t[64:128].rearrange("p k g h -> p k (g h)"), WEi.reg()))
    kb.copy(V(CK.t[0:64], CK.reg()), V(CTre.t[0:64].rearrange("p g h -> p (g h)"), CTre.reg()))
    kb.ts(V(CK.t[64:128], CK.reg()), V(CTim.t[64:128].rearrange("p g h -> p (g h)"), CTim.reg()), -1.0, None, op0=ALU.mult)
    ktmp = al([128])
    for s in range(4):
        for tau in range(8):
            bk = PS[(s * 8 + tau) % 4]
            kb.mm(V(bk.t[:, 0:128], bk.reg()), V(MK.t[:, tau, s * 128:(s + 1) * 128], MK.reg()),
                  V(CK.t[:, s * 128:(s + 1) * 128], CK.reg()))
            if tau == 0:
                kb.tt(ktmp[:], V(bk.t[:, 0:128], bk.reg()), cs(C_BM), ALU.mult)
                kb.stt(V(Kblk.t[:, s, tau, :], Kblk.reg()), ident, V(Dcol.t[:, s:s + 1], Dcol.reg()), ktmp[:], ALU.mult, ALU.add)
            else:
                kb.tt(V(Kblk.t[:, s, tau, :], Kblk.reg()), V(bk.t[:, 0:128], bk.reg()), cs(C_BM), ALU.mult)
    id64 = V(cst.t[0:64, C_ID:C_ID + 64], cst.reg())
    nb = 0
    for s in range(4):
        for i in range(8):
            k = 7 - i
            for ri, WE in enumerate([WEr, WEi]):
                bk = PS[4 + nb % 4]
                nb += 1
                kb.transpose(V(bk.t[:, 0:64], bk.reg()),
                             V(WE.t[0:64, k, s * 8:(s + 1) * 8, :].rearrange("p g h -> p (g h)"), WE.reg()), id64)
                for gp in range(2):
                    wv_ = V(W_end.t[:, s, i, ri, gp * 64:(gp + 1) * 64], W_end.reg((s, i, ri, gp)))
                    rmc = V(cst.t[:, C_RM + gp:C_RM + gp + 1], cst.reg())
                    if gp == 0:
                        kb.ts(wv_, V(bk.t[:, 0:64], bk.reg()), rmc, None, op0=ALU.mult)
                    else:
                        kb.actf(wv_, V(bk.t[:, 0:64], bk.reg()), AF.Identity, scale=rmc)
    cprod(WEr, WEi, 1, bk8(CTre), bk8(CTim))
    kb.memset(W_car[:], 0.0, e=kb.pool)
    for gp in range(2):
        hs = slice(gp * 64, (gp + 1) * 64)
        for ri, WE in enumerate([WEr, WEi]):
            src = V(WE.t[hs, :, gp::2, :].rearrange("p j q h -> p q j h"), WE.reg())
            dst = V(W_car.t[hs, :, :, ri, gp * 16:(gp + 1) * 16], W_car.reg())
            if ri == 0:
                kb.copy(dst, src)
            else:
                kb.ts(dst, src, -1.0, None, op0=ALU.mult)
        kb.copy(V(f8s.t[hs, :], f8s.reg()), V(ang.t[hs, 8, gp::2], ang.reg()))
        kb.copy(V(r8s.t[hs, :], r8s.reg()), V(mag.t[hs, 8, gp::2], mag.reg()))

    barrier(kb, DS)
    PIDX[0] = 0
    if stop == "P":
        dump("wend", W_end[:], [128, 4, 8, 2, 128], BF16)
        dump("wcar", W_car[:], [128, 16, 8, 2, 32], BF16)
        dump("kblk", Kblk[:], [128, 4, 8, 128], BF16)
        dump("f8s", f8s[:], [128, 16], F32)
        dump("r8s", r8s[:], [128, 16], F32)
        dump("cpa", CPA[:], [128, 88], F32)
        dump("cpb", CPB[:], [128, 120], F32)
        dump("pr", PR[:], [128, 9, 32], F32)
        dump("pi", PI_[:], [128, 9, 32], F32)
        return finish()
    arena.reset(p_off)
    UD = arena.alloc([4, 8, 256], BF16)
    XH = arena.alloc([16, 2, 257], BF16)
    yT = arena.alloc([4, TT], BF16)
    wu = arena.alloc([4, 8, 128], BF16)
    xc = [arena.alloc([8, 512], BF16) for _ in range(2)]
    RCall = arena.alloc([16, 256], F32)
    RSall = arena.alloc([16, 256], F32)
    SA2 = [arena.alloc([256], F32) for _ in range(2)]; SB2 = [arena.alloc([256], F32) for _ in range(2)]
    SA3 = [arena.alloc([256], F32) for _ in range(2)]
    RT = [dict(phi=arena.alloc([256], F32), phf=arena.alloc([256], F32), ti=arena.alloc([256], I32), tg=arena.alloc([256], F32))
          for _ in range(2)]
    Ep = [arena.alloc([2, 256], F32) for _ in range(2)]
    Wp = [arena.alloc([2, 256], F32) for _ in range(2)]
    SA = [arena.alloc([256], F32) for _ in range(2)]; SB = [arena.alloc([256], F32) for _ in range(2)]
    s_frac_off = arena.off

    def interleave(*chains):
        n = max(len(c) for c in chains)
        for k in range(n):
            for c in chains:
                if k < len(c):
                    c[k]()
    ds_wu = dsem("wu"); ds_xc = [dsem("xc0"), dsem("xc1")]; ds_y = [dsem(f"ysp{i}") for i in range(4)]
    kb.memset(XH[:], 0.0)
    CI = cs(C_CI, 256)
    def rot_chain(pair, t):
        phi, phf, ti, tg = t["phi"], t["phf"], t["ti"], t["tg"]

        def fr(dst):
            return [lambda: kb.copy(ti[:], phi[:]),
                    lambda: kb.tt(tg[:], phi[:], ti[:], ALU.subtract),
                    lambda: kb.stt(tg[:], tg[:], 0.5, tg[:], ALU.is_gt, ALU.subtract),
                    lambda: kb.stt(dst[:], tg[:], 0.5, tg[:], ALU.is_gt, ALU.subtract)]
        ch = [lambda: kb.ts(phi[:], CI, V(f8s.t[:, pair:pair + 1], f8s.reg()), None, op0=ALU.mult)]
        ch += fr(phf)
        ch += [lambda: kb.actf(V(RSall.t[:, pair, :], RSall.reg(pair)), phf[:], AF.Sin, scale=TWO_PI),
               lambda: kb.ts(tg[:], phf[:], 0.25, None, op0=ALU.add),
               lambda: kb.stt(tg[:], tg[:], 0.5, tg[:], ALU.is_gt, ALU.subtract),
               lambda: kb.stt(phi[:], tg[:], 0.5, tg[:], ALU.is_gt, ALU.subtract),
               lambda: kb.actf(V(RCall.t[:, pair, :], RCall.reg(pair)), phi[:], AF.Sin, scale=TWO_PI)]
        return ch

    for pair in range(0, 16, 2):
        interleave(rot_chain(pair, RT[0]), rot_chain(pair + 1, RT[1]))
    xcn = 0
    for seq in range(NSEQ):
        kb.dma(kb.sp, wu[:], V(Wb_in.t[12:16].rearrange("n p k c -> p n k c"), Wb_in.reg()), ds_wu)
        for tc in range(4):
            xb = xc[xcn % 2]
            kb.dma(kb.sp, xb[:], V(XTB.t[seq, tc], XTB.reg((seq, tc))), ds_xc[xcn % 2])
            xcn += 1
            for s in range(4):
                bk = PS[s]
                for kt in range(8):
                    kb.mm(bk[:], V(wu.t[:, s, kt, :], wu.reg()), V(xb.t[:, kt, :], xb.reg()), start=(kt == 0), stop=(kt == 7))
                dstv = V(UD.t[:, s, :, tc * 64:(tc + 1) * 64], UD.reg())
                srcv = V(bk.t[:, :].rearrange("p (c i) -> p i c", i=8), bk.reg())
                if s % 2 == 0:
                    kb.actf(dstv, srcv, AF.Copy)
                else:
                    kb.copy(dstv, srcv)
        def pair_chains(pair):
            s, q = pair // 4, pair % 4
            bk = PS[4 + pair % 4]
            for ri in range(2):
                for i in range(8):
                    kb.mm(V(bk.t[:, ri * 256:(ri + 1) * 256], bk.reg()),
                          V(W_end.t[32 * q:32 * q + 32, s, i, ri, :], W_end.reg()),
                          V(UD.t[32 * q:32 * q + 32, s, i, :], UD.reg()),
                          start=(i == 0), stop=(i == 7), tile_position=(32 * q, 0))
            k2 = pair % 2
            rc = V(RCall.t[:, pair, :], RCall.reg(pair)); rs = V(RSall.t[:, pair, :], RSall.reg(pair))
            ep, wp = Ep[k2], Wp[k2]
            sa, sb_, sa2, sb2, sa3 = SA[k2], SB[k2], SA2[k2], SB2[k2], SA3[k2]
            ere = V(bk.t[:, 0:256], bk.reg()); eim = V(bk.t[:, 256:512], bk.reg())
            epr = V(ep.t[:, 0, :], ep.reg()); epi = V(ep.t[:, 1, :], ep.reg())
            dec = V(r8s.t[:, pair:pair + 1].to_broadcast([128, 256]), r8s.reg())
            wr = V(wp.t[:, 0, :], wp.reg()); wi = V(wp.t[:, 1, :], wp.reg())
            PL = kb.pool
            xre = V(XH.t[:, pair, 0, 1:257], XH.reg(pair)); xim = V(XH.t[:, pair, 1, 1:257], XH.reg(pair))
            dchain = [lambda: kb.tt(sa[:], rc, ere, ALU.mult),
                      lambda: kb.tt(sb_[:], rs, eim, ALU.mult),
                      lambda: kb.tt(epr, sa[:], sb_[:], ALU.add),
                      lambda: kb.tt(sa[:], rc, eim, ALU.mult),
                      lambda: kb.tt(sb_[:], rs, ere, ALU.mult),
                      lambda: kb.tt(epi, sa[:], sb_[:], ALU.subtract),
                      lambda: kb.scan(wr, dec, epr, 0.0),
                      lambda: kb.scan(wi, dec, epi, 0.0)]
            pchain = [lambda: kb.tt(sa2[:], wr, rc, ALU.mult, e=PL),
                      lambda: kb.tt(sb2[:], wi, rs, ALU.mult, e=PL),
                      lambda: kb.tt(xre, sa2[:], sb2[:], ALU.subtract, e=PL),
                      lambda: kb.tt(sa3[:], wi, rc, ALU.mult, e=PL)]
            dtail = [lambda: kb.tt(sb_[:], wr, rs, ALU.mult),
                     lambda: kb.tt(xim, sa3[:], sb_[:], ALU.add)]
            return dchain, pchain, dtail

        for pair in range(0, 16, 2):
            da, pa, ta = pair_chains(pair)
            db, pb, tb = pair_chains(pair + 1)
            interleave(da, db)
            interleave(pa, pb)
            interleave(ta, tb)
        nb = 0
        for s in range(4):
            for j in range(8):
                bk = PS[nb % 4]
                half = (nb // 4) % 2
                nb += 1
                yv = V(bk.t[:, half * 256:(half + 1) * 256], bk.reg())
                for i in range(j + 1):
                    kb.mm(yv, V(Kblk.t[:, s, j - i, :], Kblk.reg()), V(UD.t[:, s, i, :], UD.reg()), start=(i == 0), stop=False, inc=False)
                for q in range(4):
                    pair = s * 4 + q
                    for ri in range(2):
                        last = (q == 3 and ri == 1)
                        kb.mm(V(bk.t[32 * q:32 * q + 32, half * 256:(half + 1) * 256], bk.reg()),
                              V(W_car.t[:, pair, j, ri, :], W_car.reg()),
                              V(XH.t[:, pair, ri, 0:256], XH.reg(pair)),
                              start=False, stop=(ri == 1), inc=last, tile_position=(0, 32 * q))
                kb.actf(V(yT.t[:, s, j::8], yT.reg()), yv, AF.Gelu_apprx_tanh)
        for tc in range(4):
            kb.dma(kb.act, V(YTB.t[seq, tc], YTB.reg((seq, tc))), V(yT.t[:, :, tc * 512:(tc + 1) * 512], yT.reg()), ds_y[tc])

    if stop == "S":
        barrier(kb, DS)
        dump("ytb", V(YTB.t[:], YTB.reg()), [NSEQ, 4, 128, 4, 512], BF16)
        dump("ud", UD[:], [128, 4, 8, 256], BF16)
        dump("xh", XH[:], [128, 16, 2, 257], BF16)
        return finish()
    Wb_sbo = wscratch("sbo", w_sbo, 512, DM)
    Wb_glu = wscratch("glu", w_glu, 512, 2048)
    Wb_o = wscratch("o", w_o, DM, DM)
    Wb_up = wscratch("up", w_up, DM, 2 * DFF)
    Wb_dn = wscratch("dn", w_dn, DFF, DM)
    Wb_pe = wscratch("pe", w_pe, 256, DM)
    Wb_peg = wscratch("peg", w_peg, DM, DM)

    for seq in range(NSEQ):
        barrier(kb, DS)
        PIDX[0] = 0
        ds_o = [dsem("o0"), dsem("o1")]
        arena.reset()
        qT = arena.alloc([4, TT], BF16)
        kz = arena.alloc([8, TT], BF16)
        vv = arena.alloc([16, 512], BF16)
        wqk = arena.alloc([8, 8, 128], BF16)
        wv = arena.alloc([4, 8, 128], BF16)
        xc = [arena.alloc([8, 512], BF16) for _ in range(2)]
        NBUF = 3
        e_t = [arena.alloc([512], F32) for _ in range(NBUF)]
        sp_t = [arena.alloc([512], BF16) for _ in range(NBUF)]
        e3_t = [arena.alloc([512], F32) for _ in range(NBUF)]
        w_t = [arena.alloc([512], BF16) for _ in range(NBUF)]
        A_t = arena.alloc([512], BF16)
        ds_w1 = dsem("wqk"); ds_w2 = dsem("wv"); ds_x = [dsem("qx0"), dsem("qx1")]
        kb.dma(kb.sp, wqk[:], V(Wb_in.t[0:8].rearrange("n p k c -> p n k c"), Wb_in.reg()), ds_w1)
        kb.dma(kb.sp, wv[:], V(Wb_in.t[8:12].rearrange("n p k c -> p n k c"), Wb_in.reg()), ds_w2)
        for tc_ in range(4):
            kb.memset(V(kz.t[:, :, tc_ * 512:(tc_ + 1) * 512], kz.reg(tc_)), 0.0)
        nb = 0
        for tc in range(4):
            xb = xc[tc % 2]
            kb.dma(kb.sp, xb[:], V(XTB.t[seq, tc], XTB.reg((seq, tc))), ds_x[tc % 2])
            for nt in range(8):
                bk = PS[nb % 4]
                nb += 1
                for kt in range(8):
                    kb.mm(bk[:], V(wqk.t[:, nt, kt, :], wqk.reg()), V(xb.t[:, kt, :], xb.reg()), start=(kt == 0), stop=(kt == 7))
                if nt < 4:
                    dv = V(qT.t[:, nt, tc * 512:(tc + 1) * 512], qT.reg(tc))
                    if nt % 2 == 0:
                        kb.actf(dv, bk[:], AF.Copy, scale=0.125)
                    else:
                        kb.ts(dv, bk[:], 0.125, None, op0=ALU.mult)
                else:
                    for hh in range(2):
                        hd = 2 * (nt - 4) + hh
                        hs_ = slice(hh * 64, hh * 64 + 64)
                        dv = V(kz.t[hs_, hd, tc * 512:(tc + 1) * 512], kz.reg(tc))
                        sv_ = V(bk.t[hs_, :], bk.reg())
                        if nt % 2 == 0:
                            kb.actf(dv, sv_, AF.Copy)
                        else:
                            kb.copy(dv, sv_)
            for t4 in range(4):
                bk = PS[4 + t4 % 4]
                for j in range(4):
                    for kt in range(8):
                        kb.mm(V(bk.t[:, j * 128:(j + 1) * 128], bk.reg()), V(xb.t[:, kt, t4 * 128:(t4 + 1) * 128], xb.reg()),
                              V(wv.t[:, j, kt, :], wv.reg()), start=(kt == 0), stop=(kt == 7), inc=(kt == 7 and j == 3))
                dv = V(vv.t[:, tc * 4 + t4, :], vv.reg(tc))
                if t4 % 2 == 0:
                    kb.actf(dv, bk[:], AF.Copy)
                else:
                    kb.copy(dv, bk[:])
        tiles = []
        for h in range(8):
            for qc in range(4):
                for kbk in range(4 * qc + 3, -1, -1):
                    m = kbk - 4 * qc
                    c0 = 128 * m if m > 0 else 0
                    tiles.append((h, qc, kbk, c0, m >= 0, kbk == 4 * qc + 3, kbk == 0))
        ZB = [PS[0], PS[1]]
        RB = [PS[2], PS[3]]
        OB = [PS[4], PS[5]]
        nt_ = len(tiles)
        ostate = {}

        def hp(h):
            return slice((h % 2) * 64, (h % 2) * 64 + 64)

        def stage1(i):
            h, qc, kbk, c0, diag, first, lastk = tiles[i]
            zb = ZB[i % 2]
            n = 512 - c0
            qv = V(qT.t[:, h // 2, qc * 512 + c0:(qc + 1) * 512], qT.reg(qc))
            kv = V(kz.t[:, h, kbk * 128:(kbk + 1) * 128], kz.reg(kbk // 4))
            kb.mm(V(zb.t[:, c0:512], zb.reg()), kv, qv)
            ev = V(e_t[i % NBUF].t[:, c0:512], e_t[i % NBUF].reg())
            kb.actf(ev, V(zb.t[:, c0:512], zb.reg()), AF.Exp)
            if diag:
                e1 = V(e_t[i % NBUF].t[:, c0:c0 + 128], e_t[i % NBUF].reg())
                kb.tt(e1, e1, cmask, ALU.mult)
            spv = V(sp_t[i % NBUF].t[:, c0:512], sp_t[i % NBUF].reg())
            kb.actf(spv, ev, AF.Ln, bias=1.0)

        def stage2(i):
            h, qc, kbk, c0, diag, first, lastk = tiles[i]
            rb = RB[i % 2]
            spv = V(sp_t[i % NBUF].t[:, c0:512], sp_t[i % NBUF].reg())
            rv = V(rb.t[:, c0:512], rb.reg())
            kb.mm(rv, trineg[:], spv, start=True, stop=first)
            if not first:
                kb.mm(rv, onesneg[:], V(A_t.t[:, c0:512], A_t.reg()), start=False, stop=True)
            if not lastk:
                if first:
                    kb.memset(A_t[:], 0.0)
                kb.tt(V(A_t.t[:, c0:512], A_t.reg()), V(A_t.t[:, c0:512], A_t.reg()), spv, ALU.add)
            e3v = V(e3_t[i % NBUF].t[:, c0:512], e3_t[i % NBUF].reg())
            kb.actf(e3v, rv, AF.Exp)
            ev = V(e_t[i % NBUF].t[:, c0:512], e_t[i % NBUF].reg())
            kb.tt(V(w_t[i % NBUF].t[:, c0:512], w_t[i % NBUF].reg()), ev, e3v, ALU.mult)
            if first and c0 > 0:
                kb.memset(V(w_t[i % NBUF].t[:, 0:c0], w_t[i % NBUF].reg()), 0.0)

        def stage3(i):
            h, qc, kbk, c0, diag, first, lastk = tiles[i]
            ob = OB[(h * 4 + qc) % 2]
            ov = V(ob.t[:, c0:512], ob.reg())
            vt = V(vv.t[:, kbk, (h // 2) * 128:(h // 2 + 1) * 128], vv.reg(kbk // 4))
            wt = w_t[i % NBUF]
            if first:
                kb.mm(ob[:], vt, wt[:], start=True, stop=lastk, inc=True)
            else:
                kb.mm(ov, vt, V(wt.t[:, c0:512], wt.reg()), start=False, stop=lastk, inc=True)
            if lastk:
                dv = V(attnT.t[hp(h), h // 2, qc * 512:(qc + 1) * 512], attnT.reg(qc))
                kb.copy(dv, V(ob.t[hp(h), :], ob.reg()))

        for st in range(nt_ + 2):
            if st < nt_:
                stage1(st)
            if 0 <= st - 1 < nt_:
                stage2(st - 1)
            if 0 <= st - 2 < nt_:
                stage3(st - 2)

        barrier(kb, DS)
        PIDX[0] = 2
        if stop == "Q":
            dump("attn", attnT[:], [128, 4, TT], BF16)
            dump("qT", qT[:], [128, 4, TT], BF16)
            dump("vv", vv[:], [128, 16, 512], BF16)
            return finish()
        arena.reset()
        xcb = arena.alloc([8, 512], BF16)
        ycb = arena.alloc([4, 512], BF16)
        pcb = arena.alloc([2, 512], BF16)
        xr = [arena.alloc([512], F32) for _ in range(2)]
        wga = [arena.alloc([8, 128], BF16) for _ in range(2)]
        wgs = [arena.alloc([8, 128], BF16) for _ in range(2)]
        wsb = [arena.alloc([4, 128], BF16) for _ in range(2)]
        wz1 = [arena.alloc([4, 128], BF16) for _ in range(2)]
        wz2 = [arena.alloc([4, 128], BF16) for _ in range(2)]
        wo_ = [arena.alloc([8, 128], BF16) for _ in range(4)]
        wuv = [arena.alloc([8, 128], BF16) for _ in range(3)]
        wug = [arena.alloc([8, 128], BF16) for _ in range(3)]
        wd_ = [arena.alloc([22, 128], BF16) for _ in range(2)]
        wpe_ = [arena.alloc([2, 128], BF16) for _ in range(2)]
        wpg_ = [arena.alloc([8, 128], BF16) for _ in range(2)]
        tmp = [arena.alloc([512], F32) for _ in range(6)]
        mixT = arena.alloc([8, 512], BF16)
        r1 = arena.alloc([8, 512], F32)
        h1 = arena.alloc([8, 512], F32)
        h1b = arena.alloc([8, 512], BF16)
        aT = arena.alloc([22, 512], BF16)
        mean_sb = arena.alloc([512], F32); m2 = arena.alloc([512], F32); rstd = arena.alloc([512], F32); lt = arena.alloc([512], F32)
        ot = [arena.alloc([DM], F32) for _ in range(2)]
        dsn = lambda nm, k=2: [dsem(nm + str(i_)) for i_ in range(k)]
        d_x = dsem("ex"); d_y = dsem("ey"); d_p = dsem("ep"); d_xr = dsn("xr")
        d_ga = dsn("ga"); d_gs = dsn("gs"); d_sb = dsn("sb"); d_z1 = dsn("z1"); d_z2 = dsn("z2"); d_wo = dsn("wo", 4)
        d_uv = dsn("uv", 3); d_ug = dsn("ug", 3); d_wd = dsn("wd"); d_pe = dsn("pe"); d_pg = dsn("pg")
        kb.memset(halo[:], 0.0)

        wlc = {}

        def wl(dst, dsl, wb, nt, i):
            k = wlc.get(id(dst), 0)
            wlc[id(dst)] = k + 1
            sl = k % len(dst)
            kb.dma(kb.sp, dst[sl][:], V(wb.t[nt], wb.reg()), dsl[sl])
            return dst[sl]

        bkc = [0]

        def nbk():
            b = PS[bkc[0] % 8]
            bkc[0] += 1
            return b

        def ln_pre(src, n):
            kb.actf(V(aT.t[:, n, :], aT.reg(n)), V(src.t[:, n, :], src.reg(n)), AF.Copy)
            kb.actf(V(aT.t[:, 8 + n, :], aT.reg(8 + n)), V(src.t[:, n, :], src.reg(n)), AF.Square)

        ltb = [arena.alloc([512], F32) for _ in range(2)]
        cvx = arena.alloc([512], F32)

        ln_deferred = []

        def ln_flush(k=1):
            for _ in range(k):
                if ln_deferred:
                    dst_, n_, gk_, bk2_ = ln_deferred.pop(0)
                    dv_ = V(dst_.t[:, n_, :], dst_.reg(n_))
                    kb.ts(dv_, dv_, lncol(gk_, n_), lncol(bk2_, n_), op0=ALU.mult, op1=ALU.add)

        def layer_norm(src, gk, bk_, dst, dstb, pre_done, hook=None, mid_hook=None):
            if not pre_done:
                for n in range(8):
                    ln_pre(src, n)
            if mid_hook is not None:
                mid_hook()
            bm, bq = nbk(), nbk()
            for kt in range(8):
                kb.mm(bm[:], onesD[:], V(aT.t[:, kt, :], aT.reg(kt)), start=(kt == 0), stop=(kt == 7))
            for kt in range(8):
                kb.mm(bq[:], onesD[:], V(aT.t[:, 8 + kt, :], aT.reg(8 + kt)), start=(kt == 0), stop=(kt == 7))
            kb.actf(m2[:], bm[:], AF.Square)
            kb.actf(mean_sb[:], bm[:], AF.Copy)
            kb.tt(m2[:], bq[:], m2[:], ALU.subtract)
            kb.actf(rstd[:], m2[:], AF.Sqrt, bias=EPS)
            kb.op(kb.dve, lambda h: h.reciprocal(out=rstd.t[:], in_=rstd.t[:]), r=[rstd[:]], w=[rstd[:]])
            for n in range(8):
                e = kb.pool if n in (1, 4, 6) else kb.dve
                if dstb is not None:
                    dn = V(dst.t[:, n, :], dst.reg(n))
                    kb.tt(dn, V(src.t[:, n, :], src.reg(n)), mean_sb[:], ALU.subtract, e=e)
                    kb.tt(dn, dn, rstd[:], ALU.mult, e=e)
                    kb.actf(V(dstb.t[:, n, :], dstb.reg(n)), dn, AF.Identity, scale=lncol(gk, n), bias=lncol(bk_, n))
                    ln_deferred.append((dst, n, gk, bk_))
                else:
                    lt_ = ltb[n % 2]
                    kb.tt(lt_[:], V(src.t[:, n, :], src.reg(n)), mean_sb[:], ALU.subtract, e=e)
                    kb.tt(lt_[:], lt_[:], rstd[:], ALU.mult, e=e)
                    kb.actf(V(dst.t[:, n, :], dst.reg(n)), lt_[:], AF.Identity, scale=lncol(gk, n), bias=lncol(bk_, n))
                if hook is not None:
                    hook(n)

        def e1_loads(tc):
            kb.dma(kb.sp, xcb[:], V(XTB.t[seq, tc], XTB.reg((seq, tc))), d_x)
            kb.dma(kb.sp, ycb[:], V(YTB.t[seq, tc], YTB.reg((seq, tc))), d_y)

        def e1_step(tc, n):
            tsl = slice(tc * 512, (tc + 1) * 512)
            a1 = wl(wga, d_ga, Wb_in, 16 + n, n); a2 = wl(wgs, d_gs, Wb_in, 24 + n, n)
            a3 = wl(wsb, d_sb, Wb_sbo, n, n); a4 = wl(wz1, d_z1, Wb_glu, n, n); a5 = wl(wz2, d_z2, Wb_glu, 8 + n, n)
            bga, bgs, bab, bz1, bz2 = nbk(), nbk(), nbk(), nbk(), nbk()
            for kt in range(8):
                kb.mm(bga[:], V(a1.t[:, kt, :], a1.reg()), V(xcb.t[:, kt, :], xcb.reg()), start=(kt == 0), stop=(kt == 7))
            for kt in range(8):
                kb.mm(bgs[:], V(a2.t[:, kt, :], a2.reg()), V(xcb.t[:, kt, :], xcb.reg()), start=(kt == 0), stop=(kt == 7))
            for kt in range(4):
                kb.mm(bab[:], V(a3.t[:, kt, :], a3.reg()), V(attnT.t[:, kt, tsl], attnT.reg(tc)), start=(kt == 0), stop=(kt == 3))
            for kt in range(4):
                kb.mm(bz1[:], V(a4.t[:, kt, :], a4.reg()), V(ycb.t[:, kt, :], ycb.reg()), start=(kt == 0), stop=(kt == 3))
            for kt in range(4):
                kb.mm(bz2[:], V(a5.t[:, kt, :], a5.reg()), V(ycb.t[:, kt, :], ycb.reg()), start=(kt == 0), stop=(kt == 3))
            kb.actf(tmp[0][:], bga[:], AF.Sigmoid)
            kb.actf(tmp[1][:], bgs[:], AF.Sigmoid)
            kb.actf(tmp[2][:], bz2[:], AF.Sigmoid)
            kb.tt(tmp[3][:], tmp[0][:], bab[:], ALU.mult)
            kb.tt(tmp[4][:], tmp[2][:], bz1[:], ALU.mult)
            kb.tt(tmp[4][:], tmp[4][:], tmp[1][:], ALU.mult)
            kb.tt(V(mixT.t[:, n, :], mixT.reg()), tmp[3][:], tmp[4][:], ALU.add)

        for tc in range(4):
            nxt = tc + 1 if tc < 3 else None
            if tc == 0:
                e1_loads(0)
                for n in range(8):
                    e1_step(0, n)
            kb.dma(kb.sp, pcb[:], V(PTB.t[seq, tc], PTB.reg((seq, tc))), d_p)

            def hook1(n, nxt=nxt):
                if nxt is not None and n % 2 == 1:
                    e1_step(nxt, n // 2)

            def hook2(n, nxt=nxt):
                if nxt is not None and n in (1, 3, 5):
                    e1_step(nxt, 5 + n // 2)

            def mid2(nxt=nxt):
                if nxt is not None:
                    e1_step(nxt, 4)
            for n in range(8):
                a1 = wl(wo_, d_wo, Wb_o, n, n)
                xrn = xr[n % 2]
                kb.dma(kb.sp, xrn[:], V(XTF.t[seq, tc, :, n, :], XTF.reg((seq, tc))), d_xr[n % 2])
                bk = nbk()
                for kt in range(8):
                    kb.mm(bk[:], V(a1.t[:, kt, :], a1.reg()), V(mixT.t[:, kt, :], mixT.reg()), start=(kt == 0), stop=(kt == 7))
                kb.stt(V(r1.t[:, n, :], r1.reg(n)), xrn[:], ALPHA, bk[:], ALU.mult, ALU.add)
                ln_pre(r1, n)
            if nxt is not None:
                e1_loads(nxt)
            layer_norm(r1, 0, 1, h1, h1b, True, hook=hook1)
            f1st = {}

            def f1_A(j):
                a1 = wl(wuv, d_uv, Wb_up, j, j); a2 = wl(wug, d_ug, Wb_up, 22 + j, j)
                bv, bg = nbk(), nbk()
                for kt in range(8):
                    kb.mm(bv[:], V(a1.t[:, kt, :], a1.reg()), V(h1b.t[:, kt, :], h1b.reg(kt)), start=(kt == 0), stop=(kt == 7))
                for kt in range(8):
                    kb.mm(bg[:], V(a2.t[:, kt, :], a2.reg()), V(h1b.t[:, kt, :], h1b.reg(kt)), start=(kt == 0), stop=(kt == 7))
                for which, (bk, c44) in enumerate([(bv, j), (bg, 22 + j)]):
                    cv = ([tmp[0], tmp[1], cvx][j % 3]) if which == 0 else tmp[2 + j % 2]
                    hnew = V(halo.t[:, c44, tc % 2, :], halo.reg((c44, tc % 2)))
                    kb.actf(cv[:], bk[:], AF.Identity, scale=cwcol(2, c44), bias=cbcol(c44))
                    kb.actf(hnew, V(bk.t[:, 510:512], bk.reg()), AF.Copy)
                    hold = lambda a, b: V(halo.t[:, c44, (tc + 1) % 2, a:b], halo.reg((c44, (tc + 1) % 2)))
                    hc2 = V(hcT.t[:, c44, 0:2], hcT.reg(c44)); hc1 = V(hcT.t[:, c44, 0:1], hcT.reg(c44))
                    kb.actf(hc2, hold(0, 2), AF.Identity, scale=cwcol(0, c44))
                    kb.actf(hc1, hold(1, 2), AF.Identity, scale=cwcol(1, c44), bias=hc1)
                f1st[j] = (bv, bg)

            def f1_B(j):
                bv, bg = f1st.pop(j)
                ln_flush(1)
                cvs = []
                for which, (bk, c44) in enumerate([(bv, j), (bg, 22 + j)]):
                    cv = ([tmp[0], tmp[1], cvx][j % 3]) if which == 0 else tmp[2 + j % 2]
                    hl = lambda a, b: V(halo.t[:, c44, (tc + 1) % 2, a:b], halo.reg((c44, (tc + 1) % 2)))
                    kb.stt(V(cv.t[:, 1:512], cv.reg()), V(bk.t[:, 0:511], bk.reg()), cwcol(1, c44), V(cv.t[:, 1:512], cv.reg()), ALU.mult, ALU.add)
                    kb.stt(V(cv.t[:, 2:512], cv.reg()), V(bk.t[:, 0:510], bk.reg()), cwcol(0, c44), V(cv.t[:, 2:512], cv.reg()), ALU.mult, ALU.add)
                    kb.tt(V(cv.t[:, 0:2], cv.reg()), V(cv.t[:, 0:2], cv.reg()), V(hcT.t[:, c44, 0:2], hcT.reg(c44)), ALU.add)
                    cvs.append(cv)
                sg = tmp[4 + j % 2]
                kb.actf(sg[:], cvs[1][:], AF.Silu)
                kb.tt(V(aT.t[:, j, :], aT.reg(j)), sg[:], cvs[0][:], ALU.mult, e=kb.pool)

            for j in range(23):
                if j < 22:
                    f1_A(j)
                if j >= 1:
                    f1_B(j - 1)
            for n in range(8):
                a1 = wl(wd_, d_wd, Wb_dn, n, n); a2 = wl(wpe_, d_pe, Wb_pe, n, n); a3 = wl(wpg_, d_pg, Wb_peg, n, n)
                bf_, bpe, bpg = nbk(), nbk(), nbk()
                for kt in range(22):
                    kb.mm(bf_[:], V(a1.t[:, kt, :], a1.reg()), V(aT.t[:, kt, :], aT.reg(kt)), start=(kt == 0), stop=(kt == 21))
                for kt in range(2):
                    kb.mm(bpe[:], V(a2.t[:, kt, :], a2.reg()), V(pcb.t[:, kt, :], pcb.reg()), start=(kt == 0), stop=(kt == 1))
                for kt in range(8):
                    kb.mm(bpg[:], V(a3.t[:, kt, :], a3.reg()), V(h1b.t[:, kt, :], h1b.reg(kt)), start=(kt == 0), stop=(kt == 7))
                sgt = tmp[n % 2]; tt_ = tmp[2 + n % 2]
                kb.actf(sgt[:], bpg[:], AF.Sigmoid)
                kb.tt(tt_[:], sgt[:], bpe[:], ALU.mult)
                kb.tt(tt_[:], tt_[:], bf_[:], ALU.add)
                kb.stt(V(r1.t[:, n, :], r1.reg(n)), V(h1.t[:, n, :], h1.reg(n)), ALPHA, tt_[:], ALU.mult, ALU.add)
            layer_norm(r1, 2, 3, r1, None, False, hook=hook2, mid_hook=mid2)
            for t4 in range(4):
                o_ = ot[t4 % 2]
                for hh in range(2):
                    bk = nbk()
                    for d4 in range(4):
                        n = hh * 4 + d4
                        kb.transpose(V(bk.t[:, d4 * 128:(d4 + 1) * 128], bk.reg()),
                                     V(r1.t[:, n, t4 * 128:(t4 + 1) * 128], r1.reg(n)), ident, inc=(d4 == 3))
                    if hh == 0:
                        kb.actf(V(o_.t[:, 0:512], o_.reg()), bk[:], AF.Copy)
                    else:
                        kb.copy(V(o_.t[:, 512:1024], o_.reg()), bk[:])
                tok0 = tc * 512 + t4 * 128
                kb.dma(kb.act, V(out.t[seq, tok0:tok0 + 128, :], out.reg((seq, tc, t4))), o_[:], ds_o[t4 % 2])
    barrier(kb, DS)
    return kb


_CACHE = {}


def kernel(**inputs):
    n = 8
    if "kb" not in _CACHE:
        _CACHE["kb"] = build()
    kb = _CACHE["kb"]
    cst = host_consts()
    x = np.ascontiguousarray(np.asarray(inputs["x"], np.float32))
    p = np.ascontiguousarray(np.asarray(inputs["p"], np.float32))[0]
    shared = {"cst": cst}
    for k in ["w_in", "w_sb_out", "ssm_a_re", "ssm_a_im", "ssm_log_step", "ssm_b_re", "ssm_b_im", "ssm_c_re",
              "ssm_c_im", "ssm_d", "w_glu", "w_o", "ln1_g", "ln1_b", "w_up", "conv_w", "conv_b", "w_down",
              "w_pe", "w_pe_gate", "ln2_g", "ln2_b"]:
        shared[k] = np.ascontiguousarray(np.asarray(inputs[k], np.float32)[0])
    in_maps = []
    for c in range(n):
        m = dict(shared)
        m["x"] = x[2 * c:2 * c + 2]
        m["p"] = p[2 * c:2 * c + 2]
        in_maps.append(m)
    res = run_bass_kernel_spmd(kb.nc, in_maps, core_ids=list(range(n)))
    return np.concatenate([r["out"] for r in res.results], axis=0)
```
